# Optimizing a Trainium2 kernel written in Bass

```python
import jax, jax.numpy as jnp
from jax import lax
import numpy as np

D_MODEL = 2048
BATCH = 8
SEQ = 4096
DEPTH = 4

N_EVEN = (DEPTH + 1) // 2
N_ODD = DEPTH // 2

MLA_HEADS = 8
QK_NOPE_DIM = 128
QK_ROPE_DIM = 64
QK_HEAD_DIM = QK_NOPE_DIM + QK_ROPE_DIM
V_HEAD_DIM = 128
Q_LORA_RANK = D_MODEL // 4
KV_LORA_RANK = D_MODEL // 4
ROPE_THETA = 10000.0
MLA_WIDTH = MLA_HEADS * V_HEAD_DIM
Q_BLOCK = 128

POOL_WINDOWS = (2, 4, 8, 16)
POOL_GROUPS = len(POOL_WINDOWS)
POOL_GROUP_DIM = D_MODEL // 8
POOL_WIDTH = POOL_GROUPS * POOL_GROUP_DIM

EVEN_IN_SIZES = (Q_LORA_RANK, KV_LORA_RANK, QK_ROPE_DIM, POOL_WIDTH)
EVEN_IN_DIM = sum(EVEN_IN_SIZES)
EVEN_MIX_DIM = MLA_WIDTH + POOL_WIDTH

CONV_WIDTH = 3
CONV_DIM = D_MODEL

D_FF = 4 * D_MODEL
RMS_EPS = 1e-6

kernel_name = "hybrid_mla_pool_shortconv_trunk"


def _rmsnorm(x, g):
    xf = x.astype(jnp.float32)
    y = xf * lax.rsqrt(jnp.mean(jnp.square(xf), axis=-1, keepdims=True) + RMS_EPS)
    return (y * g.astype(jnp.float32)).astype(x.dtype)


def _rope_tables(positions):
    inv_freq = 1.0 / (ROPE_THETA ** (jnp.arange(0, QK_ROPE_DIM, 2, dtype=jnp.float32) / QK_ROPE_DIM))
    ang = positions.astype(jnp.float32)[..., None] * inv_freq
    return jnp.cos(ang)[:, :, None, :], jnp.sin(ang)[:, :, None, :]


def _rope(x, cos, sin):
    xf = x.astype(jnp.float32)
    x1, x2 = jnp.split(xf, 2, axis=-1)
    return jnp.concatenate([x1 * cos - x2 * sin, x2 * cos + x1 * sin], axis=-1).astype(x.dtype)


def _causal_attention(q, k, v):
    B, S, H, Dh = q.shape
    Dv = v.shape[-1]
    n_blocks = S // Q_BLOCK
    scale = Dh ** -0.5
    qb = q.reshape(B, n_blocks, Q_BLOCK, H, Dh).transpose(1, 0, 2, 3, 4)
    key_pos = jnp.arange(S)
    neg = jnp.finfo(jnp.float32).min

    def one_block(args):
        q_blk, blk = args
        s = jnp.einsum('bqhd,bkhd->bhqk', q_blk, k, preferred_element_type=jnp.float32) * scale
        q_pos = blk * Q_BLOCK + jnp.arange(Q_BLOCK)
        s = jnp.where(key_pos[None, :] <= q_pos[:, None], s, neg)
        p = jax.nn.softmax(s, axis=-1).astype(v.dtype)
        return jnp.einsum('bhqk,bkhe->bqhe', p, v)

    out = lax.map(one_block, (qb, jnp.arange(n_blocks)))
    return out.transpose(1, 0, 2, 3, 4).reshape(B, S, H, Dv)


def _mla(c_q, c_kv, k_rope, cos, sin, q_a_g, kv_a_g, w_uq, w_ukv, q_norm_g, k_norm_g):
    B, S, _ = c_q.shape
    c_q = _rmsnorm(c_q, q_a_g)
    c_kv = _rmsnorm(c_kv, kv_a_g)
    q = (c_q @ w_uq).reshape(B, S, MLA_HEADS, QK_HEAD_DIM)
    kv = (c_kv @ w_ukv).reshape(B, S, MLA_HEADS, QK_NOPE_DIM + V_HEAD_DIM)
    k_nope, v = kv[..., :QK_NOPE_DIM], kv[..., QK_NOPE_DIM:]
    k_r = jnp.broadcast_to(k_rope[:, :, None, :], (B, S, MLA_HEADS, QK_ROPE_DIM))
    k = jnp.concatenate([k_nope, k_r], axis=-1)
    q = _rmsnorm(q, q_norm_g)
    k = _rmsnorm(k, k_norm_g)
    q = jnp.concatenate([q[..., :QK_NOPE_DIM], _rope(q[..., QK_NOPE_DIM:], cos, sin)], axis=-1)
    k = jnp.concatenate([k[..., :QK_NOPE_DIM], _rope(k[..., QK_NOPE_DIM:], cos, sin)], axis=-1)
    out = _causal_attention(q, k, v)
    return out.reshape(B, S, MLA_WIDTH)


def _pool_mixer(u, pool_w, pool_scale):
    B, S, _ = u.shape
    uf = u.astype(jnp.float32).reshape(B, S, POOL_GROUPS, POOL_GROUP_DIM)
    cs = jnp.pad(jnp.cumsum(uf, axis=1), ((0, 0), (1, 0), (0, 0), (0, 0)))
    t = jnp.arange(S)[:, None]
    w = jnp.array(POOL_WINDOWS, dtype=jnp.int32)[None, :]
    start = jnp.maximum(t + 1 - w, 0)
    count = jnp.minimum(t + 1, w).astype(jnp.float32)
    lagged = cs[:, start, jnp.arange(POOL_GROUPS)[None, :]]
    mean = (cs[:, 1:] - lagged) / count[None, :, :, None]
    pooled = (mean - uf).astype(u.dtype)
    y = jnp.einsum('bsgc,gcd->bsgd', pooled, pool_w)
    y = y * pool_scale.reshape(POOL_GROUPS, POOL_GROUP_DIM)
    return y.reshape(B, S, POOL_WIDTH)


def _short_conv(x_normed, w_in, conv_w, w_out):
    S = x_normed.shape[1]
    gate_b, gate_c, u = jnp.split(x_normed @ w_in, 3, axis=-1)
    v = gate_c * u
    v_pad = jnp.pad(v, ((0, 0), (CONV_WIDTH - 1, 0), (0, 0)))
    conv = conv_w[0] * v_pad[:, 0:S]
    for j in range(1, CONV_WIDTH):
        conv = conv + conv_w[j] * v_pad[:, j:j + S]
    return (gate_b * conv) @ w_out


def _mlp(h, w_up, w_down):
    return jnp.square(jax.nn.relu(h @ w_up)) @ w_down


def setup_inputs(seed: int = 0) -> dict:
    key = jax.random.key(seed)
    ks = jax.random.split(key, 20)
    D = D_MODEL

    def nrm(k, shape, fan_in):
        return jax.random.normal(k, shape, jnp.float32) * (fan_in ** -0.5)

    def gain(k, shape):
        return 1.0 + 0.02 * jax.random.normal(k, shape, jnp.float32)

    x = jax.random.normal(ks[0], (BATCH, SEQ, D), jnp.float32)
    positions = jnp.broadcast_to(jnp.arange(SEQ, dtype=jnp.int32)[None, :], (BATCH, SEQ))
    return {
        "x": x,
        "positions": positions,
        "mix_norm_g": gain(ks[1], (DEPTH, D)),
        "mlp_norm_g": gain(ks[2], (DEPTH, D)),
        "w_mlp_up": nrm(ks[3], (DEPTH, D, D_FF), D),
        "w_mlp_down": nrm(ks[4], (DEPTH, D_FF, D), D_FF),
        "even_w_in": nrm(ks[5], (N_EVEN, D, EVEN_IN_DIM), D),
        "even_q_a_norm_g": gain(ks[6], (N_EVEN, Q_LORA_RANK)),
        "even_kv_a_norm_g": gain(ks[7], (N_EVEN, KV_LORA_RANK)),
        "even_w_uq": nrm(ks[8], (N_EVEN, Q_LORA_RANK, MLA_HEADS * QK_HEAD_DIM), Q_LORA_RANK),
        "even_w_ukv": nrm(ks[9], (N_EVEN, KV_LORA_RANK, MLA_HEADS * (QK_NOPE_DIM + V_HEAD_DIM)), KV_LORA_RANK),
        "even_q_norm_g": gain(ks[10], (N_EVEN, QK_HEAD_DIM)),
        "even_k_norm_g": gain(ks[11], (N_EVEN, QK_HEAD_DIM)),
        "even_pool_w": nrm(ks[12], (N_EVEN, POOL_GROUPS, POOL_GROUP_DIM, POOL_GROUP_DIM), POOL_GROUP_DIM),
        "even_pool_scale": gain(ks[13], (N_EVEN, POOL_WIDTH)),
        "even_w_out": nrm(ks[14], (N_EVEN, EVEN_MIX_DIM, D), EVEN_MIX_DIM),
        "odd_w_in": nrm(ks[15], (N_ODD, D, 3 * CONV_DIM), D),
        "odd_conv_w": nrm(ks[16], (N_ODD, CONV_WIDTH, CONV_DIM), CONV_WIDTH),
        "odd_w_out": nrm(ks[17], (N_ODD, CONV_DIM, D), CONV_DIM),
    }


def reference(x, positions, mix_norm_g, mlp_norm_g, w_mlp_up, w_mlp_down,
              even_w_in, even_q_a_norm_g, even_kv_a_norm_g, even_w_uq, even_w_ukv,
              even_q_norm_g, even_k_norm_g, even_pool_w, even_pool_scale, even_w_out,
              odd_w_in, odd_conv_w, odd_w_out):
    cos, sin = _rope_tables(positions)
    split_at = list(np.cumsum(EVEN_IN_SIZES)[:-1])
    for layer in range(DEPTH):
        h = _rmsnorm(x, mix_norm_g[layer])
        if layer % 2 == 0:
            e = layer // 2
            c_q, c_kv, k_rope, u_pool = jnp.split(h @ even_w_in[e], split_at, axis=-1)
            a = _mla(c_q, c_kv, k_rope, cos, sin, even_q_a_norm_g[e], even_kv_a_norm_g[e],
                     even_w_uq[e], even_w_ukv[e], even_q_norm_g[e], even_k_norm_g[e])
            b = _pool_mixer(u_pool, even_pool_w[e], even_pool_scale[e])
            x = x + jnp.concatenate([a, b], axis=-1) @ even_w_out[e]
        else:
            o = layer // 2
            x = x + _short_conv(h, odd_w_in[o], odd_conv_w[o], odd_w_out[o])
        x = x + _mlp(_rmsnorm(x, mlp_norm_g[layer]), w_mlp_up[layer], w_mlp_down[layer])
    return x
```

```python
import contextlib
import numpy as np
import ml_dtypes
import concourse.bass as bass
import concourse.mybir as mybir
from concourse.bass_utils import run_bass_kernel_spmd

F32 = mybir.dt.float32
BF16 = mybir.dt.bfloat16
I32 = mybir.dt.int32
ALU = mybir.AluOpType
AF = mybir.ActivationFunctionType

D = 2048
DFF = 8192
NC_ = 16
TT = 512
EPS = 1e-6
NH = 8
SEM_ROT = 8000


class Buf:
    __slots__ = ("w", "r", "name", "excl")

    def __init__(self, name="", excl=False):
        self.w = None
        self.r = {}
        self.name = name
        self.excl = excl


class Eng:
    def __init__(self, K, h, name, is_pe=False, ndma=0):
        self.K = K
        self.h = h
        self.name = name
        self.is_pe = is_pe
        self.sem = None
        self.cnt = 0
        self.seen = {}
        self.old = []
        self.dsems = [K.new_sem(f"{name}_d{i}") for i in range(ndma)]
        self.dcnt = [0] * ndma
        self.di = 0
        self.nsem = 0

    def wait(self, sem, val):
        if self.seen.get(sem, 0) >= val:
            return
        self.h.wait_ge(sem, val)
        self.seen[sem] = val

    def signal(self, ins):
        if self.sem is None or self.cnt >= SEM_ROT:
            if self.sem is not None:
                self.old.append((self.sem, self.cnt))
            self.sem = self.K.new_sem(f"{self.name}_s{self.nsem}")
            self.nsem += 1
            self.cnt = 0
        self.cnt += 1
        ins.then_inc(self.sem, 1)
        return (self.sem, self.cnt)


class CastGroup:
    def __init__(self, K, name):
        self.sem = K.new_sem(name)
        self.n = 0
        self.buf = Buf(name)
        self.gated = False


class KB:
    def __init__(self, nc):
        self.nc = nc
        self.es = contextlib.ExitStack()
        self.nsem = 0
        self.pe = Eng(self, nc.tensor, "pe", is_pe=True)
        self.act = Eng(self, nc.scalar, "act")
        self.dve = Eng(self, nc.vector, "dve")
        self.pool = Eng(self, nc.gpsimd, "pool", ndma=8)
        self.sp = Eng(self, nc.sync, "sp", ndma=12)
        self.engs = [self.pe, self.act, self.dve, self.pool, self.sp]
        self.banks = []
        self.bank_bufs = []
        for i in range(8):
            self.banks.append(self.es.enter_context(nc.psum_tensor(f"psb{i}", [128, 512], F32)))
            self.bank_bufs.append(Buf(f"bank{i}", excl=True))
        self.bank_rr = 0
        self.nrot = 7
        self.uid = 0
        self.phase_es = None

    def new_sem(self, name):
        self.nsem += 1
        return self.es.enter_context(self.nc.semaphore(name))

    def sb(self, name, shape, dtype, glob=False):
        self.uid += 1
        es = self.es if glob or self.phase_es is None else self.phase_es
        return es.enter_context(self.nc.sbuf_tensor(f"{name}_{self.uid}", list(shape), dtype))

    def begin_phase(self):
        self.phase_es = contextlib.ExitStack()

    def end_phase(self):
        self.barrier()
        self.phase_es.close()
        self.phase_es = None

    def next_bank(self):
        i = self.bank_rr
        self.bank_rr = (self.bank_rr + 1) % self.nrot
        return self.banks[i], self.bank_bufs[i]

    def _deps(self, reads, writes, own=None):
        deps = {}

        def add(tok):
            if tok is None:
                return
            s, v = tok
            if deps.get(s, 0) < v:
                deps[s] = v

        for b in reads:
            add(b.w)
            if b.excl:
                for s, v in b.r.items():
                    if s is not own:
                        add((s, v))
        for b in writes:
            add(b.w)
            for s, v in b.r.items():
                add((s, v))
        return deps

    def _commit(self, tok, reads, writes):
        s, v = tok
        for b in reads:
            if b.r.get(s, 0) < v:
                b.r[s] = v
        for b in writes:
            b.w = tok
            b.r = {}

    def op(self, eng, fn, reads=(), writes=()):
        deps = self._deps(reads, writes, own=eng.sem)
        for s, v in deps.items():
            if eng.is_pe and s is eng.sem:
                continue
            eng.wait(s, v)
        ins = fn(eng.h)
        tok = eng.signal(ins)
        self._commit(tok, reads, writes)
        return tok

    def dma(self, q, out, in_, reads=(), writes=(), **kw):
        deps = self._deps(reads, writes)
        i = q.di
        q.di = (q.di + 1) % len(q.dsems)
        sem = q.dsems[i]
        if q.dcnt[i] > 0:
            q.wait(sem, 16 * q.dcnt[i])
        for s, v in deps.items():
            q.wait(s, v)
        ins = q.h.dma_start(out=out, in_=in_, **kw)
        q.dcnt[i] += 1
        ins.then_inc(sem, 16)
        tok = (sem, 16 * q.dcnt[i])
        self._commit(tok, reads, writes)
        return tok

    def cast(self, grp, out, in_, gate=None):
        if gate is not None and not grp.gated:
            self.pool.wait(*gate)
        grp.gated = True
        ins = self.pool.h.dma_start(out=out, in_=in_)
        grp.n += 1
        ins.then_inc(grp.sem, 16)
        grp.buf.w = (grp.sem, 16 * grp.n)

    def barrier(self):
        toks = []
        for e in self.engs:
            if e.sem is not None and e.cnt > 0:
                toks.append((e.sem, e.cnt))
            for s, c in zip(e.dsems, e.dcnt):
                if c > 0:
                    toks.append((s, 16 * c))
        for e in self.engs:
            for s, v in toks:
                e.wait(s, v)

    def mm_group(self, out_ap, bank_buf, pairs, reads):
        n = len(pairs)

        def fn(h):
            ins = None
            for i, (l, r) in enumerate(pairs):
                ins = h.matmul(out_ap, lhsT=l, rhs=r, start=(i == 0), stop=(i == n - 1))
            return ins

        return self.op(self.pe, fn, reads=reads, writes=[bank_buf])


def _ring(K, name, n, shape, dtype):
    return [(K.sb(f"{name}{i}", shape, dtype), Buf(f"{name}{i}")) for i in range(n)]


class Ring:
    def __init__(self, K, name, n, shape, dtype):
        self.items = _ring(K, name, n, shape, dtype)
        self.i = 0

    def next(self):
        it = self.items[self.i]
        self.i = (self.i + 1) % len(self.items)
        return it


def build_program(S, layers, do_mixer=True, do_mlp=True):
    NT = S // TT
    NB = S // 128
    nc = bass.Bass("TRN2", target_bir_lowering=False)
    K = KB(nc)

    def din(name, shape, dt=F32):
        return nc.dram_tensor(name, list(shape), dt, kind="ExternalInput").ap()

    x_in = din("x", [S, D])
    pos_in = din("positions", [1, S], I32)
    mix_g = din("mix_norm_g", [4, D])
    mlp_g = din("mlp_norm_g", [4, D])
    w_up = din("w_mlp_up", [4, D, DFF])
    w_down = din("w_mlp_down", [4, DFF, D])
    e_w_in = din("even_w_in", [2, D, 2112])
    e_qa_g = din("even_q_a_norm_g", [2, 512])
    e_kva_g = din("even_kv_a_norm_g", [2, 512])
    e_w_uq = din("even_w_uq", [2, 512, 1536])
    e_w_ukv = din("even_w_ukv", [2, 512, 2048])
    e_qn_g = din("even_q_norm_g", [2, 192])
    e_kn_g = din("even_k_norm_g", [2, 192])
    e_pool_w = din("even_pool_w", [2, 4, 256, 256])
    e_pool_s = din("even_pool_scale", [2, 1024])
    e_w_out = din("even_w_out", [2, D, D])
    o_w_in = din("odd_w_in", [2, D, 3 * D])
    o_conv_w = din("odd_conv_w", [2, 3, D])
    o_w_out = din("odd_w_out", [2, D, D])
    ident_in = din("c_ident", [128, 128])
    tri_in = din("c_tri", [128, 128])
    rope_in = din("c_rope", [64, 2])
    rc_in = din("c_rc", [128, 4, TT])
    y_out = nc.dram_tensor("y", [S, D], F32, kind="ExternalOutput").ap()

    XT = nc.dram_tensor("XT", [NC_, 128, S], F32).ap()
    xt_bufs = [[Buf(f"xt{c}_{t}") for t in range(NT)] for c in range(NC_)]
    WU = {}
    WD = {}
    wu_bufs = {}
    wd_bufs = {}
    for l in layers:
        WU[l] = nc.dram_tensor(f"WU{l}", [16, 128, 16, 512], BF16).ap()
        WD[l] = nc.dram_tensor(f"WD{l}", [16, 128, 64, 128], BF16).ap()
        wu_bufs[l] = [Buf() for _ in range(16)]
        wd_bufs[l] = [Buf() for _ in range(16)]

    ident = K.sb("ident", [128, 128], F32, glob=True)
    ident_b = Buf("ident")
    K.dma(K.sp, ident[:], ident_in, writes=[ident_b])
    ones_d = K.sb("ones_d", [128, 128], BF16, glob=True)
    ones_d_b = Buf("ones_d")
    K.op(K.dve, lambda h: h.memset(ones_d[:], 1.0 / D), writes=[ones_d_b])
    mlp_gT = K.sb("mlp_gT", [128, 4, NC_], F32, glob=True)
    mlp_gT_b = Buf("mlp_gT")
    mix_gT = K.sb("mix_gT", [128, 4, NC_], F32, glob=True)
    mix_gT_b = Buf("mix_gT")
    def load_transposed(parts, dsts):
        K.begin_phase()
        stg = K.sb("vstage", [128, 128], F32)
        stg_b = Buf()
        K.op(K.dve, lambda h: h.memset(stg[:], 0.0), writes=[stg_b])
        for r0, n, ap in parts:
            K.dma(K.sp, stg[r0:r0 + n, :], ap, writes=[stg_b])
        bank, bb = K.next_bank()
        K.op(K.pe, lambda h: h.transpose(out=bank[:, 0:128], in_=stg[:], identity=ident[:]),
             reads=[stg_b, ident_b], writes=[bb])
        for c0, n, view, vb in dsts:
            K.op(K.dve, lambda h, c0=c0, n=n, view=view: h.tensor_copy(out=view, in_=bank[:, c0:c0 + n]),
                 reads=[bb], writes=[vb])
        K.end_phase()

    load_transposed(
        [(0, 64, mlp_g.rearrange("l (c p) -> (l c) p", p=128)), (64, 64, mix_g.rearrange("l (c p) -> (l c) p", p=128))],
        [(0, 64, mlp_gT[:].rearrange("p l c -> p (l c)"), mlp_gT_b),
         (64, 64, mix_gT[:].rearrange("p l c -> p (l c)"), mix_gT_b)])

    cg_up = {l: CastGroup(K, f"cg_up{l}") for l in layers}
    cg_dn = {l: CastGroup(K, f"cg_dn{l}") for l in layers}
    cg_mx = {l: CastGroup(K, f"cg_mx{l}") for l in layers}

    def cast_mlp_weights(l, gate=None):
        src_u = w_up[l].rearrange("(k p) (mg c) -> mg p k c", p=128, c=512)
        src_d = w_down[l].rearrange("(k p) (m c) -> m p k c", p=128, c=128)
        for mg in range(16):
            K.cast(cg_up[l], WU[l][mg], src_u[mg], gate)
            wu_bufs[l][mg] = cg_up[l].buf
        for m in range(16):
            K.cast(cg_dn[l], WD[l][m], src_d[m], gate)
            wd_bufs[l][m] = cg_dn[l].buf

    def prologue():
        K.begin_phase()
        xr = Ring(K, "xtok", 4, [128, D], F32)
        st = Ring(K, "xstage", 4, [128, NC_, 128], F32)
        for tb in range(NB):
            xt_, xb = xr.next()
            K.dma(K.sp, xt_[:], x_in[tb * 128:(tb + 1) * 128, :], writes=[xb])
            sg, sgb = st.next()
            for q in range(4):
                bank, bb = K.next_bank()

                def fn(h, q=q, bank=bank, xt_=xt_):
                    ins = None
                    for j in range(4):
                        c = q * 4 + j
                        ins = h.transpose(out=bank[:, j * 128:(j + 1) * 128], in_=xt_[:, c * 128:(c + 1) * 128],
                                          identity=ident[:])
                    return ins

                K.op(K.pe, fn, reads=[xb, ident_b], writes=[bb])
                eng = K.act if q % 2 == 0 else K.dve
                if eng is K.act:
                    K.op(eng, lambda h, q=q, bank=bank, sg=sg: h.activation(
                        out=sg[:, q * 4:(q + 1) * 4, :], in_=bank[:].rearrange("p (j t) -> p j t", j=4),
                        func=AF.Copy), reads=[bb], writes=[sgb])
                else:
                    K.op(eng, lambda h, q=q, bank=bank, sg=sg: h.tensor_copy(
                        out=sg[:, q * 4:(q + 1) * 4, :], in_=bank[:].rearrange("p (j t) -> p j t", j=4)),
                        reads=[bb], writes=[sgb])
            t = tb // 4
            K.dma(K.sp, XT[:, :, tb * 128:(tb + 1) * 128].rearrange("c p t -> p c t"), sg[:],
                  reads=[sgb], writes=[xt_bufs[c][t] for c in range(NC_)])
        K.end_phase()

    def epilogue():
        K.begin_phase()
        ld = Ring(K, "eld", 4, [128, NC_, 128], F32)
        ot = Ring(K, "eout", 4, [128, D], F32)
        for tb in range(NB):
            t = tb // 4
            lt, lb = ld.next()
            K.dma(K.sp, lt[:], XT[:, :, tb * 128:(tb + 1) * 128].rearrange("c p t -> p c t"),
                  reads=[xt_bufs[c][t] for c in range(NC_)], writes=[lb])
            og, ogb = ot.next()
            for q in range(4):
                bank, bb = K.next_bank()

                def fn(h, q=q, bank=bank, lt=lt):
                    ins = None
                    for j in range(4):
                        c = q * 4 + j
                        ins = h.transpose(out=bank[:, j * 128:(j + 1) * 128], in_=lt[:, c, :], identity=ident[:])
                    return ins

                K.op(K.pe, fn, reads=[lb, ident_b], writes=[bb])
                if q % 2 == 0:
                    K.op(K.act, lambda h, q=q, bank=bank, og=og: h.activation(
                        out=og[:, q * 512:(q + 1) * 512], in_=bank[:], func=AF.Copy), reads=[bb], writes=[ogb])
                else:
                    K.op(K.dve, lambda h, q=q, bank=bank, og=og: h.tensor_copy(
                        out=og[:, q * 512:(q + 1) * 512], in_=bank[:]), reads=[bb], writes=[ogb])
            K.dma(K.sp, y_out[tb * 128:(tb + 1) * 128, :], og[:], reads=[ogb], writes=[Buf()])
        K.end_phase()

    class Norm:
        def __init__(self, gT, gT_b, HT, ht_bufs):
            self.xr = Ring(K, "nx", 4, [128, TT], F32)
            self.sq = Ring(K, "nsq", 2, [128, TT], BF16)
            self.rs = K.sb("nrs", [128, TT], F32)
            self.rs_b = Buf("nrs")
            self.rstd = K.sb("nrstd", [128, TT], F32)
            self.rstd_b = Buf("nrstd")
            self.gT, self.gT_b, self.HT, self.ht_bufs = gT, gT_b, HT, ht_bufs
            self.stat_bank = K.banks[7]
            self.stat_bb = K.bank_bufs[7]

        def stage_a(self, l, t):
            pend = []
            for c in range(NC_):
                xt_, xb = self.xr.next()
                K.dma(K.sp, xt_[:], XT[c][:, t * TT:(t + 1) * TT], reads=[xt_bufs[c][t]], writes=[xb])
                sq, sqb = self.sq.next()
                K.op(K.act, lambda h, sq=sq, xt_=xt_: h.activation(out=sq[:], in_=xt_[:], func=AF.Square),
                     reads=[xb], writes=[sqb])
                K.op(K.pe, lambda h, sq=sq, c=c: h.matmul(self.stat_bank[:], lhsT=ones_d[:], rhs=sq[:],
                                                         start=(c == 0), stop=(c == NC_ - 1)),
                     reads=[sqb, ones_d_b], writes=[self.stat_bb])
            K.op(K.act, lambda h: h.activation(out=self.rs[:], in_=self.stat_bank[:], func=AF.Sqrt, bias=EPS,
                                               scale=1.0), reads=[self.stat_bb], writes=[self.rs_b])
            K.op(K.dve, lambda h: h.reciprocal(out=self.rstd[:], in_=self.rs[:]), reads=[self.rs_b],
                 writes=[self.rstd_b])

        def stage_b(self, l, t):
            for c in range(NC_):
                xt_, xb = self.xr.next()
                K.dma(K.sp, xt_[:], XT[c][:, t * TT:(t + 1) * TT], reads=[xt_bufs[c][t]], writes=[xb])
                K.op(K.dve, lambda h, xt_=xt_, c=c: h.scalar_tensor_tensor(
                    out=self.HT[:, c, :], in0=xt_[:], scalar=self.gT[:, l, c:c + 1], in1=self.rstd[:],
                    op0=ALU.mult, op1=ALU.mult), reads=[xb, self.gT_b, self.rstd_b], writes=[self.ht_bufs[c]])

    def mlp_phase(l):
        K.begin_phase()
        HT = K.sb("HT", [128, NC_, TT], BF16)
        ht_bufs = [Buf(f"ht{c}") for c in range(NC_)]
        AT = K.sb("AT", [128, 64, TT], BF16)
        at_bufs = [Buf(f"at{c}") for c in range(64)]
        wus = Ring(K, "wus", 2, [128, 16, 512], BF16)
        wds = Ring(K, "wds", 2, [128, 64, 128], BF16)
        xres = Ring(K, "xres", 2, [128, TT], F32)
        ores = Ring(K, "ores", 3, [128, TT], F32)
        relu_r = Ring(K, "relu_r", 3, [128, TT], F32)
        norm = Norm(mlp_gT, mlp_gT_b, HT, ht_bufs)

        loads = []
        for t in range(NT):
            for mg in range(16):
                loads.append(("u", t, mg))
            for m in range(16):
                loads.append(("d", t, m))
        slot_of = {}

        def issue_load(i):
            if i >= len(loads):
                return
            kind, t, j = loads[i]
            if kind == "u":
                s, sb_ = wus.next()
                K.dma(K.sp, s[:], WU[l][j], reads=[wu_bufs[l][j]], writes=[sb_])
            else:
                s, sb_ = wds.next()
                K.dma(K.sp, s[:], WD[l][j], reads=[wd_bufs[l][j]], writes=[sb_])
            slot_of[i] = (s, sb_)

        norm.stage_a(l, 0)
        norm.stage_b(l, 0)
        issue_load(0)
        issue_load(1)
        li = 0
        for t in range(NT):
            for mg in range(16):
                s, sb_ = slot_of.pop(li)
                for j in range(4):
                    bank, bb = K.next_bank()
                    pairs = [(s[:, k, j * 128:(j + 1) * 128], HT[:, k, :]) for k in range(NC_)]
                    K.mm_group(bank[:], bb, pairs, reads=[sb_] + ht_bufs)
                    ci = mg * 4 + j
                    r_, rb = relu_r.next()
                    K.op(K.act, lambda h, bank=bank, r_=r_: h.activation(out=r_[:], in_=bank[:], func=AF.Relu),
                         reads=[bb], writes=[rb])
                    K.op(K.dve, lambda h, r_=r_, ci=ci: h.tensor_tensor(
                        out=AT[:, ci, :], in0=r_[:], in1=r_[:], op=ALU.mult),
                        reads=[rb], writes=[at_bufs[ci]])
                li += 1
                issue_load(li + 1)
            for m in range(16):
                s, sb_ = slot_of.pop(li)
                bank, bb = K.next_bank()
                pairs = [(s[:, k, :], AT[:, k, :]) for k in range(64)]
                K.mm_group(bank[:], bb, pairs, reads=[sb_] + at_bufs)
                xr_, xrb = xres.next()
                K.dma(K.sp, xr_[:], XT[m][:, t * TT:(t + 1) * TT], reads=[xt_bufs[m][t]], writes=[xrb])
                o_, ob = ores.next()
                K.op(K.dve, lambda h, bank=bank, xr_=xr_, o_=o_: h.tensor_tensor(
                    out=o_[:], in0=bank[:], in1=xr_[:], op=ALU.add), reads=[bb, xrb], writes=[ob])
                K.dma(K.sp, XT[m][:, t * TT:(t + 1) * TT], o_[:], reads=[ob], writes=[xt_bufs[m][t]])
                li += 1
                issue_load(li + 1)
                if t + 1 < NT:
                    if m == 3:
                        norm.stage_a(l, t + 1)
                    if m == 9:
                        norm.stage_b(l, t + 1)
        K.end_phase()


    def small(name, shape, dt=F32):
        return K.sb(name, shape, dt, glob=True), Buf(name)

    ones_512, ones_512_b = small("ones_512", [128, 128], BF16)
    ones_1, ones_1_b = small("ones_1", [128, 128], BF16)
    K.op(K.dve, lambda h: h.memset(ones_512[:], 1.0 / 512), writes=[ones_512_b])
    K.op(K.dve, lambda h: h.memset(ones_1[:], 1.0), writes=[ones_1_b])
    halfpi, halfpi_b = small("halfpi", [128, 1], F32)
    K.op(K.dve, lambda h: h.memset(halfpi[:], float(np.pi / 2)), writes=[halfpi_b])
    ones_f, ones_f_b = small("ones_f", [1, 128], F32)
    K.op(K.dve, lambda h: h.memset(ones_f[:], 1.0), writes=[ones_f_b])
    tri_f, tri_f_b = small("tri_f", [128, 128], F32)
    K.dma(K.sp, tri_f[:], tri_in, writes=[tri_f_b])
    tri, tri_b = small("tri", [128, 128], BF16)
    K.op(K.dve, lambda h: h.tensor_copy(out=tri[:], in_=tri_f[:]), reads=[tri_f_b], writes=[tri_b])
    ropec, ropec_b = small("ropec", [64, 2], F32)
    K.dma(K.sp, ropec[:], rope_in, writes=[ropec_b])
    cwT, cwT_b = small("cwT", [128, 2, 3, NC_], F32)
    psT, psT_b = small("psT", [128, 2, 8], F32)
    qagT, qagT_b = small("qagT", [128, 2, 4], F32)
    kvagT, kvagT_b = small("kvagT", [128, 2, 4], F32)
    GQK, GQK_b = small("GQK", [128, 2, 2, 3], F32)
    load_transposed(
        [(0, 96, o_conv_w.rearrange("o j (c p) -> (o j c) p", p=128)),
         (96, 16, e_pool_s.rearrange("e (c p) -> (e c) p", p=128)),
         (112, 8, e_qa_g.rearrange("e (c p) -> (e c) p", p=128)),
         (120, 8, e_kva_g.rearrange("e (c p) -> (e c) p", p=128))],
        [(0, 96, cwT[:].rearrange("p o j c -> p (o j c)"), cwT_b),
         (96, 16, psT[:].rearrange("p e c -> p (e c)"), psT_b),
         (112, 8, qagT[:].rearrange("p e c -> p (e c)"), qagT_b),
         (120, 8, kvagT[:].rearrange("p e c -> p (e c)"), kvagT_b)])
    gparts = []
    for e in range(2):
        for qk, g in enumerate((e_qn_g, e_kn_g)):
            r = (e * 2 + qk) * 3
            gparts.append((r, 1, g[e:e + 1, 0:128]))
            gparts.append((r + 1, 1, g[e:e + 1, 128:192]))
            gparts.append((r + 2, 1, g[e:e + 1, 160:192]))
            gparts.append((r + 2, 1, g[e:e + 1, 128:160]))
    K.begin_phase()
    stg = K.sb("gstage", [128, 128], F32)
    stg_b = Buf()
    K.op(K.dve, lambda h: h.memset(stg[:], 0.0), writes=[stg_b])
    for e in range(2):
        for qk, g in enumerate((e_qn_g, e_kn_g)):
            r = (e * 2 + qk) * 3
            K.dma(K.sp, stg[r:r + 1, 0:128], g[e:e + 1, 0:128], writes=[stg_b])
            K.dma(K.sp, stg[r + 1:r + 2, 0:64], g[e:e + 1, 128:192], writes=[stg_b])
            K.dma(K.sp, stg[r + 2:r + 3, 0:32], g[e:e + 1, 160:192], writes=[stg_b])
            K.dma(K.sp, stg[r + 2:r + 3, 32:64], g[e:e + 1, 128:160], writes=[stg_b])
    bank, bb = K.next_bank()
    K.op(K.pe, lambda h: h.transpose(out=bank[:, 0:128], in_=stg[:], identity=ident[:]),
         reads=[stg_b, ident_b], writes=[bb])
    K.op(K.dve, lambda h: h.tensor_copy(out=GQK[:].rearrange("p e q k -> p (e q k)"), in_=bank[:, 0:12]),
         reads=[bb], writes=[GQK_b])
    K.end_phase()

    def dscr(name, shape, dt):
        return nc.dram_tensor(name, list(shape), dt).ap()

    C2s = dscr("C2s", [64, S], F32)
    S2s = dscr("S2s", [64, S], F32)
    cs_bufs = [Buf() for _ in range(NT)]
    CQN = dscr("CQN", [4, 128, S], BF16)
    CKVN = dscr("CKVN", [4, 128, S], BF16)
    cqn_bufs = [Buf() for _ in range(NT)]
    ckvn_bufs = [Buf() for _ in range(NT)]
    KRBs = dscr("KRBs", [64, S], F32)
    KRSSQ = dscr("KRSSQ", [128, S], F32)
    krb_bufs = [Buf() for _ in range(NT)]
    krs_bufs = [Buf() for _ in range(NT)]
    MIXT = dscr("MIXT", [NC_, 128, S], BF16)
    mixt_bufs = [[Buf() for _ in range(NT)] for _ in range(NC_)]
    EW = {}
    OW = {}
    for l in layers:
        if l % 2 == 0:
            e = l // 2
            EW[e] = dict(
                WEI=dscr(f"WEI{e}", [4, 128, 16, 512], BF16), WEIR=dscr(f"WEIR{e}", [128, 16, 128], BF16),
                WUQ=dscr(f"WUQ{e}", [128, 4, 1536], BF16), WUQS=dscr(f"WUQS{e}", [128, 4, 8, 64], BF16),
                WUKV=dscr(f"WUKV{e}", [128, 4, 2048], BF16), WEO=dscr(f"WEO{e}", [128, 16, 2048], BF16),
                PW=dscr(f"PW{e}", [128, 4, 2, 256], BF16), b=Buf())
        else:
            o = l // 2
            OW[o] = dict(WCI=dscr(f"WCI{o}", [16, 128, 3, 16, 128], BF16), WCO=dscr(f"WCO{o}", [128, 16, 2048], BF16),
                         b=Buf())

    def cast_mixer_weights(l, gate=None):
        if l % 2 == 0:
            e = l // 2
            W = EW[e]
            W["b"] = cg_mx[l].buf
            cols = [(0, 512), (512, 1024), (1088, 1600), (1600, 2112)]
            for g, (a, b_) in enumerate(cols):
                K.cast(cg_mx[l], W["WEI"][g], e_w_in[e][:, a:b_].rearrange("(k p) c -> p k c", p=128), gate)
            for (da, db, sa, sb_) in ((0, 64, 1024, 1088), (64, 96, 1056, 1088), (96, 128, 1024, 1056)):
                K.cast(cg_mx[l], W["WEIR"][:, :, da:db], e_w_in[e][:, sa:sb_].rearrange("(k p) c -> p k c", p=128), gate)
            K.cast(cg_mx[l], W["WUQ"], e_w_uq[e].rearrange("(k p) n -> p k n", p=128), gate)
            for k in range(4):
                v = e_w_uq[e][k * 128:(k + 1) * 128, :].rearrange("p (h d) -> p h d", d=192)
                K.cast(cg_mx[l], W["WUQS"][:, k, :, 0:32], v[:, :, 160:192], gate)
                K.cast(cg_mx[l], W["WUQS"][:, k, :, 32:64], v[:, :, 128:160], gate)
            K.cast(cg_mx[l], W["WUKV"], e_w_ukv[e].rearrange("(k p) n -> p k n", p=128), gate)
            for kq in range(4):
                K.cast(cg_mx[l], W["WEO"][:, kq * 4:(kq + 1) * 4, :],
                      e_w_out[e][kq * 512:(kq + 1) * 512, :].rearrange("(k p) n -> p k n", p=128), gate)
            for g in range(4):
                K.cast(cg_mx[l], W["PW"][:, g], e_pool_w[e][g].rearrange("(c p) d -> p c d", p=128), gate)
        else:
            o = l // 2
            W = OW[o]
            W["b"] = cg_mx[l].buf
            for c in range(NC_):
                for j in range(3):
                    K.cast(cg_mx[l], W["WCI"][c][:, j],
                          o_w_in[o][:, j * D + c * 128: j * D + (c + 1) * 128].rearrange("(k p) q -> p k q", p=128), gate)
            for kq in range(4):
                K.cast(cg_mx[l], W["WCO"][:, kq * 4:(kq + 1) * 4, :],
                      o_w_out[o][kq * 512:(kq + 1) * 512, :].rearrange("(k p) n -> p k n", p=128), gate)

    def load_resident(dst, src, src_b, dst_b, nsplit, axis_len):
        step = axis_len // nsplit
        for i in range(nsplit):
            K.dma(K.sp, dst[:, i * step:(i + 1) * step], src[:, i * step:(i + 1) * step], reads=[src_b], writes=[dst_b])

    def outproj_tile(W, W_b, MT, mt_reads, t, xres, ores, hook=None):
        for m in range(NC_):
            bank, bb = K.next_bank()
            pairs = [(W[:, k, m * 128:(m + 1) * 128], MT[:, k, :]) for k in range(NC_)]
            K.mm_group(bank[:], bb, pairs, reads=[W_b] + mt_reads)
            xr_, xrb = xres.next()
            K.dma(K.sp, xr_[:], XT[m][:, t * TT:(t + 1) * TT], reads=[xt_bufs[m][t]], writes=[xrb])
            o_, ob = ores.next()
            K.op(K.dve, lambda h, bank=bank, xr_=xr_, o_=o_: h.tensor_tensor(
                out=o_[:], in0=bank[:], in1=xr_[:], op=ALU.add), reads=[bb, xrb], writes=[ob])
            K.dma(K.sp, XT[m][:, t * TT:(t + 1) * TT], o_[:], reads=[ob], writes=[xt_bufs[m][t]])
            if hook is not None:
                hook(m)

    def odd_mixer(l):
        o = l // 2
        W = OW[o]
        K.begin_phase()
        HT = K.sb("HT", [128, NC_, TT], BF16)
        ht_bufs = [Buf() for _ in range(NC_)]
        ZT = K.sb("ZT", [128, NC_, TT], BF16)
        zt_bufs = [Buf() for _ in range(NC_)]
        WCO = K.sb("WCO", [128, NC_, D], BF16)
        WCO_b = Buf()
        load_resident(WCO, W["WCO"], W["b"], WCO_b, 4, NC_)
        slots = Ring(K, "wci", 2, [128, 3, NC_, 128], BF16)
        VH = K.sb("VH", [128, NC_, 2], F32)
        vh_bufs = [Buf() for _ in range(NC_)]
        K.op(K.dve, lambda h: h.memset(VH[:], 0.0), writes=vh_bufs)
        vr = Ring(K, "vr", 2, [128, TT + 2], F32)
        gcr = Ring(K, "gcr", 2, [128, TT], F32)
        tmpa = Ring(K, "tmpa", 2, [128, TT], F32)
        tmpb = Ring(K, "tmpb", 2, [128, TT], F32)
        xres = Ring(K, "xres", 2, [128, TT], F32)
        ores = Ring(K, "ores", 3, [128, TT], F32)
        norm = Norm(mix_gT, mix_gT_b, HT, ht_bufs)
        order = [(t, c) for t in range(NT) for c in range(NC_)]
        slot_of = {}

        def issue(i):
            if i >= len(order):
                return
            t, c = order[i]
            s_, sb_ = slots.next()
            K.dma(K.sp, s_[:], W["WCI"][c], reads=[W["b"]], writes=[sb_])
            slot_of[i] = (s_, sb_)

        norm.stage_a(l, 0)
        norm.stage_b(l, 0)
        issue(0)
        issue(1)
        li = 0
        for t in range(NT):
            for c in range(NC_):
                s_, sb_ = slot_of.pop(li)
                bks = []
                for j in range(3):
                    bank, bb = K.next_bank()
                    pairs = [(s_[:, j, k, :], HT[:, k, :]) for k in range(NC_)]
                    K.mm_group(bank[:], bb, pairs, reads=[sb_] + ht_bufs)
                    bks.append((bank, bb))
                li += 1
                issue(li + 1)
                (bB, bBb), (bC, bCb), (bU, bUb) = bks
                gc_, gcb = gcr.next()
                K.op(K.act, lambda h, gc_=gc_, bC=bC: h.activation(out=gc_[:], in_=bC[:], func=AF.Copy),
                     reads=[bCb], writes=[gcb])
                v_, vb = vr.next()
                K.op(K.dve, lambda h, v_=v_, c=c: h.tensor_copy(out=v_[:, 0:2], in_=VH[:, c, :]),
                     reads=[vh_bufs[c]], writes=[vb])
                K.op(K.dve, lambda h, v_=v_, gc_=gc_, bU=bU: h.tensor_tensor(
                    out=v_[:, 2:TT + 2], in0=bU[:], in1=gc_[:], op=ALU.mult), reads=[bUb, gcb, vb], writes=[vb])
                K.op(K.dve, lambda h, v_=v_, c=c: h.tensor_copy(out=VH[:, c, :], in_=v_[:, TT:TT + 2]),
                     reads=[vb], writes=[vh_bufs[c]])
                ta, tab = tmpa.next()
                tb_, tbb = tmpb.next()
                K.op(K.dve, lambda h, ta=ta, v_=v_, c=c: h.tensor_scalar(
                    out=ta[:], in0=v_[:, 0:TT], scalar1=cwT[:, o, 0, c:c + 1], scalar2=None, op0=ALU.mult),
                    reads=[vb, cwT_b], writes=[tab])
                K.op(K.dve, lambda h, ta=ta, tb_=tb_, v_=v_, c=c: h.scalar_tensor_tensor(
                    out=tb_[:], in0=v_[:, 1:TT + 1], scalar=cwT[:, o, 1, c:c + 1], in1=ta[:], op0=ALU.mult,
                    op1=ALU.add), reads=[vb, tab, cwT_b], writes=[tbb])
                K.op(K.dve, lambda h, ta=ta, tb_=tb_, v_=v_, c=c: h.scalar_tensor_tensor(
                    out=ta[:], in0=v_[:, 2:TT + 2], scalar=cwT[:, o, 2, c:c + 1], in1=tb_[:], op0=ALU.mult,
                    op1=ALU.add), reads=[vb, tbb, cwT_b], writes=[tab])
                K.op(K.dve, lambda h, ta=ta, bB=bB, c=c: h.tensor_tensor(
                    out=ZT[:, c, :], in0=bB[:], in1=ta[:], op=ALU.mult), reads=[bBb, tab], writes=[zt_bufs[c]])

            def hook(m, t=t):
                if t + 1 < NT:
                    if m == 3:
                        norm.stage_a(l, t + 1)
                    if m == 9:
                        norm.stage_b(l, t + 1)

            outproj_tile(WCO, WCO_b, ZT, zt_bufs, t, xres, ores, hook)
        K.end_phase()

    def rope_tables():
        K.begin_phase()
        posi = K.sb("posi", [64, S], I32)
        posi_b = Buf()
        K.dma(K.sp, posi[:], pos_in.broadcast_to([64, S]),
              writes=[posi_b])
        ang = K.sb("ang", [64, S], F32)
        ang_b = Buf()
        K.op(K.dve, lambda h: h.tensor_copy(out=ang[:], in_=posi[:]), reads=[posi_b], writes=[ang_b])
        K.op(K.dve, lambda h: h.tensor_scalar(out=ang[:], in0=ang[:], scalar1=ropec[:, 0:1], scalar2=None,
                                              op0=ALU.mult), reads=[ang_b, ropec_b], writes=[ang_b])
        mr = Ring(K, "mr", 2, [64, TT], F32)
        ki = Ring(K, "ki", 2, [64, TT], I32)
        kf = Ring(K, "kf", 2, [64, TT], F32)
        fl = Ring(K, "fl", 2, [64, TT], F32)
        orr = Ring(K, "orr", 4, [64, TT], F32)
        PI = float(np.pi)
        C1 = 6.28125
        C2 = float(2.0 * np.pi - 6.28125)
        for t in range(NT):
            sl = slice(t * TT, (t + 1) * TT)
            m_, mb = mr.next()
            k_i, kib = ki.next()
            k_f, kfb = kf.next()
            f_, fb = fl.next()
            K.op(K.dve, lambda h: h.tensor_scalar(out=m_[:], in0=ang[:, sl], scalar1=float(1.0 / (2 * np.pi)),
                                                  scalar2=None, op0=ALU.mult), reads=[ang_b], writes=[mb])
            K.op(K.dve, lambda h: h.tensor_copy(out=k_i[:], in_=m_[:]), reads=[mb], writes=[kib])
            K.op(K.dve, lambda h: h.tensor_copy(out=k_f[:], in_=k_i[:]), reads=[kib], writes=[kfb])
            K.op(K.dve, lambda h: h.scalar_tensor_tensor(out=m_[:], in0=k_f[:], scalar=-C1, in1=ang[:, sl],
                                                         op0=ALU.mult, op1=ALU.add), reads=[kfb, ang_b, mb],
                 writes=[mb])
            K.op(K.dve, lambda h: h.scalar_tensor_tensor(out=m_[:], in0=k_f[:], scalar=-C2, in1=m_[:],
                                                         op0=ALU.mult, op1=ALU.add), reads=[kfb, mb], writes=[mb])
            K.op(K.dve, lambda h: h.tensor_single_scalar(out=f_[:], in_=m_[:], scalar=PI, op=ALU.is_gt),
                 reads=[mb], writes=[fb])
            K.op(K.dve, lambda h: h.scalar_tensor_tensor(out=m_[:], in0=f_[:], scalar=-2 * PI, in1=m_[:],
                                                         op0=ALU.mult, op1=ALU.add), reads=[fb, mb], writes=[mb])
            K.op(K.dve, lambda h: h.tensor_single_scalar(out=f_[:], in_=m_[:], scalar=-PI, op=ALU.is_lt),
                 reads=[mb, fb], writes=[fb])
            K.op(K.dve, lambda h: h.scalar_tensor_tensor(out=m_[:], in0=f_[:], scalar=2 * PI, in1=m_[:],
                                                         op0=ALU.mult, op1=ALU.add), reads=[fb, mb], writes=[mb])
            K.op(K.dve, lambda h: h.tensor_scalar(out=m_[:], in0=m_[:], scalar1=PI, scalar2=-PI, op0=ALU.min,
                                                  op1=ALU.max), reads=[mb], writes=[mb])
            o_, ob = orr.next()
            K.op(K.act, lambda h: h.activation(out=o_[:], in_=m_[:], func=AF.Sin), reads=[mb], writes=[ob])
            K.op(K.dve, lambda h: h.tensor_scalar(out=o_[:], in0=o_[:], scalar1=ropec[:, 1:2], scalar2=None,
                                                  op0=ALU.mult), reads=[ob, ropec_b], writes=[ob])
            K.dma(K.sp, S2s[:, sl], o_[:], reads=[ob], writes=[cs_bufs[t]])
            K.op(K.dve, lambda h: h.scalar_tensor_tensor(out=f_[:], in0=m_[:], scalar=-1.0, in1=m_[:],
                                                         op0=ALU.mult, op1=ALU.max), reads=[mb, fb], writes=[fb])
            o2, o2b = orr.next()
            K.op(K.act, lambda h: h.activation(out=o2[:], in_=f_[:], func=AF.Sin, bias=halfpi[0:64, :], scale=-1.0),
                 reads=[fb, halfpi_b], writes=[o2b])
            K.dma(K.sp, C2s[:, sl], o2[:], reads=[o2b], writes=[cs_bufs[t]])
        K.end_phase()


    negc = K.sb("negc", [128, 2], F32, glob=True)
    negc_b = Buf()

    def even_setup(e):
        K.begin_phase()
        row = K.sb("grow", [1, 2, 192], F32)
        row_b = Buf()
        K.dma(K.sp, row[:, 0, :], e_qn_g[e:e + 1, :], writes=[row_b])
        K.dma(K.sp, row[:, 1, :], e_kn_g[e:e + 1, :], writes=[row_b])
        K.op(K.dve, lambda h: h.scalar_tensor_tensor(out=row[:], in0=row[:], scalar=-1.0, in1=row[:],
                                                     op0=ALU.mult, op1=ALU.max), reads=[row_b], writes=[row_b])
        mx = K.sb("gmx", [1, 2], F32)
        mx_b = Buf()
        K.op(K.dve, lambda h: h.tensor_reduce(out=mx[:], in_=row[:], axis=mybir.AxisListType.X, op=ALU.max),
             reads=[row_b], writes=[mx_b])
        pr = K.sb("gpr", [1, 1], F32)
        pr_b = Buf()
        K.op(K.dve, lambda h: h.tensor_tensor(out=pr[:], in0=mx[:, 0:1], in1=mx[:, 1:2], op=ALU.mult),
             reads=[mx_b], writes=[pr_b])
        K.op(K.dve, lambda h: h.tensor_scalar(out=pr[:], in0=pr[:], scalar1=-float(np.sqrt(192.0)), scalar2=None,
                                              op0=ALU.mult), reads=[pr_b], writes=[pr_b])
        bank, bb = K.next_bank()
        K.op(K.pe, lambda h: h.matmul(bank[:, 0:1], lhsT=ones_f[0:1, :], rhs=pr[0:1, 0:1], start=True, stop=True),
             reads=[pr_b, ones_f_b], writes=[bb])
        K.op(K.act, lambda h: h.activation(out=negc[:, e:e + 1], in_=bank[:, 0:1], func=AF.Copy),
             reads=[bb], writes=[negc_b])
        K.end_phase()

    def even_e1(l):
        e = l // 2
        W = EW[e]
        K.begin_phase()
        HT = K.sb("HT", [128, NC_, TT], BF16)
        ht_bufs = [Buf() for _ in range(NC_)]
        norm = Norm(mix_gT, mix_gT_b, HT, ht_bufs)
        WI = K.sb("WI", [128, 4, NC_, 512], BF16)
        WI_b = Buf()
        for g in range(4):
            K.dma(K.sp, WI[:, g], W["WEI"][g], reads=[W["b"]], writes=[WI_b])
        WIR = K.sb("WIR", [128, NC_, 128], BF16)
        WIR_b = Buf()
        K.dma(K.sp, WIR[:], W["WEIR"], reads=[W["b"]], writes=[WIR_b])
        PWs = K.sb("PWs", [128, 4, 2, 256], BF16)
        PWs_b = Buf()
        K.dma(K.sp, PWs[:], W["PW"], reads=[W["b"]], writes=[PWs_b])
        RC = K.sb("RC", [128, 4, TT], F32)
        RC_b = Buf()
        K.dma(K.sp, RC[:], rc_in, writes=[RC_b])
        raw = K.sb("raw", [128, 4, TT], F32)
        raw_bufs = [Buf() for _ in range(4)]
        sqr = Ring(K, "sqr", 5, [128, TT], BF16)
        rs2 = K.sb("rs2", [128, TT], F32)
        rs2_b = Buf()
        rstd2 = K.sb("rstd2", [128, TT], F32)
        rstd2_b = Buf()
        latr = Ring(K, "latr", 2, [128, 4, TT], BF16)
        ctr = Ring(K, "ctr", 2, [64, 2, TT], F32)
        t12 = Ring(K, "t12", 2, [64, 2, TT], F32)
        krr = Ring(K, "krr", 2, [64, TT], F32)
        kssr = Ring(K, "kssr", 2, [128, TT], F32)
        UH = K.sb("UH", [128, 8, 16], F32)
        uh_bufs = [Buf() for _ in range(8)]
        K.op(K.dve, lambda h: h.memset(UH[:], 0.0), writes=uh_bufs)
        ur = Ring(K, "ur", 2, [128, TT + 16], F32)
        ar = Ring(K, "ar", 2, [128, TT + 16], F32)
        br_ = Ring(K, "br", 2, [128, TT + 16], F32)
        plr = Ring(K, "plr", 2, [128, 2, TT], BF16)
        btr = Ring(K, "btr", 2, [128, TT], BF16)
        tmp0 = Ring(K, "tmp0", 2, [128, TT], F32)

        def latent(t, g, gT, gT_b, DST, dst_bufs):
            bks = []
            for j in range(4):
                bank, bb = K.next_bank()
                pairs = [(WI[:, g, k, j * 128:(j + 1) * 128], HT[:, k, :]) for k in range(NC_)]
                K.mm_group(bank[:], bb, pairs, reads=[WI_b] + ht_bufs)
                K.op(K.act, lambda h, bank=bank, j=j: h.activation(out=raw[:, j, :], in_=bank[:], func=AF.Copy),
                     reads=[bb], writes=[raw_bufs[j]])
                sq, sqb = sqr.next()
                K.op(K.act, lambda h, bank=bank, sq=sq: h.activation(out=sq[:], in_=bank[:], func=AF.Square),
                     reads=[bb], writes=[sqb])
                bks.append((sq, sqb))
            for j, (sq, sqb) in enumerate(bks):
                K.op(K.pe, lambda h, sq=sq, j=j: h.matmul(norm.stat_bank[:], lhsT=ones_512[:], rhs=sq[:],
                                                         start=(j == 0), stop=(j == 3)),
                     reads=[sqb, ones_512_b], writes=[norm.stat_bb])
            K.op(K.act, lambda h: h.activation(out=rs2[:], in_=norm.stat_bank[:], func=AF.Sqrt, bias=EPS, scale=1.0),
                 reads=[norm.stat_bb], writes=[rs2_b])
            K.op(K.dve, lambda h: h.reciprocal(out=rstd2[:], in_=rs2[:]), reads=[rs2_b], writes=[rstd2_b])
            lt, ltb = latr.next()
            for j in range(4):
                K.op(K.dve, lambda h, j=j, lt=lt: h.scalar_tensor_tensor(
                    out=lt[:, j, :], in0=raw[:, j, :], scalar=gT[:, e, j:j + 1], in1=rstd2[:], op0=ALU.mult,
                    op1=ALU.mult), reads=[raw_bufs[j], rstd2_b, gT_b], writes=[ltb])
            K.dma(K.sp, DST[:, :, t * TT:(t + 1) * TT].rearrange("c p t -> p c t"), lt[:], reads=[ltb],
                  writes=[dst_bufs[t]])

        def krope(t):
            sl = slice(t * TT, (t + 1) * TT)
            bR, bRb = K.next_bank()
            K.mm_group(bR[0:64, :], bRb, [(WIR[:, k, 0:64], HT[:, k, :]) for k in range(NC_)], reads=[WIR_b] + ht_bufs)
            bS, bSb = K.next_bank()
            K.mm_group(bS[0:64, :], bSb, [(WIR[:, k, 64:128], HT[:, k, :]) for k in range(NC_)],
                       reads=[WIR_b] + ht_bufs)
            sq, sqb = sqr.next()
            K.op(K.act, lambda h: h.activation(out=sq[0:64, :], in_=bR[0:64, :], func=AF.Square), reads=[bRb],
                 writes=[sqb])
            bT, bTb = K.next_bank()
            K.op(K.pe, lambda h: h.matmul(bT[:], lhsT=ones_1[0:64, :], rhs=sq[0:64, :], start=True, stop=True),
                 reads=[sqb, ones_1_b], writes=[bTb])
            ks, ksb = kssr.next()
            K.op(K.act, lambda h: h.activation(out=ks[:], in_=bT[:], func=AF.Copy), reads=[bTb], writes=[ksb])
            K.dma(K.sp, KRSSQ[:, sl], ks[:], reads=[ksb], writes=[krs_bufs[t]])
            ct, ctb = ctr.next()
            K.dma(K.sp, ct[:, 0, :], C2s[:, sl], reads=[cs_bufs[t]], writes=[ctb])
            K.dma(K.sp, ct[:, 1, :], S2s[:, sl], reads=[cs_bufs[t]], writes=[ctb])
            tt_, ttb = t12.next()
            K.op(K.dve, lambda h: h.scalar_tensor_tensor(out=tt_[:, 0, :], in0=bR[0:64, :], scalar=GQK[0:64, e, 1, 1:2],
                                                         in1=ct[:, 0, :], op0=ALU.mult, op1=ALU.mult),
                 reads=[bRb, ctb, GQK_b], writes=[ttb])
            K.op(K.dve, lambda h: h.scalar_tensor_tensor(out=tt_[:, 1, :], in0=bS[0:64, :], scalar=GQK[0:64, e, 1, 2:3],
                                                         in1=ct[:, 1, :], op0=ALU.mult, op1=ALU.mult),
                 reads=[bSb, ctb, GQK_b], writes=[ttb])
            kr, krb = krr.next()
            K.op(K.dve, lambda h: h.tensor_tensor(out=kr[:], in0=tt_[:, 0, :], in1=tt_[:, 1, :], op=ALU.add),
                 reads=[ttb], writes=[krb])
            K.dma(K.sp, KRBs[:, sl], kr[:], reads=[krb], writes=[krb_bufs[t]])

        def pool(t):
            for g in range(4):
                w = 2 << g
                pl, plb = plr.next()
                for ci in range(2):
                    i = g * 2 + ci
                    bank, bb = K.next_bank()
                    pairs = [(WI[:, 2 + i // 4, k, (i % 4) * 128:(i % 4 + 1) * 128], HT[:, k, :]) for k in range(NC_)]
                    K.mm_group(bank[:], bb, pairs, reads=[WI_b] + ht_bufs)
                    u_, ub = ur.next()
                    K.op(K.dve, lambda h, u_=u_, i=i: h.tensor_copy(out=u_[:, 0:16], in_=UH[:, i, :]),
                         reads=[uh_bufs[i]], writes=[ub])
                    K.op(K.act, lambda h, u_=u_, bank=bank: h.activation(out=u_[:, 16:TT + 16], in_=bank[:],
                                                                         func=AF.Copy), reads=[bb, ub], writes=[ub])
                    K.op(K.dve, lambda h, u_=u_, i=i: h.tensor_copy(out=UH[:, i, :], in_=u_[:, TT:TT + 16]),
                         reads=[ub], writes=[uh_bufs[i]])
                    a_, ab = ar.next()
                    b_, bbf = br_.next()
                    K.op(K.dve, lambda h, a_=a_, u_=u_: h.tensor_tensor(
                        out=a_[:, 1:TT + 16], in0=u_[:, 1:TT + 16], in1=u_[:, 0:TT + 15], op=ALU.add),
                        reads=[ub], writes=[ab])
                    cur, curb, oth, othb = a_, ab, b_, bbf
                    sh = 1
                    lo = 1
                    while sh * 2 < w:
                        sh *= 2
                        nlo = lo + sh
                        K.op(K.dve, lambda h, cur=cur, oth=oth, sh=sh, nlo=nlo: h.tensor_tensor(
                            out=oth[:, nlo:TT + 16], in0=cur[:, nlo:TT + 16], in1=cur[:, nlo - sh:TT + 16 - sh],
                            op=ALU.add), reads=[curb], writes=[othb])
                        cur, curb, oth, othb = oth, othb, cur, curb
                        lo = nlo
                    if t == 0:
                        tm, tmb = tmp0.next()
                        K.op(K.dve, lambda h, cur=cur, tm=tm, g=g: h.tensor_tensor(
                            out=tm[:], in0=cur[:, 16:TT + 16], in1=RC[:, g, :], op=ALU.mult),
                            reads=[curb, RC_b], writes=[tmb])
                        K.op(K.dve, lambda h, tm=tm, u_=u_, pl=pl, ci=ci: h.tensor_tensor(
                            out=pl[:, ci, :], in0=tm[:], in1=u_[:, 16:TT + 16], op=ALU.subtract),
                            reads=[tmb, ub], writes=[plb])
                    else:
                        K.op(K.dve, lambda h, cur=cur, u_=u_, pl=pl, ci=ci, w=w: h.scalar_tensor_tensor(
                            out=pl[:, ci, :], in0=cur[:, 16:TT + 16], scalar=1.0 / w, in1=u_[:, 16:TT + 16],
                            op0=ALU.mult, op1=ALU.subtract), reads=[curb, ub], writes=[plb])
                for dc in range(2):
                    bank, bb = K.next_bank()
                    pairs = [(PWs[:, g, c, dc * 128:(dc + 1) * 128], pl[:, c, :]) for c in range(2)]
                    K.mm_group(bank[:], bb, pairs, reads=[PWs_b, plb])
                    bt, btb = btr.next()
                    ch = g * 2 + dc
                    K.op(K.act, lambda h, bt=bt, bank=bank, ch=ch: h.activation(
                        out=bt[:], in_=bank[:], func=AF.Copy, scale=psT[:, e, ch:ch + 1]),
                        reads=[bb, psT_b], writes=[btb])
                    K.dma(K.sp, MIXT[8 + ch][:, t * TT:(t + 1) * TT], bt[:], reads=[btb], writes=[mixt_bufs[8 + ch][t]])

        for t in range(NT):
            norm.stage_a(l, t)
            norm.stage_b(l, t)
            latent(t, 0, qagT, qagT_b, CQN, cqn_bufs)
            latent(t, 1, kvagT, kvagT_b, CKVN, ckvn_bufs)
            krope(t)
            pool(t)
        K.end_phase()

    def even_e2(l):
        e = l // 2
        W = EW[e]
        K.begin_phase()
        K.nrot = 4
        K.bank_rr = 0
        WUQ = K.sb("WUQ", [128, 4, 1536], BF16)
        WUQS = K.sb("WUQS", [128, 4, 8, 64], BF16)
        WUKV = K.sb("WUKV", [128, 4, 2048], BF16)
        Wb = Buf()
        K.dma(K.sp, WUQ[:], W["WUQ"], reads=[W["b"]], writes=[Wb])
        K.dma(K.sp, WUQS[:], W["WUQS"], reads=[W["b"]], writes=[Wb])
        K.dma(K.sp, WUKV[:], W["WUKV"], reads=[W["b"]], writes=[Wb])
        KNs = [K.sb(f"KN{i}", [128, S], BF16) for i in range(2)]
        KRs = [K.sb(f"KR{i}", [64, S], BF16) for i in range(2)]
        VVs = [K.sb(f"VV{i}", [128, NB, 128], BF16) for i in range(2)]
        kn_bufs = [[Buf() for _ in range(NT)] for _ in range(2)]
        kr_bufs = [[Buf() for _ in range(NT)] for _ in range(2)]
        vv_bufs = [[Buf() for _ in range(NT)] for _ in range(2)]
        latr = Ring(K, "latr", 3, [128, 4, TT], BF16)
        sqr = Ring(K, "sqr", 4, [128, TT], BF16)
        kssr = Ring(K, "kssr", 2, [128, TT], F32)
        krbr = Ring(K, "krbr", 2, [64, TT], F32)
        ssr = Ring(K, "ssr", 2, [128, TT], F32)
        rsr = Ring(K, "rsr", 2, [128, TT], F32)
        rstdr = Ring(K, "rstdr", 3, [128, TT], F32)
        ctr = Ring(K, "ctr", 2, [64, 2, TT], F32)
        t12 = Ring(K, "t12", 2, [64, 3, TT], F32)
        qnr = Ring(K, "qnr", 2, [128, TT], BF16)
        qrr = Ring(K, "qrr", 2, [64, TT], BF16)
        ptr = Ring(K, "ptr", 6, [128, TT], BF16)
        linr = Ring(K, "linr", 2, [128, TT], F32)
        otr = Ring(K, "otr", 4, [128, TT], BF16)
        SCALE = float(192.0 ** -0.5)

        def build_kv(hd):
            par = hd % 2
            KN, KR, VV = KNs[par], KRs[par], VVs[par]
            for t in range(NT):
                sl = slice(t * TT, (t + 1) * TT)
                ck, ckb = latr.next()
                K.dma(K.sp, ck[:], CKVN[:, :, sl].rearrange("c p t -> p c t"), reads=[ckvn_bufs[t]], writes=[ckb])
                ks, ksb = kssr.next()
                K.dma(K.sp, ks[:], KRSSQ[:, sl], reads=[krs_bufs[t]], writes=[ksb])
                kb_, kbb = krbr.next()
                K.dma(K.sp, kb_[:], KRBs[:, sl], reads=[krb_bufs[t]], writes=[kbb])
                bank, bb = K.next_bank()
                K.mm_group(bank[:], bb, [(WUKV[:, k, hd * 256:hd * 256 + 128], ck[:, k, :]) for k in range(4)],
                           reads=[Wb, ckb])
                sq, sqb = sqr.next()
                K.op(K.act, lambda h: h.activation(out=sq[:], in_=bank[:], func=AF.Square), reads=[bb], writes=[sqb])
                bV, bVb = K.next_bank()

                def fnv(h):
                    ins = None
                    for blk in range(4):
                        for k in range(4):
                            ins = h.matmul(bV[:, blk * 128:(blk + 1) * 128], lhsT=ck[:, k, blk * 128:(blk + 1) * 128],
                                           rhs=WUKV[:, k, hd * 256 + 128:hd * 256 + 256], start=(k == 0),
                                           stop=(k == 3))
                    return ins

                K.op(K.pe, fnv, reads=[Wb, ckb], writes=[bVb])
                K.op(K.act, lambda h: h.activation(
                    out=VV[:, t * 4:(t + 1) * 4, :], in_=bV[:].rearrange("p (b d) -> p b d", b=4), func=AF.Copy),
                    reads=[bVb], writes=[vv_bufs[par][t]])
                bT, bTb = K.next_bank()
                K.op(K.pe, lambda h: h.matmul(bT[:], lhsT=ones_1[:], rhs=sq[:], start=True, stop=True),
                     reads=[sqb, ones_1_b], writes=[bTb])
                ss, ssb = ssr.next()
                K.op(K.dve, lambda h: h.tensor_tensor(out=ss[:], in0=bT[:], in1=ks[:], op=ALU.add),
                     reads=[bTb, ksb], writes=[ssb])
                rs, rsb = rsr.next()
                K.op(K.act, lambda h: h.activation(out=rs[:], in_=ss[:], func=AF.Sqrt, bias=EPS, scale=1.0 / 192),
                     reads=[ssb], writes=[rsb])
                rstd, rstdb = rstdr.next()
                K.op(K.dve, lambda h: h.reciprocal(out=rstd[:], in_=rs[:]), reads=[rsb], writes=[rstdb])
                K.op(K.dve, lambda h: h.scalar_tensor_tensor(
                    out=KN[:, sl], in0=bank[:], scalar=GQK[:, e, 1, 0:1], in1=rstd[:], op0=ALU.mult, op1=ALU.mult),
                    reads=[bb, rstdb, GQK_b], writes=[kn_bufs[par][t]])
                K.op(K.dve, lambda h: h.tensor_tensor(out=KR[:, sl], in0=kb_[:], in1=rstd[0:64, :], op=ALU.mult),
                     reads=[kbb, rstdb], writes=[kr_bufs[par][t]])

        def qpro(hd, i):
            sl = slice(i * TT, (i + 1) * TT)
            cq, cqb = latr.next()
            K.dma(K.sp, cq[:], CQN[:, :, sl].rearrange("c p t -> p c t"), reads=[cqn_bufs[i]], writes=[cqb])
            ct, ctb = ctr.next()
            K.dma(K.sp, ct[:, 0, :], C2s[:, sl], reads=[cs_bufs[i]], writes=[ctb])
            K.dma(K.sp, ct[:, 1, :], S2s[:, sl], reads=[cs_bufs[i]], writes=[ctb])
            bN, bNb = K.next_bank()
            K.mm_group(bN[:], bNb, [(WUQ[:, k, hd * 192:hd * 192 + 128], cq[:, k, :]) for k in range(4)],
                       reads=[Wb, cqb])
            bR, bRb = K.next_bank()
            K.mm_group(bR[0:64, :], bRb, [(WUQ[:, k, hd * 192 + 128:hd * 192 + 192], cq[:, k, :]) for k in range(4)],
                       reads=[Wb, cqb])
            bS, bSb = K.next_bank()
            K.mm_group(bS[0:64, :], bSb, [(WUQS[:, k, hd, :], cq[:, k, :]) for k in range(4)], reads=[Wb, cqb])
            sq1, sq1b = sqr.next()
            K.op(K.act, lambda h: h.activation(out=sq1[:], in_=bN[:], func=AF.Square), reads=[bNb], writes=[sq1b])
            sq2, sq2b = sqr.next()
            K.op(K.act, lambda h: h.activation(out=sq2[0:64, :], in_=bR[0:64, :], func=AF.Square), reads=[bRb],
                 writes=[sq2b])
            bT, bTb = K.next_bank()

            def fns(h):
                h.matmul(bT[:], lhsT=ones_1[:], rhs=sq1[:], start=True, stop=False)
                return h.matmul(bT[:], lhsT=ones_1[0:64, :], rhs=sq2[0:64, :], start=False, stop=True)

            K.op(K.pe, fns, reads=[sq1b, sq2b, ones_1_b], writes=[bTb])
            rs, rsb = rsr.next()
            K.op(K.act, lambda h: h.activation(out=rs[:], in_=bT[:], func=AF.Sqrt, bias=EPS, scale=1.0 / 192),
                 reads=[bTb], writes=[rsb])
            rstd, rstdb = rstdr.next()
            K.op(K.dve, lambda h: h.reciprocal(out=rstd[:], in_=rs[:]), reads=[rsb], writes=[rstdb])
            qn, qnb = qnr.next()
            K.op(K.dve, lambda h: h.scalar_tensor_tensor(
                out=qn[:], in0=bN[:], scalar=GQK[:, e, 0, 0:1], in1=rstd[:], op0=ALU.mult, op1=ALU.mult),
                reads=[bNb, rstdb, GQK_b], writes=[qnb])
            tt_, ttb = t12.next()
            K.op(K.dve, lambda h: h.scalar_tensor_tensor(
                out=tt_[:, 0, :], in0=bR[0:64, :], scalar=GQK[0:64, e, 0, 1:2], in1=ct[:, 0, :], op0=ALU.mult,
                op1=ALU.mult), reads=[bRb, ctb, GQK_b], writes=[ttb])
            K.op(K.dve, lambda h: h.scalar_tensor_tensor(
                out=tt_[:, 1, :], in0=bS[0:64, :], scalar=GQK[0:64, e, 0, 2:3], in1=ct[:, 1, :], op0=ALU.mult,
                op1=ALU.mult), reads=[bSb, ctb, GQK_b], writes=[ttb])
            K.op(K.dve, lambda h: h.tensor_tensor(out=tt_[:, 2, :], in0=tt_[:, 0, :], in1=tt_[:, 1, :], op=ALU.add),
                 reads=[ttb], writes=[ttb])
            qr, qrb = qrr.next()
            K.op(K.dve, lambda h: h.tensor_tensor(out=qr[:], in0=tt_[:, 2, :], in1=rstd[0:64, :], op=ALU.mult),
                 reads=[ttb, rstdb], writes=[qrb])
            return qn, qnb, qr, qrb

        def attention(hd, i, q):
            qn, qnb, qr, qrb = q
            par = hd % 2
            KN, KR, VV = KNs[par], KRs[par], VVs[par]
            sl = slice(i * TT, (i + 1) * TT)
            bO, bOb = K.banks[4 + (i % 2)], K.bank_bufs[4 + (i % 2)]
            bL, bLb = K.banks[6 + (i % 2)], K.bank_bufs[6 + (i % 2)]
            nkb = 4 * i + 4

            def emit_s(j):
                d = j - 4 * i
                c0 = 128 * d if d > 0 else 0
                bank, bb = K.next_bank()
                tk = j // 4
                K.mm_group(bank[:, c0:], bb, [(KN[:, j * 128:(j + 1) * 128], qn[:, c0:]),
                                              (KR[:, j * 128:(j + 1) * 128], qr[:, c0:])],
                           reads=[kn_bufs[par][tk], kr_bufs[par][tk], qnb, qrb])
                pt, ptb = ptr.next()
                K.op(K.act, lambda h: h.activation(out=pt[:, c0:], in_=bank[:, c0:], func=AF.Exp,
                                                   bias=negc[:, e:e + 1], scale=SCALE),
                     reads=[bb, negc_b], writes=[ptb])
                if d >= 0:
                    K.op(K.dve, lambda h: h.tensor_tensor(out=pt[:, c0:c0 + 128], in0=pt[:, c0:c0 + 128],
                                                          in1=tri[:], op=ALU.mult), reads=[ptb, tri_b], writes=[ptb])
                return pt, ptb, c0

            DEPTH = 2
            pend = [emit_s(j) for j in range(min(DEPTH, nkb))]
            for j in range(nkb):
                pt, ptb, c0 = pend.pop(0)
                if j + DEPTH < nkb:
                    pend.append(emit_s(j + DEPTH))
                tk = j // 4

                def fpv(h, pt=pt, c0=c0, j=j):
                    h.matmul(bO[:, c0:], lhsT=VV[:, j, :], rhs=pt[:, c0:], start=(j == 0), stop=(j == nkb - 1))
                    return h.matmul(bL[:, c0:], lhsT=ones_1[:], rhs=pt[:, c0:], start=(j == 0), stop=(j == nkb - 1))

                K.op(K.pe, fpv, reads=[vv_bufs[par][tk], ones_1_b, ptb], writes=[bOb, bLb])
            li_, lib = linr.next()
            K.op(K.dve, lambda h: h.reciprocal(out=li_[:], in_=bL[:]), reads=[bLb], writes=[lib])
            ot, otb = otr.next()
            K.op(K.dve, lambda h: h.tensor_tensor(out=ot[:], in0=bO[:], in1=li_[:], op=ALU.mult),
                 reads=[bOb, lib], writes=[otb])
            K.dma(K.pool, MIXT[hd][:, sl], ot[:], reads=[otb], writes=[mixt_bufs[hd][i]])

        build_kv(0)
        q = qpro(0, 0)
        for hd in range(NH):
            for i in range(NT):
                if i + 1 < NT:
                    nq = qpro(hd, i + 1)
                elif hd + 1 < NH:
                    build_kv(hd + 1)
                    nq = qpro(hd + 1, 0)
                else:
                    nq = None
                attention(hd, i, q)
                q = nq
        K.nrot = 7
        K.bank_rr = 0
        K.end_phase()

    def even_e3(l):
        e = l // 2
        W = EW[e]
        K.begin_phase()
        WEO = K.sb("WEO", [128, NC_, D], BF16)
        WEO_b = Buf()
        load_resident(WEO, W["WEO"], W["b"], WEO_b, 4, NC_)
        mtr = Ring(K, "mtr", 2, [128, NC_, TT], BF16)
        xres = Ring(K, "xres", 2, [128, TT], F32)
        ores = Ring(K, "ores", 3, [128, TT], F32)

        def load(t):
            mt, mtb = mtr.next()
            K.dma(K.sp, mt[:], MIXT[:, :, t * TT:(t + 1) * TT].rearrange("c p t -> p c t"),
                  reads=[mixt_bufs[c][t] for c in range(NC_)], writes=[mtb])
            return mt, mtb

        nxt = load(0)
        for t in range(NT):
            mt, mtb = nxt
            if t + 1 < NT:
                nxt = load(t + 1)
            outproj_tile(WEO, WEO_b, mt, [mtb], t, xres, ores)
        K.end_phase()

    def cast_layer(l, gate=None):
        if do_mixer:
            cast_mixer_weights(l, gate)
        if do_mlp:
            cast_mlp_weights(l, gate)

    cast_layer(layers[0])
    prologue()
    has_even = do_mixer and any(l % 2 == 0 for l in layers)
    if has_even:
        rope_tables()
    for li_, l in enumerate(layers):
        if li_ + 1 < len(layers):
            gate = (K.dve.sem, K.dve.cnt) if K.dve.cnt > 0 else None
            cast_layer(layers[li_ + 1], gate)
        if do_mixer:
            if l % 2 == 0:
                even_setup(l // 2)
                even_e1(l)
                even_e2(l)
                even_e3(l)
            else:
                odd_mixer(l)
        if do_mlp:
            mlp_phase(l)
    epilogue()
    K.barrier()
    K.es.close()
    return nc, K


def make_consts():
    p = np.arange(128)
    tri = (p[None, :] >= p[:, None]).astype(np.float32)
    inv_freq = (1.0 / (np.float32(10000.0) ** (np.arange(0, 64, 2, dtype=np.float32) / np.float32(64)))).astype(np.float32)
    rope = np.zeros((64, 2), np.float32)
    rope[:, 0] = np.concatenate([inv_freq, inv_freq])
    rope[:32, 1] = -1.0
    rope[32:, 1] = 1.0
    t = np.arange(TT)
    rc = np.zeros((128, 4, TT), np.float32)
    for g in range(4):
        w = 2 << g
        rc[:, g, :] = (1.0 / np.minimum(t + 1, w)).astype(np.float32)[None, :]
    return {"c_ident": np.eye(128, dtype=np.float32), "c_tri": tri, "c_rope": rope, "c_rc": rc}


_CACHE = {}


def kernel(**inputs):
    S = 4096
    n = 8
    key = ("full", S)
    if key not in _CACHE:
        _CACHE[key] = build_program(S, [0, 1, 2, 3])
    nc, _ = _CACHE[key]
    consts = make_consts()
    shared = {k: np.ascontiguousarray(v) for k, v in inputs.items() if k not in ("x", "positions")}
    in_maps = []
    for b in range(n):
        m = dict(shared)
        m.update(consts)
        m["x"] = np.ascontiguousarray(inputs["x"][b])
        m["positions"] = np.ascontiguousarray(inputs["positions"][b:b + 1])
        in_maps.append(m)
    res = run_bass_kernel_spmd(nc, in_maps, core_ids=list(range(n)))
    out = np.stack([np.asarray(r["y"]) for r in res.results], axis=0)
    return out.astype(np.float32, copy=False)
```

```python
import contextlib
import numpy as np
import ml_dtypes
import concourse.bass as bass
import concourse.mybir as mybir
from concourse.bass_utils import run_bass_kernel_spmd

F32 = mybir.dt.float32
BF16 = mybir.dt.bfloat16
I32 = mybir.dt.int32
ALU = mybir.AluOpType
AF = mybir.ActivationFunctionType

D = 2048
DFF = 8192
NC_ = 16
TT = 512
EPS = 1e-6
NH = 8
SEM_ROT = 8000


class Buf:
    __slots__ = ("w", "r", "name", "excl")

    def __init__(self, name="", excl=False):
        self.w = None
        self.r = {}
        self.name = name
        self.excl = excl


class Eng:
    def __init__(self, K, h, name, is_pe=False, ndma=0):
        self.K = K
        self.h = h
        self.name = name
        self.is_pe = is_pe
        self.sem = None
        self.cnt = 0
        self.seen = {}
        self.old = []
        self.dsems = [K.new_sem(f"{name}_d{i}") for i in range(ndma)]
        self.dcnt = [0] * ndma
        self.di = 0
        self.nsem = 0

    def wait(self, sem, val):
        if self.seen.get(sem, 0) >= val:
            return
        self.h.wait_ge(sem, val)
        self.seen[sem] = val

    def signal(self, ins):
        if self.sem is None or self.cnt >= SEM_ROT:
            if self.sem is not None:
                self.old.append((self.sem, self.cnt))
            self.sem = self.K.new_sem(f"{self.name}_s{self.nsem}")
            self.nsem += 1
            self.cnt = 0
        self.cnt += 1
        ins.then_inc(self.sem, 1)
        return (self.sem, self.cnt)


class CastGroup:
    def __init__(self, K, name):
        self.sem = K.new_sem(name)
        self.n = 0
        self.buf = Buf(name)
        self.gated = False


class KB:
    def __init__(self, nc):
        self.nc = nc
        self.es = contextlib.ExitStack()
        self.nsem = 0
        self.pe = Eng(self, nc.tensor, "pe", is_pe=True)
        self.act = Eng(self, nc.scalar, "act")
        self.dve = Eng(self, nc.vector, "dve")
        self.pool = Eng(self, nc.gpsimd, "pool", ndma=8)
        self.sp = Eng(self, nc.sync, "sp", ndma=12)
        self.engs = [self.pe, self.act, self.dve, self.pool, self.sp]
        self.banks = []
        self.bank_bufs = []
        for i in range(8):
            self.banks.append(self.es.enter_context(nc.psum_tensor(f"psb{i}", [128, 512], F32)))
            self.bank_bufs.append(Buf(f"bank{i}", excl=True))
        self.bank_rr = 0
        self.nrot = 7
        self.uid = 0
        self.phase_es = None

    def new_sem(self, name):
        self.nsem += 1
        return self.es.enter_context(self.nc.semaphore(name))

    def sb(self, name, shape, dtype, glob=False):
        self.uid += 1
        es = self.es if glob or self.phase_es is None else self.phase_es
        return es.enter_context(self.nc.sbuf_tensor(f"{name}_{self.uid}", list(shape), dtype))

    def begin_phase(self):
        self.phase_es = contextlib.ExitStack()

    def end_phase(self):
        self.barrier()
        self.phase_es.close()
        self.phase_es = None

    def next_bank(self):
        i = self.bank_rr
        self.bank_rr = (self.bank_rr + 1) % self.nrot
        return self.banks[i], self.bank_bufs[i]

    def _deps(self, reads, writes, own=None):
        deps = {}

        def add(tok):
            if tok is None:
                return
            s, v = tok
            if deps.get(s, 0) < v:
                deps[s] = v

        for b in reads:
            add(b.w)
            if b.excl:
                for s, v in b.r.items():
                    if s is not own:
                        add((s, v))
        for b in writes:
            add(b.w)
            for s, v in b.r.items():
                add((s, v))
        return deps

    def _commit(self, tok, reads, writes):
        s, v = tok
        for b in reads:
            if b.r.get(s, 0) < v:
                b.r[s] = v
        for b in writes:
            b.w = tok
            b.r = {}

    def op(self, eng, fn, reads=(), writes=()):
        deps = self._deps(reads, writes, own=eng.sem)
        for s, v in deps.items():
            if eng.is_pe and s is eng.sem:
                continue
            eng.wait(s, v)
        ins = fn(eng.h)
        tok = eng.signal(ins)
        self._commit(tok, reads, writes)
        return tok

    def dma(self, q, out, in_, reads=(), writes=(), **kw):
        deps = self._deps(reads, writes)
        i = q.di
        q.di = (q.di + 1) % len(q.dsems)
        sem = q.dsems[i]
        if q.dcnt[i] > 0:
            q.wait(sem, 16 * q.dcnt[i])
        for s, v in deps.items():
            q.wait(s, v)
        ins = q.h.dma_start(out=out, in_=in_, **kw)
        q.dcnt[i] += 1
        ins.then_inc(sem, 16)
        tok = (sem, 16 * q.dcnt[i])
        self._commit(tok, reads, writes)
        return tok

    def cast(self, grp, out, in_, gate=None):
        if gate is not None and not grp.gated:
            self.pool.wait(*gate)
        grp.gated = True
        ins = self.pool.h.dma_start(out=out, in_=in_)
        grp.n += 1
        ins.then_inc(grp.sem, 16)
        grp.buf.w = (grp.sem, 16 * grp.n)

    def barrier(self):
        toks = []
        for e in self.engs:
            if e.sem is not None and e.cnt > 0:
                toks.append((e.sem, e.cnt))
            for s, c in zip(e.dsems, e.dcnt):
                if c > 0:
                    toks.append((s, 16 * c))
        for e in self.engs:
            for s, v in toks:
                e.wait(s, v)

    def mm_group(self, out_ap, bank_buf, pairs, reads):
        n = len(pairs)

        def fn(h):
            ins = None
            for i, (l, r) in enumerate(pairs):
                ins = h.matmul(out_ap, lhsT=l, rhs=r, start=(i == 0), stop=(i == n - 1))
            return ins

        return self.op(self.pe, fn, reads=reads, writes=[bank_buf])


def _ring(K, name, n, shape, dtype):
    return [(K.sb(f"{name}{i}", shape, dtype), Buf(f"{name}{i}")) for i in range(n)]


class Ring:
    def __init__(self, K, name, n, shape, dtype):
        self.items = _ring(K, name, n, shape, dtype)
        self.i = 0

    def next(self):
        it = self.items[self.i]
        self.i = (self.i + 1) % len(self.items)
        return it


def build_program(S, layers, do_mixer=True, do_mlp=True):
    NT = S // TT
    NB = S // 128
    nc = bass.Bass("TRN2", target_bir_lowering=False)
    K = KB(nc)

    def din(name, shape, dt=F32):
        return nc.dram_tensor(name, list(shape), dt, kind="ExternalInput").ap()

    x_in = din("x", [S, D])
    pos_in = din("positions", [1, S], I32)
    mix_g = din("mix_norm_g", [4, D])
    mlp_g = din("mlp_norm_g", [4, D])
    w_up = din("w_mlp_up", [4, D, DFF])
    w_down = din("w_mlp_down", [4, DFF, D])
    e_w_in = din("even_w_in", [2, D, 2112])
    e_qa_g = din("even_q_a_norm_g", [2, 512])
    e_kva_g = din("even_kv_a_norm_g", [2, 512])
    e_w_uq = din("even_w_uq", [2, 512, 1536])
    e_w_ukv = din("even_w_ukv", [2, 512, 2048])
    e_qn_g = din("even_q_norm_g", [2, 192])
    e_kn_g = din("even_k_norm_g", [2, 192])
    e_pool_w = din("even_pool_w", [2, 4, 256, 256])
    e_pool_s = din("even_pool_scale", [2, 1024])
    e_w_out = din("even_w_out", [2, D, D])
    o_w_in = din("odd_w_in", [2, D, 3 * D])
    o_conv_w = din("odd_conv_w", [2, 3, D])
    o_w_out = din("odd_w_out", [2, D, D])
    ident_in = din("c_ident", [128, 128])
    tri_in = din("c_tri", [128, 128])
    rope_in = din("c_rope", [64, 2])
    rc_in = din("c_rc", [128, 4, TT])
    y_out = nc.dram_tensor("y", [S, D], F32, kind="ExternalOutput").ap()

    XT = nc.dram_tensor("XT", [NC_, 128, S], F32).ap()
    xt_bufs = [[Buf(f"xt{c}_{t}") for t in range(NT)] for c in range(NC_)]
    WU = {}
    WD = {}
    wu_bufs = {}
    wd_bufs = {}
    for l in layers:
        WU[l] = nc.dram_tensor(f"WU{l}", [16, 128, 16, 512], BF16).ap()
        WD[l] = nc.dram_tensor(f"WD{l}", [16, 128, 64, 128], BF16).ap()
        wu_bufs[l] = [Buf() for _ in range(16)]
        wd_bufs[l] = [Buf() for _ in range(16)]

    ident = K.sb("ident", [128, 128], F32, glob=True)
    ident_b = Buf("ident")
    K.dma(K.sp, ident[:], ident_in, writes=[ident_b])
    ones_d = K.sb("ones_d", [128, 128], BF16, glob=True)
    ones_d_b = Buf("ones_d")
    K.op(K.dve, lambda h: h.memset(ones_d[:], 1.0 / D), writes=[ones_d_b])
    mlp_gT = K.sb("mlp_gT", [128, 4, NC_], F32, glob=True)
    mlp_gT_b = Buf("mlp_gT")
    mix_gT = K.sb("mix_gT", [128, 4, NC_], F32, glob=True)
    mix_gT_b = Buf("mix_gT")
    def load_transposed(parts, dsts):
        K.begin_phase()
        stg = K.sb("vstage", [128, 128], F32)
        stg_b = Buf()
        K.op(K.dve, lambda h: h.memset(stg[:], 0.0), writes=[stg_b])
        for r0, n, ap in parts:
            K.dma(K.sp, stg[r0:r0 + n, :], ap, writes=[stg_b])
        bank, bb = K.next_bank()
        K.op(K.pe, lambda h: h.transpose(out=bank[:, 0:128], in_=stg[:], identity=ident[:]),
             reads=[stg_b, ident_b], writes=[bb])
        for c0, n, view, vb in dsts:
            K.op(K.dve, lambda h, c0=c0, n=n, view=view: h.tensor_copy(out=view, in_=bank[:, c0:c0 + n]),
                 reads=[bb], writes=[vb])
        K.end_phase()

    load_transposed(
        [(0, 64, mlp_g.rearrange("l (c p) -> (l c) p", p=128)), (64, 64, mix_g.rearrange("l (c p) -> (l c) p", p=128))],
        [(0, 64, mlp_gT[:].rearrange("p l c -> p (l c)"), mlp_gT_b),
         (64, 64, mix_gT[:].rearrange("p l c -> p (l c)"), mix_gT_b)])

    cg_up = {l: CastGroup(K, f"cg_up{l}") for l in layers}
    cg_dn = {l: CastGroup(K, f"cg_dn{l}") for l in layers}
    cg_mx = {l: CastGroup(K, f"cg_mx{l}") for l in layers}

    def cast_mlp_weights(l, gate=None):
        src_u = w_up[l].rearrange("(k p) (mg c) -> mg p k c", p=128, c=512)
        src_d = w_down[l].rearrange("(k p) (m c) -> m p k c", p=128, c=128)
        for mg in range(16):
            K.cast(cg_up[l], WU[l][mg], src_u[mg], gate)
            wu_bufs[l][mg] = cg_up[l].buf
        for m in range(16):
            K.cast(cg_dn[l], WD[l][m], src_d[m], gate)
            wd_bufs[l][m] = cg_dn[l].buf

    def prologue():
        K.begin_phase()
        xr = Ring(K, "xtok", 4, [128, D], F32)
        st = Ring(K, "xstage", 4, [128, NC_, 128], F32)
        for tb in range(NB):
            xt_, xb = xr.next()
            K.dma(K.sp, xt_[:], x_in[tb * 128:(tb + 1) * 128, :], writes=[xb])
            sg, sgb = st.next()
            for q in range(4):
                bank, bb = K.next_bank()

                def fn(h, q=q, bank=bank, xt_=xt_):
                    ins = None
                    for j in range(4):
                        c = q * 4 + j
                        ins = h.transpose(out=bank[:, j * 128:(j + 1) * 128], in_=xt_[:, c * 128:(c + 1) * 128],
                                          identity=ident[:])
                    return ins

                K.op(K.pe, fn, reads=[xb, ident_b], writes=[bb])
                eng = K.act if q % 2 == 0 else K.dve
                if eng is K.act:
                    K.op(eng, lambda h, q=q, bank=bank, sg=sg: h.activation(
                        out=sg[:, q * 4:(q + 1) * 4, :], in_=bank[:].rearrange("p (j t) -> p j t", j=4),
                        func=AF.Copy), reads=[bb], writes=[sgb])
                else:
                    K.op(eng, lambda h, q=q, bank=bank, sg=sg: h.tensor_copy(
                        out=sg[:, q * 4:(q + 1) * 4, :], in_=bank[:].rearrange("p (j t) -> p j t", j=4)),
                        reads=[bb], writes=[sgb])
            t = tb // 4
            K.dma(K.sp, XT[:, :, tb * 128:(tb + 1) * 128].rearrange("c p t -> p c t"), sg[:],
                  reads=[sgb], writes=[xt_bufs[c][t] for c in range(NC_)])
        K.end_phase()

    def epilogue():
        K.begin_phase()
        ld = Ring(K, "eld", 4, [128, NC_, 128], F32)
        ot = Ring(K, "eout", 4, [128, D], F32)
        for tb in range(NB):
            t = tb // 4
            lt, lb = ld.next()
            K.dma(K.sp, lt[:], XT[:, :, tb * 128:(tb + 1) * 128].rearrange("c p t -> p c t"),
                  reads=[xt_bufs[c][t] for c in range(NC_)], writes=[lb])
            og, ogb = ot.next()
            for q in range(4):
                bank, bb = K.next_bank()

                def fn(h, q=q, bank=bank, lt=lt):
                    ins = None
                    for j in range(4):
                        c = q * 4 + j
                        ins = h.transpose(out=bank[:, j * 128:(j + 1) * 128], in_=lt[:, c, :], identity=ident[:])
                    return ins

                K.op(K.pe, fn, reads=[lb, ident_b], writes=[bb])
                if q % 2 == 0:
                    K.op(K.act, lambda h, q=q, bank=bank, og=og: h.activation(
                        out=og[:, q * 512:(q + 1) * 512], in_=bank[:], func=AF.Copy), reads=[bb], writes=[ogb])
                else:
                    K.op(K.dve, lambda h, q=q, bank=bank, og=og: h.tensor_copy(
                        out=og[:, q * 512:(q + 1) * 512], in_=bank[:]), reads=[bb], writes=[ogb])
            K.dma(K.sp, y_out[tb * 128:(tb + 1) * 128, :], og[:], reads=[ogb], writes=[Buf()])
        K.end_phase()

    class Norm:
        def __init__(self, gT, gT_b, HT, ht_bufs):
            self.xr = Ring(K, "nx", 4, [128, TT], F32)
            self.sq = Ring(K, "nsq", 2, [128, TT], BF16)
            self.rs = K.sb("nrs", [128, TT], F32)
            self.rs_b = Buf("nrs")
            self.rstd = K.sb("nrstd", [128, TT], F32)
            self.rstd_b = Buf("nrstd")
            self.gT, self.gT_b, self.HT, self.ht_bufs = gT, gT_b, HT, ht_bufs
            self.stat_bank = K.banks[7]
            self.stat_bb = K.bank_bufs[7]

        def stage_a(self, l, t):
            pend = []
            for c in range(NC_):
                xt_, xb = self.xr.next()
                K.dma(K.sp, xt_[:], XT[c][:, t * TT:(t + 1) * TT], reads=[xt_bufs[c][t]], writes=[xb])
                sq, sqb = self.sq.next()
                K.op(K.act, lambda h, sq=sq, xt_=xt_: h.activation(out=sq[:], in_=xt_[:], func=AF.Square),
                     reads=[xb], writes=[sqb])
                K.op(K.pe, lambda h, sq=sq, c=c: h.matmul(self.stat_bank[:], lhsT=ones_d[:], rhs=sq[:],
                                                         start=(c == 0), stop=(c == NC_ - 1)),
                     reads=[sqb, ones_d_b], writes=[self.stat_bb])
            K.op(K.act, lambda h: h.activation(out=self.rs[:], in_=self.stat_bank[:], func=AF.Sqrt, bias=EPS,
                                               scale=1.0), reads=[self.stat_bb], writes=[self.rs_b])
            K.op(K.dve, lambda h: h.reciprocal(out=self.rstd[:], in_=self.rs[:]), reads=[self.rs_b],
                 writes=[self.rstd_b])

        def stage_b(self, l, t):
            for c in range(NC_):
                xt_, xb = self.xr.next()
                K.dma(K.sp, xt_[:], XT[c][:, t * TT:(t + 1) * TT], reads=[xt_bufs[c][t]], writes=[xb])
                K.op(K.dve, lambda h, xt_=xt_, c=c: h.scalar_tensor_tensor(
                    out=self.HT[:, c, :], in0=xt_[:], scalar=self.gT[:, l, c:c + 1], in1=self.rstd[:],
                    op0=ALU.mult, op1=ALU.mult), reads=[xb, self.gT_b, self.rstd_b], writes=[self.ht_bufs[c]])

    def mlp_phase(l):
        K.begin_phase()
        HT = K.sb("HT", [128, NC_, TT], BF16)
        ht_bufs = [Buf(f"ht{c}") for c in range(NC_)]
        AT = K.sb("AT", [128, 64, TT], BF16)
        at_bufs = [Buf(f"at{c}") for c in range(64)]
        wus = Ring(K, "wus", 2, [128, 16, 512], BF16)
        wds = Ring(K, "wds", 2, [128, 64, 128], BF16)
        xres = Ring(K, "xres", 2, [128, TT], F32)
        ores = Ring(K, "ores", 3, [128, TT], F32)
        relu_r = Ring(K, "relu_r", 3, [128, TT], F32)
        norm = Norm(mlp_gT, mlp_gT_b, HT, ht_bufs)

        loads = []
        for t in range(NT):
            for mg in range(16):
                loads.append(("u", t, mg))
            for m in range(16):
                loads.append(("d", t, m))
        slot_of = {}

        def issue_load(i):
            if i >= len(loads):
                return
            kind, t, j = loads[i]
            if kind == "u":
                s, sb_ = wus.next()
                K.dma(K.sp, s[:], WU[l][j], reads=[wu_bufs[l][j]], writes=[sb_])
            else:
                s, sb_ = wds.next()
                K.dma(K.sp, s[:], WD[l][j], reads=[wd_bufs[l][j]], writes=[sb_])
            slot_of[i] = (s, sb_)

        norm.stage_a(l, 0)
        norm.stage_b(l, 0)
        issue_load(0)
        issue_load(1)
        li = 0
        for t in range(NT):
            for mg in range(16):
                s, sb_ = slot_of.pop(li)
                for j in range(4):
                    bank, bb = K.next_bank()
                    pairs = [(s[:, k, j * 128:(j + 1) * 128], HT[:, k, :]) for k in range(NC_)]
                    K.mm_group(bank[:], bb, pairs, reads=[sb_] + ht_bufs)
                    ci = mg * 4 + j
                    r_, rb = relu_r.next()
                    K.op(K.act, lambda h, bank=bank, r_=r_: h.activation(out=r_[:], in_=bank[:], func=AF.Relu),
                         reads=[bb], writes=[rb])
                    K.op(K.dve, lambda h, r_=r_, ci=ci: h.tensor_tensor(
                        out=AT[:, ci, :], in0=r_[:], in1=r_[:], op=ALU.mult),
                        reads=[rb], writes=[at_bufs[ci]])
                li += 1
                issue_load(li + 1)
            for m in range(16):
                s, sb_ = slot_of.pop(li)
                bank, bb = K.next_bank()
                pairs = [(s[:, k, :], AT[:, k, :]) for k in range(64)]
                K.mm_group(bank[:], bb, pairs, reads=[sb_] + at_bufs)
                xr_, xrb = xres.next()
                K.dma(K.sp, xr_[:], XT[m][:, t * TT:(t + 1) * TT], reads=[xt_bufs[m][t]], writes=[xrb])
                o_, ob = ores.next()
                K.op(K.dve, lambda h, bank=bank, xr_=xr_, o_=o_: h.tensor_tensor(
                    out=o_[:], in0=bank[:], in1=xr_[:], op=ALU.add), reads=[bb, xrb], writes=[ob])
                K.dma(K.sp, XT[m][:, t * TT:(t + 1) * TT], o_[:], reads=[ob], writes=[xt_bufs[m][t]])
                li += 1
                issue_load(li + 1)
                if t + 1 < NT:
                    if m == 3:
                        norm.stage_a(l, t + 1)
                    if m == 9:
                        norm.stage_b(l, t + 1)
        K.end_phase()


    def small(name, shape, dt=F32):
        return K.sb(name, shape, dt, glob=True), Buf(name)

    ones_512, ones_512_b = small("ones_512", [128, 128], BF16)
    ones_1, ones_1_b = small("ones_1", [128, 128], BF16)
    K.op(K.dve, lambda h: h.memset(ones_512[:], 1.0 / 512), writes=[ones_512_b])
    K.op(K.dve, lambda h: h.memset(ones_1[:], 1.0), writes=[ones_1_b])
    halfpi, halfpi_b = small("halfpi", [128, 1], F32)
    K.op(K.dve, lambda h: h.memset(halfpi[:], float(np.pi / 2)), writes=[halfpi_b])
    epsc, epsc_b = small("epsc", [128, 1], F32)
    K.op(K.dve, lambda h: h.memset(epsc[:], EPS), writes=[epsc_b])
    ones_f, ones_f_b = small("ones_f", [1, 128], F32)
    K.op(K.dve, lambda h: h.memset(ones_f[:], 1.0), writes=[ones_f_b])
    tri_f, tri_f_b = small("tri_f", [128, 128], F32)
    K.dma(K.sp, tri_f[:], tri_in, writes=[tri_f_b])
    tri, tri_b = small("tri", [128, 128], BF16)
    K.op(K.dve, lambda h: h.tensor_copy(out=tri[:], in_=tri_f[:]), reads=[tri_f_b], writes=[tri_b])
    ropec, ropec_b = small("ropec", [64, 2], F32)
    K.dma(K.sp, ropec[:], rope_in, writes=[ropec_b])
    cwT, cwT_b = small("cwT", [128, 2, 3, NC_], F32)
    psT, psT_b = small("psT", [128, 2, 8], F32)
    qagT, qagT_b = small("qagT", [128, 2, 4], F32)
    kvagT, kvagT_b = small("kvagT", [128, 2, 4], F32)
    GQK, GQK_b = small("GQK", [128, 2, 2, 3], F32)
    load_transposed(
        [(0, 96, o_conv_w.rearrange("o j (c p) -> (o j c) p", p=128)),
         (96, 16, e_pool_s.rearrange("e (c p) -> (e c) p", p=128)),
         (112, 8, e_qa_g.rearrange("e (c p) -> (e c) p", p=128)),
         (120, 8, e_kva_g.rearrange("e (c p) -> (e c) p", p=128))],
        [(0, 96, cwT[:].rearrange("p o j c -> p (o j c)"), cwT_b),
         (96, 16, psT[:].rearrange("p e c -> p (e c)"), psT_b),
         (112, 8, qagT[:].rearrange("p e c -> p (e c)"), qagT_b),
         (120, 8, kvagT[:].rearrange("p e c -> p (e c)"), kvagT_b)])
    gparts = []
    for e in range(2):
        for qk, g in enumerate((e_qn_g, e_kn_g)):
            r = (e * 2 + qk) * 3
            gparts.append((r, 1, g[e:e + 1, 0:128]))
            gparts.append((r + 1, 1, g[e:e + 1, 128:192]))
            gparts.append((r + 2, 1, g[e:e + 1, 160:192]))
            gparts.append((r + 2, 1, g[e:e + 1, 128:160]))
    K.begin_phase()
    stg = K.sb("gstage", [128, 128], F32)
    stg_b = Buf()
    K.op(K.dve, lambda h: h.memset(stg[:], 0.0), writes=[stg_b])
    for e in range(2):
        for qk, g in enumerate((e_qn_g, e_kn_g)):
            r = (e * 2 + qk) * 3
            K.dma(K.sp, stg[r:r + 1, 0:128], g[e:e + 1, 0:128], writes=[stg_b])
            K.dma(K.sp, stg[r + 1:r + 2, 0:64], g[e:e + 1, 128:192], writes=[stg_b])
            K.dma(K.sp, stg[r + 2:r + 3, 0:32], g[e:e + 1, 160:192], writes=[stg_b])
            K.dma(K.sp, stg[r + 2:r + 3, 32:64], g[e:e + 1, 128:160], writes=[stg_b])
    bank, bb = K.next_bank()
    K.op(K.pe, lambda h: h.transpose(out=bank[:, 0:128], in_=stg[:], identity=ident[:]),
         reads=[stg_b, ident_b], writes=[bb])
    K.op(K.dve, lambda h: h.tensor_copy(out=GQK[:].rearrange("p e q k -> p (e q k)"), in_=bank[:, 0:12]),
         reads=[bb], writes=[GQK_b])
    K.end_phase()

    def dscr(name, shape, dt):
        return nc.dram_tensor(name, list(shape), dt).ap()

    C2s = dscr("C2s", [64, S], F32)
    S2s = dscr("S2s", [64, S], F32)
    cs_bufs = [Buf() for _ in range(NT)]
    CQN = dscr("CQN", [4, 128, S], BF16)
    CKVN = dscr("CKVN", [4, 128, S], BF16)
    cqn_bufs = [Buf() for _ in range(NT)]
    ckvn_bufs = [Buf() for _ in range(NT)]
    KRBs = dscr("KRBs", [64, S], F32)
    KRSSQ = dscr("KRSSQ", [128, S], F32)
    krb_bufs = [Buf() for _ in range(NT)]
    krs_bufs = [Buf() for _ in range(NT)]
    MIXT = dscr("MIXT", [NC_, 128, S], BF16)
    mixt_bufs = [[Buf() for _ in range(NT)] for _ in range(NC_)]
    EW = {}
    OW = {}
    for l in layers:
        if l % 2 == 0:
            e = l // 2
            EW[e] = dict(
                WEI=dscr(f"WEI{e}", [4, 128, 16, 512], BF16), WEIR=dscr(f"WEIR{e}", [128, 16, 128], BF16),
                WUQ=dscr(f"WUQ{e}", [128, 4, 1536], BF16), WUQS=dscr(f"WUQS{e}", [128, 4, 8, 64], BF16),
                WUKV=dscr(f"WUKV{e}", [128, 4, 2048], BF16), WEO=dscr(f"WEO{e}", [128, 16, 2048], BF16),
                PW=dscr(f"PW{e}", [128, 4, 2, 256], BF16), b=Buf())
        else:
            o = l // 2
            OW[o] = dict(WCI=dscr(f"WCI{o}", [16, 128, 3, 16, 128], BF16), WCO=dscr(f"WCO{o}", [128, 16, 2048], BF16),
                         b=Buf())

    def cast_mixer_weights(l, gate=None):
        if l % 2 == 0:
            e = l // 2
            W = EW[e]
            W["b"] = cg_mx[l].buf
            cols = [(0, 512), (512, 1024), (1088, 1600), (1600, 2112)]
            for g, (a, b_) in enumerate(cols):
                K.cast(cg_mx[l], W["WEI"][g], e_w_in[e][:, a:b_].rearrange("(k p) c -> p k c", p=128), gate)
            for (da, db, sa, sb_) in ((0, 64, 1024, 1088), (64, 96, 1056, 1088), (96, 128, 1024, 1056)):
                K.cast(cg_mx[l], W["WEIR"][:, :, da:db], e_w_in[e][:, sa:sb_].rearrange("(k p) c -> p k c", p=128), gate)
            K.cast(cg_mx[l], W["WUQ"], e_w_uq[e].rearrange("(k p) n -> p k n", p=128), gate)
            for k in range(4):
                v = e_w_uq[e][k * 128:(k + 1) * 128, :].rearrange("p (h d) -> p h d", d=192)
                K.cast(cg_mx[l], W["WUQS"][:, k, :, 0:32], v[:, :, 160:192], gate)
                K.cast(cg_mx[l], W["WUQS"][:, k, :, 32:64], v[:, :, 128:160], gate)
            K.cast(cg_mx[l], W["WUKV"], e_w_ukv[e].rearrange("(k p) n -> p k n", p=128), gate)
            for kq in range(4):
                K.cast(cg_mx[l], W["WEO"][:, kq * 4:(kq + 1) * 4, :],
                      e_w_out[e][kq * 512:(kq + 1) * 512, :].rearrange("(k p) n -> p k n", p=128), gate)
            for g in range(4):
                K.cast(cg_mx[l], W["PW"][:, g], e_pool_w[e][g].rearrange("(c p) d -> p c d", p=128), gate)
        else:
            o = l // 2
            W = OW[o]
            W["b"] = cg_mx[l].buf
            for c in range(NC_):
                for j in range(3):
                    K.cast(cg_mx[l], W["WCI"][c][:, j],
                          o_w_in[o][:, j * D + c * 128: j * D + (c + 1) * 128].rearrange("(k p) q -> p k q", p=128), gate)
            for kq in range(4):
                K.cast(cg_mx[l], W["WCO"][:, kq * 4:(kq + 1) * 4, :],
                      o_w_out[o][kq * 512:(kq + 1) * 512, :].rearrange("(k p) n -> p k n", p=128), gate)

    def load_resident(dst, src, src_b, dst_b, nsplit, axis_len):
        step = axis_len // nsplit
        for i in range(nsplit):
            K.dma(K.sp, dst[:, i * step:(i + 1) * step], src[:, i * step:(i + 1) * step], reads=[src_b], writes=[dst_b])

    def outproj_tile(W, W_b, MT, mt_reads, t, xres, ores, hook=None):
        for m in range(NC_):
            bank, bb = K.next_bank()
            pairs = [(W[:, k, m * 128:(m + 1) * 128], MT[:, k, :]) for k in range(NC_)]
            K.mm_group(bank[:], bb, pairs, reads=[W_b] + mt_reads)
            xr_, xrb = xres.next()
            K.dma(K.sp, xr_[:], XT[m][:, t * TT:(t + 1) * TT], reads=[xt_bufs[m][t]], writes=[xrb])
            o_, ob = ores.next()
            K.op(K.dve, lambda h, bank=bank, xr_=xr_, o_=o_: h.tensor_tensor(
                out=o_[:], in0=bank[:], in1=xr_[:], op=ALU.add), reads=[bb, xrb], writes=[ob])
            K.dma(K.sp, XT[m][:, t * TT:(t + 1) * TT], o_[:], reads=[ob], writes=[xt_bufs[m][t]])
            if hook is not None:
                hook(m)

    def odd_mixer(l):
        o = l // 2
        W = OW[o]
        K.begin_phase()
        HT = K.sb("HT", [128, NC_, TT], BF16)
        ht_bufs = [Buf() for _ in range(NC_)]
        ZT = K.sb("ZT", [128, NC_, TT], BF16)
        zt_bufs = [Buf() for _ in range(NC_)]
        WCO = K.sb("WCO", [128, NC_, D], BF16)
        WCO_b = Buf()
        load_resident(WCO, W["WCO"], W["b"], WCO_b, 4, NC_)
        slots = Ring(K, "wci", 2, [128, 3, NC_, 128], BF16)
        VH = K.sb("VH", [128, NC_, 2], F32)
        vh_bufs = [Buf() for _ in range(NC_)]
        K.op(K.dve, lambda h: h.memset(VH[:], 0.0), writes=vh_bufs)
        vr = Ring(K, "vr", 2, [128, TT + 2], F32)
        gcr = Ring(K, "gcr", 2, [128, TT], F32)
        tmpa = Ring(K, "tmpa", 2, [128, TT], F32)
        tmpb = Ring(K, "tmpb", 2, [128, TT], F32)
        xres = Ring(K, "xres", 2, [128, TT], F32)
        ores = Ring(K, "ores", 3, [128, TT], F32)
        norm = Norm(mix_gT, mix_gT_b, HT, ht_bufs)
        order = [(t, c) for t in range(NT) for c in range(NC_)]
        slot_of = {}

        def issue(i):
            if i >= len(order):
                return
            t, c = order[i]
            s_, sb_ = slots.next()
            K.dma(K.sp, s_[:], W["WCI"][c], reads=[W["b"]], writes=[sb_])
            slot_of[i] = (s_, sb_)

        norm.stage_a(l, 0)
        norm.stage_b(l, 0)
        issue(0)
        issue(1)
        li = 0
        for t in range(NT):
            for c in range(NC_):
                s_, sb_ = slot_of.pop(li)
                bks = []
                for j in range(3):
                    bank, bb = K.next_bank()
                    pairs = [(s_[:, j, k, :], HT[:, k, :]) for k in range(NC_)]
                    K.mm_group(bank[:], bb, pairs, reads=[sb_] + ht_bufs)
                    bks.append((bank, bb))
                li += 1
                issue(li + 1)
                (bB, bBb), (bC, bCb), (bU, bUb) = bks
                gc_, gcb = gcr.next()
                K.op(K.act, lambda h, gc_=gc_, bC=bC: h.activation(out=gc_[:], in_=bC[:], func=AF.Copy),
                     reads=[bCb], writes=[gcb])
                v_, vb = vr.next()
                K.op(K.dve, lambda h, v_=v_, c=c: h.tensor_copy(out=v_[:, 0:2], in_=VH[:, c, :]),
                     reads=[vh_bufs[c]], writes=[vb])
                K.op(K.dve, lambda h, v_=v_, gc_=gc_, bU=bU: h.tensor_tensor(
                    out=v_[:, 2:TT + 2], in0=bU[:], in1=gc_[:], op=ALU.mult), reads=[bUb, gcb, vb], writes=[vb])
                K.op(K.dve, lambda h, v_=v_, c=c: h.tensor_copy(out=VH[:, c, :], in_=v_[:, TT:TT + 2]),
                     reads=[vb], writes=[vh_bufs[c]])
                ta, tab = tmpa.next()
                tb_, tbb = tmpb.next()
                K.op(K.dve, lambda h, ta=ta, v_=v_, c=c: h.tensor_scalar(
                    out=ta[:], in0=v_[:, 0:TT], scalar1=cwT[:, o, 0, c:c + 1], scalar2=None, op0=ALU.mult),
                    reads=[vb, cwT_b], writes=[tab])
                K.op(K.dve, lambda h, ta=ta, tb_=tb_, v_=v_, c=c: h.scalar_tensor_tensor(
                    out=tb_[:], in0=v_[:, 1:TT + 1], scalar=cwT[:, o, 1, c:c + 1], in1=ta[:], op0=ALU.mult,
                    op1=ALU.add), reads=[vb, tab, cwT_b], writes=[tbb])
                K.op(K.dve, lambda h, ta=ta, tb_=tb_, v_=v_, c=c: h.scalar_tensor_tensor(
                    out=ta[:], in0=v_[:, 2:TT + 2], scalar=cwT[:, o, 2, c:c + 1], in1=tb_[:], op0=ALU.mult,
                    op1=ALU.add), reads=[vb, tbb, cwT_b], writes=[tab])
                K.op(K.dve, lambda h, ta=ta, bB=bB, c=c: h.tensor_tensor(
                    out=ZT[:, c, :], in0=bB[:], in1=ta[:], op=ALU.mult), reads=[bBb, tab], writes=[zt_bufs[c]])

            def hook(m, t=t):
                if t + 1 < NT:
                    if m == 3:
                        norm.stage_a(l, t + 1)
                    if m == 9:
                        norm.stage_b(l, t + 1)

            outproj_tile(WCO, WCO_b, ZT, zt_bufs, t, xres, ores, hook)
        K.end_phase()

    def rope_tables():
        K.begin_phase()
        posi = K.sb("posi", [64, S], I32)
        posi_b = Buf()
        K.dma(K.sp, posi[:], pos_in.broadcast_to([64, S]),
              writes=[posi_b])
        ang = K.sb("ang", [64, S], F32)
        ang_b = Buf()
        K.op(K.dve, lambda h: h.tensor_copy(out=ang[:], in_=posi[:]), reads=[posi_b], writes=[ang_b])
        K.op(K.dve, lambda h: h.tensor_scalar(out=ang[:], in0=ang[:], scalar1=ropec[:, 0:1], scalar2=None,
                                              op0=ALU.mult), reads=[ang_b, ropec_b], writes=[ang_b])
        mr = Ring(K, "mr", 2, [64, TT], F32)
        ki = Ring(K, "ki", 2, [64, TT], I32)
        kf = Ring(K, "kf", 2, [64, TT], F32)
        fl = Ring(K, "fl", 2, [64, TT], F32)
        orr = Ring(K, "orr", 4, [64, TT], F32)
        PI = float(np.pi)
        C1 = 6.28125
        C2 = float(2.0 * np.pi - 6.28125)
        for t in range(NT):
            sl = slice(t * TT, (t + 1) * TT)
            m_, mb = mr.next()
            k_i, kib = ki.next()
            k_f, kfb = kf.next()
            f_, fb = fl.next()
            K.op(K.dve, lambda h: h.tensor_scalar(out=m_[:], in0=ang[:, sl], scalar1=float(1.0 / (2 * np.pi)),
                                                  scalar2=None, op0=ALU.mult), reads=[ang_b], writes=[mb])
            K.op(K.dve, lambda h: h.tensor_copy(out=k_i[:], in_=m_[:]), reads=[mb], writes=[kib])
            K.op(K.dve, lambda h: h.tensor_copy(out=k_f[:], in_=k_i[:]), reads=[kib], writes=[kfb])
            K.op(K.dve, lambda h: h.scalar_tensor_tensor(out=m_[:], in0=k_f[:], scalar=-C1, in1=ang[:, sl],
                                                         op0=ALU.mult, op1=ALU.add), reads=[kfb, ang_b, mb],
                 writes=[mb])
            K.op(K.dve, lambda h: h.scalar_tensor_tensor(out=m_[:], in0=k_f[:], scalar=-C2, in1=m_[:],
                                                         op0=ALU.mult, op1=ALU.add), reads=[kfb, mb], writes=[mb])
            K.op(K.dve, lambda h: h.tensor_single_scalar(out=f_[:], in_=m_[:], scalar=PI, op=ALU.is_gt),
                 reads=[mb], writes=[fb])
            K.op(K.dve, lambda h: h.scalar_tensor_tensor(out=m_[:], in0=f_[:], scalar=-2 * PI, in1=m_[:],
                                                         op0=ALU.mult, op1=ALU.add), reads=[fb, mb], writes=[mb])
            K.op(K.dve, lambda h: h.tensor_single_scalar(out=f_[:], in_=m_[:], scalar=-PI, op=ALU.is_lt),
                 reads=[mb, fb], writes=[fb])
            K.op(K.dve, lambda h: h.scalar_tensor_tensor(out=m_[:], in0=f_[:], scalar=2 * PI, in1=m_[:],
                                                         op0=ALU.mult, op1=ALU.add), reads=[fb, mb], writes=[mb])
            K.op(K.dve, lambda h: h.tensor_scalar(out=m_[:], in0=m_[:], scalar1=PI, scalar2=-PI, op0=ALU.min,
                                                  op1=ALU.max), reads=[mb], writes=[mb])
            o_, ob = orr.next()
            K.op(K.act, lambda h: h.activation(out=o_[:], in_=m_[:], func=AF.Sin), reads=[mb], writes=[ob])
            K.op(K.dve, lambda h: h.tensor_scalar(out=o_[:], in0=o_[:], scalar1=ropec[:, 1:2], scalar2=None,
                                                  op0=ALU.mult), reads=[ob, ropec_b], writes=[ob])
            K.dma(K.sp, S2s[:, sl], o_[:], reads=[ob], writes=[cs_bufs[t]])
            K.op(K.dve, lambda h: h.scalar_tensor_tensor(out=f_[:], in0=m_[:], scalar=-1.0, in1=m_[:],
                                                         op0=ALU.mult, op1=ALU.max), reads=[mb, fb], writes=[fb])
            o2, o2b = orr.next()
            K.op(K.act, lambda h: h.activation(out=o2[:], in_=f_[:], func=AF.Sin, bias=halfpi[0:64, :], scale=-1.0),
                 reads=[fb, halfpi_b], writes=[o2b])
            K.dma(K.sp, C2s[:, sl], o2[:], reads=[o2b], writes=[cs_bufs[t]])
        K.end_phase()


    negc = K.sb("negc", [128, 2], F32, glob=True)
    negc_b = Buf()

    def even_setup(e):
        K.begin_phase()
        row = K.sb("grow", [1, 2, 192], F32)
        row_b = Buf()
        K.dma(K.sp, row[:, 0, :], e_qn_g[e:e + 1, :], writes=[row_b])
        K.dma(K.sp, row[:, 1, :], e_kn_g[e:e + 1, :], writes=[row_b])
        K.op(K.dve, lambda h: h.scalar_tensor_tensor(out=row[:], in0=row[:], scalar=-1.0, in1=row[:],
                                                     op0=ALU.mult, op1=ALU.max), reads=[row_b], writes=[row_b])
        mx = K.sb("gmx", [1, 2], F32)
        mx_b = Buf()
        K.op(K.dve, lambda h: h.tensor_reduce(out=mx[:], in_=row[:], axis=mybir.AxisListType.X, op=ALU.max),
             reads=[row_b], writes=[mx_b])
        pr = K.sb("gpr", [1, 1], F32)
        pr_b = Buf()
        K.op(K.dve, lambda h: h.tensor_tensor(out=pr[:], in0=mx[:, 0:1], in1=mx[:, 1:2], op=ALU.mult),
             reads=[mx_b], writes=[pr_b])
        K.op(K.dve, lambda h: h.tensor_scalar(out=pr[:], in0=pr[:], scalar1=-float(np.sqrt(192.0)), scalar2=None,
                                              op0=ALU.mult), reads=[pr_b], writes=[pr_b])
        bank, bb = K.next_bank()
        K.op(K.pe, lambda h: h.matmul(bank[:, 0:1], lhsT=ones_f[0:1, :], rhs=pr[0:1, 0:1], start=True, stop=True),
             reads=[pr_b, ones_f_b], writes=[bb])
        K.op(K.act, lambda h: h.activation(out=negc[:, e:e + 1], in_=bank[:, 0:1], func=AF.Copy),
             reads=[bb], writes=[negc_b])
        K.end_phase()

    def even_e1(l):
        e = l // 2
        W = EW[e]
        K.begin_phase()
        HT = K.sb("HT", [128, NC_, TT], BF16)
        ht_bufs = [Buf() for _ in range(NC_)]
        norm = Norm(mix_gT, mix_gT_b, HT, ht_bufs)
        WI = K.sb("WI", [128, 4, NC_, 512], BF16)
        WI_b = Buf()
        for g in range(4):
            K.dma(K.sp, WI[:, g], W["WEI"][g], reads=[W["b"]], writes=[WI_b])
        WIR = K.sb("WIR", [128, NC_, 128], BF16)
        WIR_b = Buf()
        K.dma(K.sp, WIR[:], W["WEIR"], reads=[W["b"]], writes=[WIR_b])
        PWs = K.sb("PWs", [128, 4, 2, 256], BF16)
        PWs_b = Buf()
        K.dma(K.sp, PWs[:], W["PW"], reads=[W["b"]], writes=[PWs_b])
        RC = K.sb("RC", [128, 4, TT], F32)
        RC_b = Buf()
        K.dma(K.sp, RC[:], rc_in, writes=[RC_b])
        raw = K.sb("raw", [128, 4, TT], F32)
        raw_bufs = [Buf() for _ in range(4)]
        sqr = Ring(K, "sqr", 5, [128, TT], BF16)
        rs2 = K.sb("rs2", [128, TT], F32)
        rs2_b = Buf()
        rstd2 = K.sb("rstd2", [128, TT], F32)
        rstd2_b = Buf()
        latr = Ring(K, "latr", 2, [128, 4, TT], BF16)
        ctr = Ring(K, "ctr", 2, [64, 2, TT], F32)
        t12 = Ring(K, "t12", 2, [64, 2, TT], F32)
        krr = Ring(K, "krr", 2, [64, TT], F32)
        kssr = Ring(K, "kssr", 2, [128, TT], F32)
        UH = K.sb("UH", [128, 8, 16], F32)
        uh_bufs = [Buf() for _ in range(8)]
        K.op(K.dve, lambda h: h.memset(UH[:], 0.0), writes=uh_bufs)
        ur = Ring(K, "ur", 2, [128, TT + 16], F32)
        ar = Ring(K, "ar", 2, [128, TT + 16], F32)
        br_ = Ring(K, "br", 2, [128, TT + 16], F32)
        plr = Ring(K, "plr", 2, [128, 2, TT], BF16)
        btr = Ring(K, "btr", 2, [128, TT], BF16)
        tmp0 = Ring(K, "tmp0", 2, [128, TT], F32)

        def latent(t, g, gT, gT_b, DST, dst_bufs):
            bks = []
            for j in range(4):
                bank, bb = K.next_bank()
                pairs = [(WI[:, g, k, j * 128:(j + 1) * 128], HT[:, k, :]) for k in range(NC_)]
                K.mm_group(bank[:], bb, pairs, reads=[WI_b] + ht_bufs)
                K.op(K.act, lambda h, bank=bank, j=j: h.activation(out=raw[:, j, :], in_=bank[:], func=AF.Copy),
                     reads=[bb], writes=[raw_bufs[j]])
                sq, sqb = sqr.next()
                K.op(K.act, lambda h, bank=bank, sq=sq: h.activation(out=sq[:], in_=bank[:], func=AF.Square),
                     reads=[bb], writes=[sqb])
                bks.append((sq, sqb))
            for j, (sq, sqb) in enumerate(bks):
                K.op(K.pe, lambda h, sq=sq, j=j: h.matmul(norm.stat_bank[:], lhsT=ones_512[:], rhs=sq[:],
                                                         start=(j == 0), stop=(j == 3)),
                     reads=[sqb, ones_512_b], writes=[norm.stat_bb])
            K.op(K.act, lambda h: h.activation(out=rs2[:], in_=norm.stat_bank[:], func=AF.Sqrt, bias=EPS, scale=1.0),
                 reads=[norm.stat_bb], writes=[rs2_b])
            K.op(K.dve, lambda h: h.reciprocal(out=rstd2[:], in_=rs2[:]), reads=[rs2_b], writes=[rstd2_b])
            lt, ltb = latr.next()
            for j in range(4):
                K.op(K.dve, lambda h, j=j, lt=lt: h.scalar_tensor_tensor(
                    out=lt[:, j, :], in0=raw[:, j, :], scalar=gT[:, e, j:j + 1], in1=rstd2[:], op0=ALU.mult,
                    op1=ALU.mult), reads=[raw_bufs[j], rstd2_b, gT_b], writes=[ltb])
            K.dma(K.sp, DST[:, :, t * TT:(t + 1) * TT].rearrange("c p t -> p c t"), lt[:], reads=[ltb],
                  writes=[dst_bufs[t]])

        def krope(t):
            sl = slice(t * TT, (t + 1) * TT)
            bR, bRb = K.next_bank()
            K.mm_group(bR[0:64, :], bRb, [(WIR[:, k, 0:64], HT[:, k, :]) for k in range(NC_)], reads=[WIR_b] + ht_bufs)
            bS, bSb = K.next_bank()
            K.mm_group(bS[0:64, :], bSb, [(WIR[:, k, 64:128], HT[:, k, :]) for k in range(NC_)],
                       reads=[WIR_b] + ht_bufs)
            sq, sqb = sqr.next()
            K.op(K.act, lambda h: h.activation(out=sq[0:64, :], in_=bR[0:64, :], func=AF.Square), reads=[bRb],
                 writes=[sqb])
            bT, bTb = K.next_bank()
            K.op(K.pe, lambda h: h.matmul(bT[:], lhsT=ones_1[0:64, :], rhs=sq[0:64, :], start=True, stop=True),
                 reads=[sqb, ones_1_b], writes=[bTb])
            ks, ksb = kssr.next()
            K.op(K.act, lambda h: h.activation(out=ks[:], in_=bT[:], func=AF.Copy), reads=[bTb], writes=[ksb])
            K.dma(K.sp, KRSSQ[:, sl], ks[:], reads=[ksb], writes=[krs_bufs[t]])
            ct, ctb = ctr.next()
            K.dma(K.sp, ct[:, 0, :], C2s[:, sl], reads=[cs_bufs[t]], writes=[ctb])
            K.dma(K.sp, ct[:, 1, :], S2s[:, sl], reads=[cs_bufs[t]], writes=[ctb])
            tt_, ttb = t12.next()
            K.op(K.dve, lambda h: h.scalar_tensor_tensor(out=tt_[:, 0, :], in0=bR[0:64, :], scalar=GQK[0:64, e, 1, 1:2],
                                                         in1=ct[:, 0, :], op0=ALU.mult, op1=ALU.mult),
                 reads=[bRb, ctb, GQK_b], writes=[ttb])
            K.op(K.dve, lambda h: h.scalar_tensor_tensor(out=tt_[:, 1, :], in0=bS[0:64, :], scalar=GQK[0:64, e, 1, 2:3],
                                                         in1=ct[:, 1, :], op0=ALU.mult, op1=ALU.mult),
                 reads=[bSb, ctb, GQK_b], writes=[ttb])
            kr, krb = krr.next()
            K.op(K.dve, lambda h: h.tensor_tensor(out=kr[:], in0=tt_[:, 0, :], in1=tt_[:, 1, :], op=ALU.add),
                 reads=[ttb], writes=[krb])
            K.dma(K.sp, KRBs[:, sl], kr[:], reads=[krb], writes=[krb_bufs[t]])

        def pool(t):
            for g in range(4):
                w = 2 << g
                pl, plb = plr.next()
                for ci in range(2):
                    i = g * 2 + ci
                    bank, bb = K.next_bank()
                    pairs = [(WI[:, 2 + i // 4, k, (i % 4) * 128:(i % 4 + 1) * 128], HT[:, k, :]) for k in range(NC_)]
                    K.mm_group(bank[:], bb, pairs, reads=[WI_b] + ht_bufs)
                    u_, ub = ur.next()
                    K.op(K.dve, lambda h, u_=u_, i=i: h.tensor_copy(out=u_[:, 0:16], in_=UH[:, i, :]),
                         reads=[uh_bufs[i]], writes=[ub])
                    K.op(K.act, lambda h, u_=u_, bank=bank: h.activation(out=u_[:, 16:TT + 16], in_=bank[:],
                                                                         func=AF.Copy), reads=[bb, ub], writes=[ub])
                    K.op(K.dve, lambda h, u_=u_, i=i: h.tensor_copy(out=UH[:, i, :], in_=u_[:, TT:TT + 16]),
                         reads=[ub], writes=[uh_bufs[i]])
                    a_, ab = ar.next()
                    b_, bbf = br_.next()
                    K.op(K.dve, lambda h, a_=a_, u_=u_: h.tensor_tensor(
                        out=a_[:, 1:TT + 16], in0=u_[:, 1:TT + 16], in1=u_[:, 0:TT + 15], op=ALU.add),
                        reads=[ub], writes=[ab])
                    cur, curb, oth, othb = a_, ab, b_, bbf
                    sh = 1
                    lo = 1
                    while sh * 2 < w:
                        sh *= 2
                        nlo = lo + sh
                        K.op(K.dve, lambda h, cur=cur, oth=oth, sh=sh, nlo=nlo: h.tensor_tensor(
                            out=oth[:, nlo:TT + 16], in0=cur[:, nlo:TT + 16], in1=cur[:, nlo - sh:TT + 16 - sh],
                            op=ALU.add), reads=[curb], writes=[othb])
                        cur, curb, oth, othb = oth, othb, cur, curb
                        lo = nlo
                    if t == 0:
                        tm, tmb = tmp0.next()
                        K.op(K.dve, lambda h, cur=cur, tm=tm, g=g: h.tensor_tensor(
                            out=tm[:], in0=cur[:, 16:TT + 16], in1=RC[:, g, :], op=ALU.mult),
                            reads=[curb, RC_b], writes=[tmb])
                        K.op(K.dve, lambda h, tm=tm, u_=u_, pl=pl, ci=ci: h.tensor_tensor(
                            out=pl[:, ci, :], in0=tm[:], in1=u_[:, 16:TT + 16], op=ALU.subtract),
                            reads=[tmb, ub], writes=[plb])
                    else:
                        K.op(K.dve, lambda h, cur=cur, u_=u_, pl=pl, ci=ci, w=w: h.scalar_tensor_tensor(
                            out=pl[:, ci, :], in0=cur[:, 16:TT + 16], scalar=1.0 / w, in1=u_[:, 16:TT + 16],
                            op0=ALU.mult, op1=ALU.subtract), reads=[curb, ub], writes=[plb])
                for dc in range(2):
                    bank, bb = K.next_bank()
                    pairs = [(PWs[:, g, c, dc * 128:(dc + 1) * 128], pl[:, c, :]) for c in range(2)]
                    K.mm_group(bank[:], bb, pairs, reads=[PWs_b, plb])
                    bt, btb = btr.next()
                    ch = g * 2 + dc
                    K.op(K.act, lambda h, bt=bt, bank=bank, ch=ch: h.activation(
                        out=bt[:], in_=bank[:], func=AF.Copy, scale=psT[:, e, ch:ch + 1]),
                        reads=[bb, psT_b], writes=[btb])
                    K.dma(K.sp, MIXT[8 + ch][:, t * TT:(t + 1) * TT], bt[:], reads=[btb], writes=[mixt_bufs[8 + ch][t]])

        for t in range(NT):
            norm.stage_a(l, t)
            norm.stage_b(l, t)
            latent(t, 0, qagT, qagT_b, CQN, cqn_bufs)
            latent(t, 1, kvagT, kvagT_b, CKVN, ckvn_bufs)
            krope(t)
            pool(t)
        K.end_phase()

    def even_e2(l):
        e = l // 2
        W = EW[e]
        K.begin_phase()
        K.nrot = 4
        K.bank_rr = 0
        WUQ = K.sb("WUQ", [128, 4, 1536], BF16)
        WUQS = K.sb("WUQS", [128, 4, 8, 64], BF16)
        WUKV = K.sb("WUKV", [128, 4, 2048], BF16)
        Wb = Buf()
        K.dma(K.sp, WUQ[:], W["WUQ"], reads=[W["b"]], writes=[Wb])
        K.dma(K.sp, WUQS[:], W["WUQS"], reads=[W["b"]], writes=[Wb])
        K.dma(K.sp, WUKV[:], W["WUKV"], reads=[W["b"]], writes=[Wb])
        KNs = [K.sb(f"KN{i}", [128, S], BF16) for i in range(2)]
        KRs = [K.sb(f"KR{i}", [64, S], BF16) for i in range(2)]
        VVs = [K.sb(f"VV{i}", [128, NB, 128], BF16) for i in range(2)]
        kn_bufs = [[Buf() for _ in range(NT)] for _ in range(2)]
        kr_bufs = [[Buf() for _ in range(NT)] for _ in range(2)]
        vv_bufs = [[Buf() for _ in range(NT)] for _ in range(2)]
        latr = Ring(K, "latr", 3, [128, 4, TT], BF16)
        sqr = Ring(K, "sqr", 4, [128, TT], BF16)
        kssr = Ring(K, "kssr", 2, [128, TT], F32)
        krbr = Ring(K, "krbr", 2, [64, TT], F32)
        ssr = Ring(K, "ssr", 2, [128, TT], F32)
        rsr = Ring(K, "rsr", 2, [128, TT], F32)
        rstdr = Ring(K, "rstdr", 3, [128, TT], F32)
        ctr = Ring(K, "ctr", 2, [64, 2, TT], F32)
        t12 = Ring(K, "t12", 2, [64, 3, TT], F32)
        qnr = Ring(K, "qnr", 2, [128, TT], BF16)
        qrr = Ring(K, "qrr", 2, [64, TT], BF16)
        ptr = Ring(K, "ptr", 6, [128, TT], BF16)
        linr = Ring(K, "linr", 2, [128, TT], F32)
        otr = Ring(K, "otr", 2, [128, TT], BF16)
        SCALE = float(192.0 ** -0.5)

        def build_kv_tile(hd, t):
            par = hd % 2
            KN, KR, VV = KNs[par], KRs[par], VVs[par]
            if True:
                sl = slice(t * TT, (t + 1) * TT)
                ck, ckb = latr.next()
                K.dma(K.sp, ck[:], CKVN[:, :, sl].rearrange("c p t -> p c t"), reads=[ckvn_bufs[t]], writes=[ckb])
                ks, ksb = kssr.next()
                K.dma(K.sp, ks[:], KRSSQ[:, sl], reads=[krs_bufs[t]], writes=[ksb])
                kb_, kbb = krbr.next()
                K.dma(K.sp, kb_[:], KRBs[:, sl], reads=[krb_bufs[t]], writes=[kbb])
                bank, bb = K.next_bank()
                K.mm_group(bank[:], bb, [(WUKV[:, k, hd * 256:hd * 256 + 128], ck[:, k, :]) for k in range(4)],
                           reads=[Wb, ckb])
                sq, sqb = sqr.next()
                K.op(K.act, lambda h: h.activation(out=sq[:], in_=bank[:], func=AF.Square), reads=[bb], writes=[sqb])
                bV, bVb = K.next_bank()

                def fnv(h):
                    ins = None
                    for blk in range(4):
                        for k in range(4):
                            ins = h.matmul(bV[:, blk * 128:(blk + 1) * 128], lhsT=ck[:, k, blk * 128:(blk + 1) * 128],
                                           rhs=WUKV[:, k, hd * 256 + 128:hd * 256 + 256], start=(k == 0),
                                           stop=(k == 3))
                    return ins

                K.op(K.pe, fnv, reads=[Wb, ckb], writes=[bVb])
                K.op(K.act, lambda h: h.activation(
                    out=VV[:, t * 4:(t + 1) * 4, :], in_=bV[:].rearrange("p (b d) -> p b d", b=4), func=AF.Copy),
                    reads=[bVb], writes=[vv_bufs[par][t]])
                bT, bTb = K.next_bank()
                K.op(K.pe, lambda h: h.matmul(bT[:], lhsT=ones_1[:], rhs=sq[:], start=True, stop=True),
                     reads=[sqb, ones_1_b], writes=[bTb])
                ss, ssb = ssr.next()
                K.op(K.dve, lambda h: h.tensor_tensor(out=ss[:], in0=bT[:], in1=ks[:], op=ALU.add),
                     reads=[bTb, ksb], writes=[ssb])
                rs, rsb = rsr.next()
                K.op(K.act, lambda h: h.activation(out=rs[:], in_=ss[:], func=AF.Ln, bias=epsc[:], scale=1.0 / 192),
                     reads=[ssb, epsc_b], writes=[rsb])
                rstd, rstdb = rstdr.next()
                K.op(K.act, lambda h: h.activation(out=rstd[:], in_=rs[:], func=AF.Exp, scale=-0.5), reads=[rsb],
                     writes=[rstdb])
                K.op(K.dve, lambda h: h.scalar_tensor_tensor(
                    out=KN[:, sl], in0=bank[:], scalar=GQK[:, e, 1, 0:1], in1=rstd[:], op0=ALU.mult, op1=ALU.mult),
                    reads=[bb, rstdb, GQK_b], writes=[kn_bufs[par][t]])
                K.op(K.dve, lambda h: h.tensor_tensor(out=KR[:, sl], in0=kb_[:], in1=rstd[0:64, :], op=ALU.mult),
                     reads=[kbb, rstdb], writes=[kr_bufs[par][t]])

        def qproA(hd, i):
            sl = slice(i * TT, (i + 1) * TT)
            cq, cqb = latr.next()
            K.dma(K.sp, cq[:], CQN[:, :, sl].rearrange("c p t -> p c t"), reads=[cqn_bufs[i]], writes=[cqb])
            ct, ctb = ctr.next()
            K.dma(K.sp, ct[:, 0, :], C2s[:, sl], reads=[cs_bufs[i]], writes=[ctb])
            K.dma(K.sp, ct[:, 1, :], S2s[:, sl], reads=[cs_bufs[i]], writes=[ctb])
            bN, bNb = K.next_bank()
            K.mm_group(bN[:], bNb, [(WUQ[:, k, hd * 192:hd * 192 + 128], cq[:, k, :]) for k in range(4)],
                       reads=[Wb, cqb])
            bR, bRb = K.next_bank()
            K.mm_group(bR[0:64, :], bRb, [(WUQ[:, k, hd * 192 + 128:hd * 192 + 192], cq[:, k, :]) for k in range(4)],
                       reads=[Wb, cqb])
            bS, bSb = K.next_bank()
            K.mm_group(bS[0:64, :], bSb, [(WUQS[:, k, hd, :], cq[:, k, :]) for k in range(4)], reads=[Wb, cqb])
            sq1, sq1b = sqr.next()
            K.op(K.act, lambda h: h.activation(out=sq1[:], in_=bN[:], func=AF.Square), reads=[bNb], writes=[sq1b])
            sq2, sq2b = sqr.next()
            K.op(K.act, lambda h: h.activation(out=sq2[0:64, :], in_=bR[0:64, :], func=AF.Square), reads=[bRb],
                 writes=[sq2b])
            bT, bTb = K.next_bank()

            def fns(h):
                h.matmul(bT[:], lhsT=ones_1[:], rhs=sq1[:], start=True, stop=False)
                return h.matmul(bT[:], lhsT=ones_1[0:64, :], rhs=sq2[0:64, :], start=False, stop=True)

            K.op(K.pe, fns, reads=[sq1b, sq2b, ones_1_b], writes=[bTb])
            return (hd, bT, bTb, bN, bNb, bR, bRb, bS, bSb, ct, ctb)

        def qproB(st):
            hd, bT, bTb, bN, bNb, bR, bRb, bS, bSb, ct, ctb = st
            rs, rsb = rsr.next()
            K.op(K.act, lambda h: h.activation(out=rs[:], in_=bT[:], func=AF.Ln, bias=epsc[:], scale=1.0 / 192),
                 reads=[bTb, epsc_b], writes=[rsb])
            rstd, rstdb = rstdr.next()
            K.op(K.act, lambda h: h.activation(out=rstd[:], in_=rs[:], func=AF.Exp, scale=-0.5), reads=[rsb],
                 writes=[rstdb])
            qn, qnb = qnr.next()
            K.op(K.dve, lambda h: h.scalar_tensor_tensor(
                out=qn[:], in0=bN[:], scalar=GQK[:, e, 0, 0:1], in1=rstd[:], op0=ALU.mult, op1=ALU.mult),
                reads=[bNb, rstdb, GQK_b], writes=[qnb])
            tt_, ttb = t12.next()
            K.op(K.dve, lambda h: h.scalar_tensor_tensor(
                out=tt_[:, 0, :], in0=bR[0:64, :], scalar=GQK[0:64, e, 0, 1:2], in1=ct[:, 0, :], op0=ALU.mult,
                op1=ALU.mult), reads=[bRb, ctb, GQK_b], writes=[ttb])
            K.op(K.dve, lambda h: h.scalar_tensor_tensor(
                out=tt_[:, 1, :], in0=bS[0:64, :], scalar=GQK[0:64, e, 0, 2:3], in1=ct[:, 1, :], op0=ALU.mult,
                op1=ALU.mult), reads=[bSb, ctb, GQK_b], writes=[ttb])
            K.op(K.dve, lambda h: h.tensor_tensor(out=tt_[:, 2, :], in0=tt_[:, 0, :], in1=tt_[:, 1, :], op=ALU.add),
                 reads=[ttb], writes=[ttb])
            qr, qrb = qrr.next()
            K.op(K.dve, lambda h: h.tensor_tensor(out=qr[:], in0=tt_[:, 2, :], in1=rstd[0:64, :], op=ALU.mult),
                 reads=[ttb, rstdb], writes=[qrb])
            return qn, qnb, qr, qrb

        def attention(hd, i, q, hook=None):
            qn, qnb, qr, qrb = q
            par = hd % 2
            KN, KR, VV = KNs[par], KRs[par], VVs[par]
            sl = slice(i * TT, (i + 1) * TT)
            bO, bOb = K.banks[4 + (i % 2)], K.bank_bufs[4 + (i % 2)]
            bL, bLb = K.banks[6 + (i % 2)], K.bank_bufs[6 + (i % 2)]
            nkb = 4 * i + 4

            def emit_s(j):
                d = j - 4 * i
                c0 = 128 * d if d > 0 else 0
                bank, bb = K.next_bank()
                tk = j // 4
                K.mm_group(bank[:, c0:], bb, [(KN[:, j * 128:(j + 1) * 128], qn[:, c0:]),
                                              (KR[:, j * 128:(j + 1) * 128], qr[:, c0:])],
                           reads=[kn_bufs[par][tk], kr_bufs[par][tk], qnb, qrb])
                pt, ptb = ptr.next()
                K.op(K.act, lambda h: h.activation(out=pt[:, c0:], in_=bank[:, c0:], func=AF.Exp,
                                                   bias=negc[:, e:e + 1], scale=SCALE),
                     reads=[bb, negc_b], writes=[ptb])
                if d >= 0:
                    K.op(K.dve, lambda h: h.tensor_tensor(out=pt[:, c0:c0 + 128], in0=pt[:, c0:c0 + 128],
                                                          in1=tri[:], op=ALU.mult), reads=[ptb, tri_b], writes=[ptb])
                return pt, ptb, c0

            DEPTH = 2
            pend = [emit_s(j) for j in range(min(DEPTH, nkb))]
            for j in range(nkb):
                pt, ptb, c0 = pend.pop(0)
                if j + DEPTH < nkb:
                    pend.append(emit_s(j + DEPTH))
                tk = j // 4

                def fpv(h, pt=pt, c0=c0, j=j):
                    h.matmul(bO[:, c0:], lhsT=VV[:, j, :], rhs=pt[:, c0:], start=(j == 0), stop=(j == nkb - 1))
                    return h.matmul(bL[:, c0:], lhsT=ones_1[:], rhs=pt[:, c0:], start=(j == 0), stop=(j == nkb - 1))

                K.op(K.pe, fpv, reads=[vv_bufs[par][tk], ones_1_b, ptb], writes=[bOb, bLb])
                if j == 0 and hook is not None:
                    hook()
            li_, lib = linr.next()
            K.op(K.dve, lambda h: h.reciprocal(out=li_[:], in_=bL[:]), reads=[bLb], writes=[lib])
            ot, otb = otr.next()
            K.op(K.dve, lambda h: h.tensor_tensor(out=ot[:], in0=bO[:], in1=li_[:], op=ALU.mult),
                 reads=[bOb, lib], writes=[otb])
            K.dma(K.sp, MIXT[hd][:, sl], ot[:], reads=[otb], writes=[mixt_bufs[hd][i]])

        for t in range(NT):
            build_kv_tile(0, t)
        q = qproB(qproA(0, 0))
        seq = [(hd, i) for hd in range(NH) for i in range(NT)]
        for n, (hd, i) in enumerate(seq):
            nq = qproB(qproA(*seq[n + 1])) if n + 1 < len(seq) else None

            def hook(hd=hd, i=i):
                if hd + 1 < NH:
                    build_kv_tile(hd + 1, i)

            attention(hd, i, q, hook)
            q = nq
        K.nrot = 7
        K.bank_rr = 0
        K.end_phase()

    def even_e3(l):
        e = l // 2
        W = EW[e]
        K.begin_phase()
        WEO = K.sb("WEO", [128, NC_, D], BF16)
        WEO_b = Buf()
        load_resident(WEO, W["WEO"], W["b"], WEO_b, 4, NC_)
        mtr = Ring(K, "mtr", 2, [128, NC_, TT], BF16)
        xres = Ring(K, "xres", 2, [128, TT], F32)
        ores = Ring(K, "ores", 3, [128, TT], F32)

        def load(t):
            mt, mtb = mtr.next()
            K.dma(K.sp, mt[:], MIXT[:, :, t * TT:(t + 1) * TT].rearrange("c p t -> p c t"),
                  reads=[mixt_bufs[c][t] for c in range(NC_)], writes=[mtb])
            return mt, mtb

        nxt = load(0)
        for t in range(NT):
            mt, mtb = nxt
            if t + 1 < NT:
                nxt = load(t + 1)
            outproj_tile(WEO, WEO_b, mt, [mtb], t, xres, ores)
        K.end_phase()

    def cast_layer(l, gate=None):
        if do_mixer:
            cast_mixer_weights(l, gate)
        if do_mlp:
            cast_mlp_weights(l, gate)

    cast_layer(layers[0])
    prologue()
    has_even = do_mixer and any(l % 2 == 0 for l in layers)
    if has_even:
        rope_tables()
    for li_, l in enumerate(layers):
        if li_ + 1 < len(layers):
            gate = (K.dve.sem, K.dve.cnt) if K.dve.cnt > 0 else None
            cast_layer(layers[li_ + 1], gate)
        if do_mixer:
            if l % 2 == 0:
                even_setup(l // 2)
                even_e1(l)
                even_e2(l)
                even_e3(l)
            else:
                odd_mixer(l)
        if do_mlp:
            mlp_phase(l)
    epilogue()
    K.barrier()
    K.es.close()
    return nc, K


def make_consts():
    p = np.arange(128)
    tri = (p[None, :] >= p[:, None]).astype(np.float32)
    inv_freq = (1.0 / (np.float32(10000.0) ** (np.arange(0, 64, 2, dtype=np.float32) / np.float32(64)))).astype(np.float32)
    rope = np.zeros((64, 2), np.float32)
    rope[:, 0] = np.concatenate([inv_freq, inv_freq])
    rope[:32, 1] = -1.0
    rope[32:, 1] = 1.0
    t = np.arange(TT)
    rc = np.zeros((128, 4, TT), np.float32)
    for g in range(4):
        w = 2 << g
        rc[:, g, :] = (1.0 / np.minimum(t + 1, w)).astype(np.float32)[None, :]
    return {"c_ident": np.eye(128, dtype=np.float32), "c_tri": tri, "c_rope": rope, "c_rc": rc}


_CACHE = {}


def kernel(**inputs):
    S = 4096
    n = 8
    key = ("full", S)
    if key not in _CACHE:
        _CACHE[key] = build_program(S, [0, 1, 2, 3])
    nc, _ = _CACHE[key]
    consts = make_consts()
    shared = {k: np.ascontiguousarray(v) for k, v in inputs.items() if k not in ("x", "positions")}
    in_maps = []
    for b in range(n):
        m = dict(shared)
        m.update(consts)
        m["x"] = np.ascontiguousarray(inputs["x"][b])
        m["positions"] = np.ascontiguousarray(inputs["positions"][b:b + 1])
        in_maps.append(m)
    res = run_bass_kernel_spmd(nc, in_maps, core_ids=list(range(n)))
    out = np.stack([np.asarray(r["y"]) for r in res.results], axis=0)
    return out.astype(np.float32, copy=False)
```

```python
import contextlib
import numpy as np
import ml_dtypes
import concourse.bass as bass
import concourse.mybir as mybir
from concourse.bass_utils import run_bass_kernel_spmd

F32 = mybir.dt.float32
BF16 = mybir.dt.bfloat16
I32 = mybir.dt.int32
ALU = mybir.AluOpType
AF = mybir.ActivationFunctionType

D = 2048
DFF = 8192
NC_ = 16
TT = 512
EPS = 1e-6
NH = 8
SEM_ROT = 8000


class Buf:
    __slots__ = ("w", "r", "name", "excl")

    def __init__(self, name="", excl=False):
        self.w = None
        self.r = {}
        self.name = name
        self.excl = excl


class Eng:
    def __init__(self, K, h, name, is_pe=False, ndma=0):
        self.K = K
        self.h = h
        self.name = name
        self.is_pe = is_pe
        self.sem = None
        self.cnt = 0
        self.seen = {}
        self.old = []
        self.dsems = [K.new_sem(f"{name}_d{i}") for i in range(ndma)]
        self.dcnt = [0] * ndma
        self.di = 0
        self.nsem = 0

    def wait(self, sem, val):
        if self.seen.get(sem, 0) >= val:
            return
        self.h.wait_ge(sem, val)
        self.seen[sem] = val

    def signal(self, ins):
        if self.sem is None or self.cnt >= SEM_ROT:
            if self.sem is not None:
                self.old.append((self.sem, self.cnt))
            self.sem = self.K.new_sem(f"{self.name}_s{self.nsem}")
            self.nsem += 1
            self.cnt = 0
        self.cnt += 1
        ins.then_inc(self.sem, 1)
        return (self.sem, self.cnt)


class CastGroup:
    def __init__(self, K, name):
        self.sem = K.new_sem(name)
        self.n = 0
        self.buf = Buf(name)
        self.gated = False


class KB:
    def __init__(self, nc):
        self.nc = nc
        self.es = contextlib.ExitStack()
        self.nsem = 0
        self.pe = Eng(self, nc.tensor, "pe", is_pe=True)
        self.act = Eng(self, nc.scalar, "act")
        self.dve = Eng(self, nc.vector, "dve")
        self.pool = Eng(self, nc.gpsimd, "pool", ndma=8)
        self.sp = Eng(self, nc.sync, "sp", ndma=12)
        self.engs = [self.pe, self.act, self.dve, self.pool, self.sp]
        self.banks = []
        self.bank_bufs = []
        for i in range(8):
            self.banks.append(self.es.enter_context(nc.psum_tensor(f"psb{i}", [128, 512], F32)))
            self.bank_bufs.append(Buf(f"bank{i}", excl=True))
        self.bank_rr = 0
        self.nrot = 7
        self.uid = 0
        self.phase_es = None

    def new_sem(self, name):
        self.nsem += 1
        return self.es.enter_context(self.nc.semaphore(name))

    def sb(self, name, shape, dtype, glob=False):
        self.uid += 1
        es = self.es if glob or self.phase_es is None else self.phase_es
        return es.enter_context(self.nc.sbuf_tensor(f"{name}_{self.uid}", list(shape), dtype))

    def begin_phase(self):
        self.phase_es = contextlib.ExitStack()

    def end_phase(self):
        self.barrier()
        self.phase_es.close()
        self.phase_es = None

    def next_bank(self):
        i = self.bank_rr
        self.bank_rr = (self.bank_rr + 1) % self.nrot
        return self.banks[i], self.bank_bufs[i]

    def _deps(self, reads, writes, own=None):
        deps = {}

        def add(tok):
            if tok is None:
                return
            s, v = tok
            if deps.get(s, 0) < v:
                deps[s] = v

        for b in reads:
            add(b.w)
            if b.excl:
                for s, v in b.r.items():
                    if s is not own:
                        add((s, v))
        for b in writes:
            add(b.w)
            for s, v in b.r.items():
                add((s, v))
        return deps

    def _commit(self, tok, reads, writes):
        s, v = tok
        for b in reads:
            if b.r.get(s, 0) < v:
                b.r[s] = v
        for b in writes:
            b.w = tok
            b.r = {}

    def op(self, eng, fn, reads=(), writes=()):
        deps = self._deps(reads, writes, own=eng.sem)
        for s, v in deps.items():
            if eng.is_pe and s is eng.sem:
                continue
            eng.wait(s, v)
        ins = fn(eng.h)
        tok = eng.signal(ins)
        self._commit(tok, reads, writes)
        return tok

    def dma(self, q, out, in_, reads=(), writes=(), **kw):
        deps = self._deps(reads, writes)
        i = q.di
        q.di = (q.di + 1) % len(q.dsems)
        sem = q.dsems[i]
        if q.dcnt[i] > 0:
            q.wait(sem, 16 * q.dcnt[i])
        for s, v in deps.items():
            q.wait(s, v)
        ins = q.h.dma_start(out=out, in_=in_, **kw)
        q.dcnt[i] += 1
        ins.then_inc(sem, 16)
        tok = (sem, 16 * q.dcnt[i])
        self._commit(tok, reads, writes)
        return tok

    def cast(self, grp, out, in_, gate=None):
        if gate is not None and not grp.gated:
            self.pool.wait(*gate)
        grp.gated = True
        ins = self.pool.h.dma_start(out=out, in_=in_)
        grp.n += 1
        ins.then_inc(grp.sem, 16)
        grp.buf.w = (grp.sem, 16 * grp.n)

    def barrier(self):
        toks = []
        for e in self.engs:
            if e.sem is not None and e.cnt > 0:
                toks.append((e.sem, e.cnt))
            for s, c in zip(e.dsems, e.dcnt):
                if c > 0:
                    toks.append((s, 16 * c))
        for e in self.engs:
            for s, v in toks:
                e.wait(s, v)

    def mm_group(self, out_ap, bank_buf, pairs, reads):
        n = len(pairs)

        def fn(h):
            ins = None
            for i, (l, r) in enumerate(pairs):
                ins = h.matmul(out_ap, lhsT=l, rhs=r, start=(i == 0), stop=(i == n - 1))
            return ins

        return self.op(self.pe, fn, reads=reads, writes=[bank_buf])


def _ring(K, name, n, shape, dtype):
    return [(K.sb(f"{name}{i}", shape, dtype), Buf(f"{name}{i}")) for i in range(n)]


class Ring:
    def __init__(self, K, name, n, shape, dtype):
        self.items = _ring(K, name, n, shape, dtype)
        self.i = 0

    def next(self):
        it = self.items[self.i]
        self.i = (self.i + 1) % len(self.items)
        return it


def build_program(S, layers, do_mixer=True, do_mlp=True):
    NT = S // TT
    NB = S // 128
    nc = bass.Bass("TRN2", target_bir_lowering=False)
    K = KB(nc)

    def din(name, shape, dt=F32):
        return nc.dram_tensor(name, list(shape), dt, kind="ExternalInput").ap()

    x_in = din("x", [S, D])
    pos_in = din("positions", [1, S], I32)
    mix_g = din("mix_norm_g", [4, D])
    mlp_g = din("mlp_norm_g", [4, D])
    w_up = din("w_mlp_up", [4, D, DFF])
    w_down = din("w_mlp_down", [4, DFF, D])
    e_w_in = din("even_w_in", [2, D, 2112])
    e_qa_g = din("even_q_a_norm_g", [2, 512])
    e_kva_g = din("even_kv_a_norm_g", [2, 512])
    e_w_uq = din("even_w_uq", [2, 512, 1536])
    e_w_ukv = din("even_w_ukv", [2, 512, 2048])
    e_qn_g = din("even_q_norm_g", [2, 192])
    e_kn_g = din("even_k_norm_g", [2, 192])
    e_pool_w = din("even_pool_w", [2, 4, 256, 256])
    e_pool_s = din("even_pool_scale", [2, 1024])
    e_w_out = din("even_w_out", [2, D, D])
    o_w_in = din("odd_w_in", [2, D, 3 * D])
    o_conv_w = din("odd_conv_w", [2, 3, D])
    o_w_out = din("odd_w_out", [2, D, D])
    ident_in = din("c_ident", [128, 128])
    tri_in = din("c_tri", [128, 128])
    rope_in = din("c_rope", [64, 2])
    rc_in = din("c_rc", [128, 4, TT])
    y_out = nc.dram_tensor("y", [S, D], F32, kind="ExternalOutput").ap()

    XT = nc.dram_tensor("XT", [NC_, 128, S], F32).ap()
    xt_bufs = [[Buf(f"xt{c}_{t}") for t in range(NT)] for c in range(NC_)]
    WU = {}
    WD = {}
    wu_bufs = {}
    wd_bufs = {}
    for l in layers:
        WU[l] = nc.dram_tensor(f"WU{l}", [16, 128, 16, 512], BF16).ap()
        WD[l] = nc.dram_tensor(f"WD{l}", [16, 128, 64, 128], BF16).ap()
        wu_bufs[l] = [Buf() for _ in range(16)]
        wd_bufs[l] = [Buf() for _ in range(16)]

    ident = K.sb("ident", [128, 128], F32, glob=True)
    ident_b = Buf("ident")
    K.dma(K.sp, ident[:], ident_in, writes=[ident_b])
    ones_d = K.sb("ones_d", [128, 128], BF16, glob=True)
    ones_d_b = Buf("ones_d")
    K.op(K.dve, lambda h: h.memset(ones_d[:], 1.0 / D), writes=[ones_d_b])
    mlp_gT = K.sb("mlp_gT", [128, 4, NC_], F32, glob=True)
    mlp_gT_b = Buf("mlp_gT")
    mix_gT = K.sb("mix_gT", [128, 4, NC_], F32, glob=True)
    mix_gT_b = Buf("mix_gT")
    def load_transposed(parts, dsts):
        K.begin_phase()
        stg = K.sb("vstage", [128, 128], F32)
        stg_b = Buf()
        K.op(K.dve, lambda h: h.memset(stg[:], 0.0), writes=[stg_b])
        for r0, n, ap in parts:
            K.dma(K.sp, stg[r0:r0 + n, :], ap, writes=[stg_b])
        bank, bb = K.next_bank()
        K.op(K.pe, lambda h: h.transpose(out=bank[:, 0:128], in_=stg[:], identity=ident[:]),
             reads=[stg_b, ident_b], writes=[bb])
        for c0, n, view, vb in dsts:
            K.op(K.dve, lambda h, c0=c0, n=n, view=view: h.tensor_copy(out=view, in_=bank[:, c0:c0 + n]),
                 reads=[bb], writes=[vb])
        K.end_phase()

    load_transposed(
        [(0, 64, mlp_g.rearrange("l (c p) -> (l c) p", p=128)), (64, 64, mix_g.rearrange("l (c p) -> (l c) p", p=128))],
        [(0, 64, mlp_gT[:].rearrange("p l c -> p (l c)"), mlp_gT_b),
         (64, 64, mix_gT[:].rearrange("p l c -> p (l c)"), mix_gT_b)])

    cg_up = {l: CastGroup(K, f"cg_up{l}") for l in layers}
    cg_dn = {l: CastGroup(K, f"cg_dn{l}") for l in layers}
    cg_mx = {l: CastGroup(K, f"cg_mx{l}") for l in layers}

    def cast_mlp_weights(l, gate=None):
        src_u = w_up[l].rearrange("(k p) (mg c) -> mg p k c", p=128, c=512)
        src_d = w_down[l].rearrange("(k p) (m c) -> m p k c", p=128, c=128)
        for mg in range(16):
            K.cast(cg_up[l], WU[l][mg], src_u[mg], gate)
            wu_bufs[l][mg] = cg_up[l].buf
        for m in range(16):
            K.cast(cg_dn[l], WD[l][m], src_d[m], gate)
            wd_bufs[l][m] = cg_dn[l].buf

    def prologue():
        K.begin_phase()
        xr = Ring(K, "xtok", 4, [128, D], F32)
        st = Ring(K, "xstage", 4, [128, NC_, 128], F32)
        for tb in range(NB):
            xt_, xb = xr.next()
            K.dma(K.sp, xt_[:], x_in[tb * 128:(tb + 1) * 128, :], writes=[xb])
            sg, sgb = st.next()
            for q in range(4):
                bank, bb = K.next_bank()

                def fn(h, q=q, bank=bank, xt_=xt_):
                    ins = None
                    for j in range(4):
                        c = q * 4 + j
                        ins = h.transpose(out=bank[:, j * 128:(j + 1) * 128], in_=xt_[:, c * 128:(c + 1) * 128],
                                          identity=ident[:])
                    return ins

                K.op(K.pe, fn, reads=[xb, ident_b], writes=[bb])
                eng = K.act if q % 2 == 0 else K.dve
                if eng is K.act:
                    K.op(eng, lambda h, q=q, bank=bank, sg=sg: h.activation(
                        out=sg[:, q * 4:(q + 1) * 4, :], in_=bank[:].rearrange("p (j t) -> p j t", j=4),
                        func=AF.Copy), reads=[bb], writes=[sgb])
                else:
                    K.op(eng, lambda h, q=q, bank=bank, sg=sg: h.tensor_copy(
                        out=sg[:, q * 4:(q + 1) * 4, :], in_=bank[:].rearrange("p (j t) -> p j t", j=4)),
                        reads=[bb], writes=[sgb])
            t = tb // 4
            K.dma(K.sp, XT[:, :, tb * 128:(tb + 1) * 128].rearrange("c p t -> p c t"), sg[:],
                  reads=[sgb], writes=[xt_bufs[c][t] for c in range(NC_)])
        K.end_phase()

    def epilogue():
        K.begin_phase()
        ld = Ring(K, "eld", 4, [128, NC_, 128], F32)
        ot = Ring(K, "eout", 4, [128, D], F32)
        for tb in range(NB):
            t = tb // 4
            lt, lb = ld.next()
            K.dma(K.sp, lt[:], XT[:, :, tb * 128:(tb + 1) * 128].rearrange("c p t -> p c t"),
                  reads=[xt_bufs[c][t] for c in range(NC_)], writes=[lb])
            og, ogb = ot.next()
            for q in range(4):
                bank, bb = K.next_bank()

                def fn(h, q=q, bank=bank, lt=lt):
                    ins = None
                    for j in range(4):
                        c = q * 4 + j
                        ins = h.transpose(out=bank[:, j * 128:(j + 1) * 128], in_=lt[:, c, :], identity=ident[:])
                    return ins

                K.op(K.pe, fn, reads=[lb, ident_b], writes=[bb])
                if q % 2 == 0:
                    K.op(K.act, lambda h, q=q, bank=bank, og=og: h.activation(
                        out=og[:, q * 512:(q + 1) * 512], in_=bank[:], func=AF.Copy), reads=[bb], writes=[ogb])
                else:
                    K.op(K.dve, lambda h, q=q, bank=bank, og=og: h.tensor_copy(
                        out=og[:, q * 512:(q + 1) * 512], in_=bank[:]), reads=[bb], writes=[ogb])
            K.dma(K.sp, y_out[tb * 128:(tb + 1) * 128, :], og[:], reads=[ogb], writes=[Buf()])
        K.end_phase()

    class Norm:
        def __init__(self, gT, gT_b, HT, ht_bufs):
            self.xr = Ring(K, "nx", 4, [128, TT], F32)
            self.sq = Ring(K, "nsq", 2, [128, TT], BF16)
            self.rs = K.sb("nrs", [128, TT], F32)
            self.rs_b = Buf("nrs")
            self.rstd = K.sb("nrstd", [128, TT], F32)
            self.rstd_b = Buf("nrstd")
            self.gT, self.gT_b, self.HT, self.ht_bufs = gT, gT_b, HT, ht_bufs
            self.stat_bank = K.banks[7]
            self.stat_bb = K.bank_bufs[7]

        def stage_a(self, l, t):
            pend = []
            for c in range(NC_):
                xt_, xb = self.xr.next()
                K.dma(K.sp, xt_[:], XT[c][:, t * TT:(t + 1) * TT], reads=[xt_bufs[c][t]], writes=[xb])
                sq, sqb = self.sq.next()
                K.op(K.act, lambda h, sq=sq, xt_=xt_: h.activation(out=sq[:], in_=xt_[:], func=AF.Square),
                     reads=[xb], writes=[sqb])
                K.op(K.pe, lambda h, sq=sq, c=c: h.matmul(self.stat_bank[:], lhsT=ones_d[:], rhs=sq[:],
                                                         start=(c == 0), stop=(c == NC_ - 1)),
                     reads=[sqb, ones_d_b], writes=[self.stat_bb])
            K.op(K.act, lambda h: h.activation(out=self.rs[:], in_=self.stat_bank[:], func=AF.Sqrt, bias=EPS,
                                               scale=1.0), reads=[self.stat_bb], writes=[self.rs_b])
            K.op(K.dve, lambda h: h.reciprocal(out=self.rstd[:], in_=self.rs[:]), reads=[self.rs_b],
                 writes=[self.rstd_b])

        def stage_b(self, l, t):
            for c in range(NC_):
                xt_, xb = self.xr.next()
                K.dma(K.sp, xt_[:], XT[c][:, t * TT:(t + 1) * TT], reads=[xt_bufs[c][t]], writes=[xb])
                K.op(K.dve, lambda h, xt_=xt_, c=c: h.scalar_tensor_tensor(
                    out=self.HT[:, c, :], in0=xt_[:], scalar=self.gT[:, l, c:c + 1], in1=self.rstd[:],
                    op0=ALU.mult, op1=ALU.mult), reads=[xb, self.gT_b, self.rstd_b], writes=[self.ht_bufs[c]])

    def mlp_phase(l):
        K.begin_phase()
        HT = K.sb("HT", [128, NC_, TT], BF16)
        ht_bufs = [Buf(f"ht{c}") for c in range(NC_)]
        AT = K.sb("AT", [128, 64, TT], BF16)
        at_bufs = [Buf(f"at{c}") for c in range(64)]
        wus = Ring(K, "wus", 2, [128, 16, 512], BF16)
        wds = Ring(K, "wds", 2, [128, 64, 128], BF16)
        xres = Ring(K, "xres", 2, [128, TT], F32)
        ores = Ring(K, "ores", 3, [128, TT], F32)
        relu_r = Ring(K, "relu_r", 3, [128, TT], F32)
        norm = Norm(mlp_gT, mlp_gT_b, HT, ht_bufs)

        loads = []
        for t in range(NT):
            for mg in range(16):
                loads.append(("u", t, mg))
            for m in range(16):
                loads.append(("d", t, m))
        slot_of = {}

        def issue_load(i):
            if i >= len(loads):
                return
            kind, t, j = loads[i]
            if kind == "u":
                s, sb_ = wus.next()
                K.dma(K.sp, s[:], WU[l][j], reads=[wu_bufs[l][j]], writes=[sb_])
            else:
                s, sb_ = wds.next()
                K.dma(K.sp, s[:], WD[l][j], reads=[wd_bufs[l][j]], writes=[sb_])
            slot_of[i] = (s, sb_)

        norm.stage_a(l, 0)
        norm.stage_b(l, 0)
        issue_load(0)
        issue_load(1)
        li = 0
        for t in range(NT):
            for mg in range(16):
                s, sb_ = slot_of.pop(li)
                for j in range(4):
                    bank, bb = K.next_bank()
                    pairs = [(s[:, k, j * 128:(j + 1) * 128], HT[:, k, :]) for k in range(NC_)]
                    K.mm_group(bank[:], bb, pairs, reads=[sb_] + ht_bufs)
                    ci = mg * 4 + j
                    r_, rb = relu_r.next()
                    K.op(K.act, lambda h, bank=bank, r_=r_: h.activation(out=r_[:], in_=bank[:], func=AF.Relu),
                         reads=[bb], writes=[rb])
                    K.op(K.dve, lambda h, r_=r_, ci=ci: h.tensor_tensor(
                        out=AT[:, ci, :], in0=r_[:], in1=r_[:], op=ALU.mult),
                        reads=[rb], writes=[at_bufs[ci]])
                li += 1
                issue_load(li + 1)
            for m in range(16):
                s, sb_ = slot_of.pop(li)
                bank, bb = K.next_bank()
                pairs = [(s[:, k, :], AT[:, k, :]) for k in range(64)]
                K.mm_group(bank[:], bb, pairs, reads=[sb_] + at_bufs)
                xr_, xrb = xres.next()
                K.dma(K.sp, xr_[:], XT[m][:, t * TT:(t + 1) * TT], reads=[xt_bufs[m][t]], writes=[xrb])
                o_, ob = ores.next()
                K.op(K.dve, lambda h, bank=bank, xr_=xr_, o_=o_: h.tensor_tensor(
                    out=o_[:], in0=bank[:], in1=xr_[:], op=ALU.add), reads=[bb, xrb], writes=[ob])
                K.dma(K.sp, XT[m][:, t * TT:(t + 1) * TT], o_[:], reads=[ob], writes=[xt_bufs[m][t]])
                li += 1
                issue_load(li + 1)
                if t + 1 < NT:
                    if m == 3:
                        norm.stage_a(l, t + 1)
                    if m == 9:
                        norm.stage_b(l, t + 1)
        K.end_phase()


    def small(name, shape, dt=F32):
        return K.sb(name, shape, dt, glob=True), Buf(name)

    ones_512, ones_512_b = small("ones_512", [128, 128], BF16)
    ones_1, ones_1_b = small("ones_1", [128, 128], BF16)
    K.op(K.dve, lambda h: h.memset(ones_512[:], 1.0 / 512), writes=[ones_512_b])
    K.op(K.dve, lambda h: h.memset(ones_1[:], 1.0), writes=[ones_1_b])
    halfpi, halfpi_b = small("halfpi", [128, 1], F32)
    K.op(K.dve, lambda h: h.memset(halfpi[:], float(np.pi / 2)), writes=[halfpi_b])
    epsc, epsc_b = small("epsc", [128, 1], F32)
    K.op(K.dve, lambda h: h.memset(epsc[:], EPS), writes=[epsc_b])
    ones_f, ones_f_b = small("ones_f", [1, 128], F32)
    K.op(K.dve, lambda h: h.memset(ones_f[:], 1.0), writes=[ones_f_b])
    tri_f, tri_f_b = small("tri_f", [128, 128], F32)
    K.dma(K.sp, tri_f[:], tri_in, writes=[tri_f_b])
    tri, tri_b = small("tri", [128, 128], BF16)
    K.op(K.dve, lambda h: h.tensor_copy(out=tri[:], in_=tri_f[:]), reads=[tri_f_b], writes=[tri_b])
    ropec, ropec_b = small("ropec", [64, 2], F32)
    K.dma(K.sp, ropec[:], rope_in, writes=[ropec_b])
    cwT, cwT_b = small("cwT", [128, 2, 3, NC_], F32)
    psT, psT_b = small("psT", [128, 2, 8], F32)
    qagT, qagT_b = small("qagT", [128, 2, 4], F32)
    kvagT, kvagT_b = small("kvagT", [128, 2, 4], F32)
    GQK, GQK_b = small("GQK", [128, 2, 2, 3], F32)
    load_transposed(
        [(0, 96, o_conv_w.rearrange("o j (c p) -> (o j c) p", p=128)),
         (96, 16, e_pool_s.rearrange("e (c p) -> (e c) p", p=128)),
         (112, 8, e_qa_g.rearrange("e (c p) -> (e c) p", p=128)),
         (120, 8, e_kva_g.rearrange("e (c p) -> (e c) p", p=128))],
        [(0, 96, cwT[:].rearrange("p o j c -> p (o j c)"), cwT_b),
         (96, 16, psT[:].rearrange("p e c -> p (e c)"), psT_b),
         (112, 8, qagT[:].rearrange("p e c -> p (e c)"), qagT_b),
         (120, 8, kvagT[:].rearrange("p e c -> p (e c)"), kvagT_b)])
    gparts = []
    for e in range(2):
        for qk, g in enumerate((e_qn_g, e_kn_g)):
            r = (e * 2 + qk) * 3
            gparts.append((r, 1, g[e:e + 1, 0:128]))
            gparts.append((r + 1, 1, g[e:e + 1, 128:192]))
            gparts.append((r + 2, 1, g[e:e + 1, 160:192]))
            gparts.append((r + 2, 1, g[e:e + 1, 128:160]))
    K.begin_phase()
    stg = K.sb("gstage", [128, 128], F32)
    stg_b = Buf()
    K.op(K.dve, lambda h: h.memset(stg[:], 0.0), writes=[stg_b])
    for e in range(2):
        for qk, g in enumerate((e_qn_g, e_kn_g)):
            r = (e * 2 + qk) * 3
            K.dma(K.sp, stg[r:r + 1, 0:128], g[e:e + 1, 0:128], writes=[stg_b])
            K.dma(K.sp, stg[r + 1:r + 2, 0:64], g[e:e + 1, 128:192], writes=[stg_b])
            K.dma(K.sp, stg[r + 2:r + 3, 0:32], g[e:e + 1, 160:192], writes=[stg_b])
            K.dma(K.sp, stg[r + 2:r + 3, 32:64], g[e:e + 1, 128:160], writes=[stg_b])
    bank, bb = K.next_bank()
    K.op(K.pe, lambda h: h.transpose(out=bank[:, 0:128], in_=stg[:], identity=ident[:]),
         reads=[stg_b, ident_b], writes=[bb])
    K.op(K.dve, lambda h: h.tensor_copy(out=GQK[:].rearrange("p e q k -> p (e q k)"), in_=bank[:, 0:12]),
         reads=[bb], writes=[GQK_b])
    K.end_phase()

    def dscr(name, shape, dt):
        return nc.dram_tensor(name, list(shape), dt).ap()

    C2s = dscr("C2s", [64, S], F32)
    S2s = dscr("S2s", [64, S], F32)
    cs_bufs = [Buf() for _ in range(NT)]
    CQN = dscr("CQN", [4, 128, S], BF16)
    CKVN = dscr("CKVN", [4, 128, S], BF16)
    cqn_bufs = [Buf() for _ in range(NT)]
    ckvn_bufs = [Buf() for _ in range(NT)]
    KRBs = dscr("KRBs", [64, S], F32)
    KRSSQ = dscr("KRSSQ", [128, S], F32)
    krb_bufs = [Buf() for _ in range(NT)]
    krs_bufs = [Buf() for _ in range(NT)]
    MIXT = dscr("MIXT", [NC_, 128, S], BF16)
    mixt_bufs = [[Buf() for _ in range(NT)] for _ in range(NC_)]
    EW = {}
    OW = {}
    for l in layers:
        if l % 2 == 0:
            e = l // 2
            EW[e] = dict(
                WEI=dscr(f"WEI{e}", [4, 128, 16, 512], BF16), WEIR=dscr(f"WEIR{e}", [128, 16, 128], BF16),
                WUQ=dscr(f"WUQ{e}", [128, 4, 1536], BF16), WUQS=dscr(f"WUQS{e}", [128, 4, 8, 64], BF16),
                WUKV=dscr(f"WUKV{e}", [128, 4, 2048], BF16), WEO=dscr(f"WEO{e}", [128, 16, 2048], BF16),
                PW=dscr(f"PW{e}", [128, 4, 2, 256], BF16), b=Buf())
        else:
            o = l // 2
            OW[o] = dict(WCI=dscr(f"WCI{o}", [16, 128, 3, 16, 128], BF16), WCO=dscr(f"WCO{o}", [128, 16, 2048], BF16),
                         b=Buf())

    def cast_mixer_weights(l, gate=None):
        if l % 2 == 0:
            e = l // 2
            W = EW[e]
            W["b"] = cg_mx[l].buf
            cols = [(0, 512), (512, 1024), (1088, 1600), (1600, 2112)]
            for g, (a, b_) in enumerate(cols):
                K.cast(cg_mx[l], W["WEI"][g], e_w_in[e][:, a:b_].rearrange("(k p) c -> p k c", p=128), gate)
            for (da, db, sa, sb_) in ((0, 64, 1024, 1088), (64, 96, 1056, 1088), (96, 128, 1024, 1056)):
                K.cast(cg_mx[l], W["WEIR"][:, :, da:db], e_w_in[e][:, sa:sb_].rearrange("(k p) c -> p k c", p=128), gate)
            K.cast(cg_mx[l], W["WUQ"], e_w_uq[e].rearrange("(k p) n -> p k n", p=128), gate)
            for k in range(4):
                v = e_w_uq[e][k * 128:(k + 1) * 128, :].rearrange("p (h d) -> p h d", d=192)
                K.cast(cg_mx[l], W["WUQS"][:, k, :, 0:32], v[:, :, 160:192], gate)
                K.cast(cg_mx[l], W["WUQS"][:, k, :, 32:64], v[:, :, 128:160], gate)
            K.cast(cg_mx[l], W["WUKV"], e_w_ukv[e].rearrange("(k p) n -> p k n", p=128), gate)
            for kq in range(4):
                K.cast(cg_mx[l], W["WEO"][:, kq * 4:(kq + 1) * 4, :],
                      e_w_out[e][kq * 512:(kq + 1) * 512, :].rearrange("(k p) n -> p k n", p=128), gate)
            for g in range(4):
                K.cast(cg_mx[l], W["PW"][:, g], e_pool_w[e][g].rearrange("(c p) d -> p c d", p=128), gate)
        else:
            o = l // 2
            W = OW[o]
            W["b"] = cg_mx[l].buf
            for c in range(NC_):
                for j in range(3):
                    K.cast(cg_mx[l], W["WCI"][c][:, j],
                          o_w_in[o][:, j * D + c * 128: j * D + (c + 1) * 128].rearrange("(k p) q -> p k q", p=128), gate)
            for kq in range(4):
                K.cast(cg_mx[l], W["WCO"][:, kq * 4:(kq + 1) * 4, :],
                      o_w_out[o][kq * 512:(kq + 1) * 512, :].rearrange("(k p) n -> p k n", p=128), gate)

    def load_resident(dst, src, src_b, dst_b, nsplit, axis_len):
        step = axis_len // nsplit
        for i in range(nsplit):
            K.dma(K.sp, dst[:, i * step:(i + 1) * step], src[:, i * step:(i + 1) * step], reads=[src_b], writes=[dst_b])

    def outproj_tile(W, W_b, MT, mt_reads, t, xres, ores, hook=None):
        for m in range(NC_):
            bank, bb = K.next_bank()
            pairs = [(W[:, k, m * 128:(m + 1) * 128], MT[:, k, :]) for k in range(NC_)]
            K.mm_group(bank[:], bb, pairs, reads=[W_b] + mt_reads)
            xr_, xrb = xres.next()
            K.dma(K.sp, xr_[:], XT[m][:, t * TT:(t + 1) * TT], reads=[xt_bufs[m][t]], writes=[xrb])
            o_, ob = ores.next()
            K.op(K.dve, lambda h, bank=bank, xr_=xr_, o_=o_: h.tensor_tensor(
                out=o_[:], in0=bank[:], in1=xr_[:], op=ALU.add), reads=[bb, xrb], writes=[ob])
            K.dma(K.sp, XT[m][:, t * TT:(t + 1) * TT], o_[:], reads=[ob], writes=[xt_bufs[m][t]])
            if hook is not None:
                hook(m)

    def odd_mixer(l):
        o = l // 2
        W = OW[o]
        K.begin_phase()
        HT = K.sb("HT", [128, NC_, TT], BF16)
        ht_bufs = [Buf() for _ in range(NC_)]
        ZT = K.sb("ZT", [128, NC_, TT], BF16)
        zt_bufs = [Buf() for _ in range(NC_)]
        WCO = K.sb("WCO", [128, NC_, D], BF16)
        WCO_b = Buf()
        load_resident(WCO, W["WCO"], W["b"], WCO_b, 4, NC_)
        slots = Ring(K, "wci", 3, [128, 3, NC_, 128], BF16)
        VH = K.sb("VH", [128, NC_, 2], F32)
        vh_bufs = [Buf() for _ in range(NC_)]
        K.op(K.dve, lambda h: h.memset(VH[:], 0.0), writes=vh_bufs)
        vr = Ring(K, "vr", 2, [128, TT + 2], F32)
        gcr = Ring(K, "gcr", 2, [128, TT], F32)
        tmpa = Ring(K, "tmpa", 2, [128, TT], F32)
        tmpb = Ring(K, "tmpb", 2, [128, TT], F32)
        xres = Ring(K, "xres", 2, [128, TT], F32)
        ores = Ring(K, "ores", 3, [128, TT], F32)
        norm = Norm(mix_gT, mix_gT_b, HT, ht_bufs)
        order = [(t, c) for t in range(NT) for c in range(NC_)]
        slot_of = {}

        def issue(i):
            if i >= len(order):
                return
            t, c = order[i]
            s_, sb_ = slots.next()
            K.dma(K.sp, s_[:], W["WCI"][c], reads=[W["b"]], writes=[sb_])
            slot_of[i] = (s_, sb_)

        norm.stage_a(l, 0)
        norm.stage_b(l, 0)
        issue(0)
        issue(1)
        issue(2)
        li = 0
        for t in range(NT):
            for c in range(NC_):
                s_, sb_ = slot_of.pop(li)
                bks = []
                for j in range(3):
                    bank, bb = K.next_bank()
                    pairs = [(s_[:, j, k, :], HT[:, k, :]) for k in range(NC_)]
                    K.mm_group(bank[:], bb, pairs, reads=[sb_] + ht_bufs)
                    bks.append((bank, bb))
                li += 1
                issue(li + 2)
                (bB, bBb), (bC, bCb), (bU, bUb) = bks
                gc_, gcb = gcr.next()
                K.op(K.act, lambda h, gc_=gc_, bC=bC: h.activation(out=gc_[:], in_=bC[:], func=AF.Copy),
                     reads=[bCb], writes=[gcb])
                v_, vb = vr.next()
                K.op(K.dve, lambda h, v_=v_, c=c: h.tensor_copy(out=v_[:, 0:2], in_=VH[:, c, :]),
                     reads=[vh_bufs[c]], writes=[vb])
                K.op(K.dve, lambda h, v_=v_, gc_=gc_, bU=bU: h.tensor_tensor(
                    out=v_[:, 2:TT + 2], in0=bU[:], in1=gc_[:], op=ALU.mult), reads=[bUb, gcb, vb], writes=[vb])
                K.op(K.dve, lambda h, v_=v_, c=c: h.tensor_copy(out=VH[:, c, :], in_=v_[:, TT:TT + 2]),
                     reads=[vb], writes=[vh_bufs[c]])
                ta, tab = tmpa.next()
                tb_, tbb = tmpb.next()
                K.op(K.dve, lambda h, ta=ta, v_=v_, c=c: h.tensor_scalar(
                    out=ta[:], in0=v_[:, 0:TT], scalar1=cwT[:, o, 0, c:c + 1], scalar2=None, op0=ALU.mult),
                    reads=[vb, cwT_b], writes=[tab])
                K.op(K.dve, lambda h, ta=ta, tb_=tb_, v_=v_, c=c: h.scalar_tensor_tensor(
                    out=tb_[:], in0=v_[:, 1:TT + 1], scalar=cwT[:, o, 1, c:c + 1], in1=ta[:], op0=ALU.mult,
                    op1=ALU.add), reads=[vb, tab, cwT_b], writes=[tbb])
                K.op(K.dve, lambda h, ta=ta, tb_=tb_, v_=v_, c=c: h.scalar_tensor_tensor(
                    out=ta[:], in0=v_[:, 2:TT + 2], scalar=cwT[:, o, 2, c:c + 1], in1=tb_[:], op0=ALU.mult,
                    op1=ALU.add), reads=[vb, tbb, cwT_b], writes=[tab])
                K.op(K.dve, lambda h, ta=ta, bB=bB, c=c: h.tensor_tensor(
                    out=ZT[:, c, :], in0=bB[:], in1=ta[:], op=ALU.mult), reads=[bBb, tab], writes=[zt_bufs[c]])

            def hook(m, t=t):
                if t + 1 < NT:
                    if m == 3:
                        norm.stage_a(l, t + 1)
                    if m == 9:
                        norm.stage_b(l, t + 1)

            outproj_tile(WCO, WCO_b, ZT, zt_bufs, t, xres, ores, hook)
        K.end_phase()

    def rope_tables():
        K.begin_phase()
        posi = K.sb("posi", [64, S], I32)
        posi_b = Buf()
        K.dma(K.sp, posi[:], pos_in.broadcast_to([64, S]),
              writes=[posi_b])
        ang = K.sb("ang", [64, S], F32)
        ang_b = Buf()
        K.op(K.dve, lambda h: h.tensor_copy(out=ang[:], in_=posi[:]), reads=[posi_b], writes=[ang_b])
        K.op(K.dve, lambda h: h.tensor_scalar(out=ang[:], in0=ang[:], scalar1=ropec[:, 0:1], scalar2=None,
                                              op0=ALU.mult), reads=[ang_b, ropec_b], writes=[ang_b])
        mr = Ring(K, "mr", 2, [64, TT], F32)
        ki = Ring(K, "ki", 2, [64, TT], I32)
        kf = Ring(K, "kf", 2, [64, TT], F32)
        fl = Ring(K, "fl", 2, [64, TT], F32)
        orr = Ring(K, "orr", 4, [64, TT], F32)
        PI = float(np.pi)
        C1 = 6.28125
        C2 = float(2.0 * np.pi - 6.28125)
        for t in range(NT):
            sl = slice(t * TT, (t + 1) * TT)
            m_, mb = mr.next()
            k_i, kib = ki.next()
            k_f, kfb = kf.next()
            f_, fb = fl.next()
            K.op(K.dve, lambda h: h.tensor_scalar(out=m_[:], in0=ang[:, sl], scalar1=float(1.0 / (2 * np.pi)),
                                                  scalar2=None, op0=ALU.mult), reads=[ang_b], writes=[mb])
            K.op(K.dve, lambda h: h.tensor_copy(out=k_i[:], in_=m_[:]), reads=[mb], writes=[kib])
            K.op(K.dve, lambda h: h.tensor_copy(out=k_f[:], in_=k_i[:]), reads=[kib], writes=[kfb])
            K.op(K.dve, lambda h: h.scalar_tensor_tensor(out=m_[:], in0=k_f[:], scalar=-C1, in1=ang[:, sl],
                                                         op0=ALU.mult, op1=ALU.add), reads=[kfb, ang_b, mb],
                 writes=[mb])
            K.op(K.dve, lambda h: h.scalar_tensor_tensor(out=m_[:], in0=k_f[:], scalar=-C2, in1=m_[:],
                                                         op0=ALU.mult, op1=ALU.add), reads=[kfb, mb], writes=[mb])
            K.op(K.dve, lambda h: h.tensor_single_scalar(out=f_[:], in_=m_[:], scalar=PI, op=ALU.is_gt),
                 reads=[mb], writes=[fb])
            K.op(K.dve, lambda h: h.scalar_tensor_tensor(out=m_[:], in0=f_[:], scalar=-2 * PI, in1=m_[:],
                                                         op0=ALU.mult, op1=ALU.add), reads=[fb, mb], writes=[mb])
            K.op(K.dve, lambda h: h.tensor_single_scalar(out=f_[:], in_=m_[:], scalar=-PI, op=ALU.is_lt),
                 reads=[mb, fb], writes=[fb])
            K.op(K.dve, lambda h: h.scalar_tensor_tensor(out=m_[:], in0=f_[:], scalar=2 * PI, in1=m_[:],
                                                         op0=ALU.mult, op1=ALU.add), reads=[fb, mb], writes=[mb])
            K.op(K.dve, lambda h: h.tensor_scalar(out=m_[:], in0=m_[:], scalar1=PI, scalar2=-PI, op0=ALU.min,
                                                  op1=ALU.max), reads=[mb], writes=[mb])
            o_, ob = orr.next()
            K.op(K.act, lambda h: h.activation(out=o_[:], in_=m_[:], func=AF.Sin), reads=[mb], writes=[ob])
            K.op(K.dve, lambda h: h.tensor_scalar(out=o_[:], in0=o_[:], scalar1=ropec[:, 1:2], scalar2=None,
                                                  op0=ALU.mult), reads=[ob, ropec_b], writes=[ob])
            K.dma(K.sp, S2s[:, sl], o_[:], reads=[ob], writes=[cs_bufs[t]])
            K.op(K.dve, lambda h: h.scalar_tensor_tensor(out=f_[:], in0=m_[:], scalar=-1.0, in1=m_[:],
                                                         op0=ALU.mult, op1=ALU.max), reads=[mb, fb], writes=[fb])
            o2, o2b = orr.next()
            K.op(K.act, lambda h: h.activation(out=o2[:], in_=f_[:], func=AF.Sin, bias=halfpi[0:64, :], scale=-1.0),
                 reads=[fb, halfpi_b], writes=[o2b])
            K.dma(K.sp, C2s[:, sl], o2[:], reads=[o2b], writes=[cs_bufs[t]])
        K.end_phase()


    negc = K.sb("negc", [128, 2], F32, glob=True)
    negc_b = Buf()

    def even_setup(e):
        K.begin_phase()
        row = K.sb("grow", [1, 2, 192], F32)
        row_b = Buf()
        K.dma(K.sp, row[:, 0, :], e_qn_g[e:e + 1, :], writes=[row_b])
        K.dma(K.sp, row[:, 1, :], e_kn_g[e:e + 1, :], writes=[row_b])
        K.op(K.dve, lambda h: h.scalar_tensor_tensor(out=row[:], in0=row[:], scalar=-1.0, in1=row[:],
                                                     op0=ALU.mult, op1=ALU.max), reads=[row_b], writes=[row_b])
        mx = K.sb("gmx", [1, 2], F32)
        mx_b = Buf()
        K.op(K.dve, lambda h: h.tensor_reduce(out=mx[:], in_=row[:], axis=mybir.AxisListType.X, op=ALU.max),
             reads=[row_b], writes=[mx_b])
        pr = K.sb("gpr", [1, 1], F32)
        pr_b = Buf()
        K.op(K.dve, lambda h: h.tensor_tensor(out=pr[:], in0=mx[:, 0:1], in1=mx[:, 1:2], op=ALU.mult),
             reads=[mx_b], writes=[pr_b])
        K.op(K.dve, lambda h: h.tensor_scalar(out=pr[:], in0=pr[:], scalar1=-float(np.sqrt(192.0)), scalar2=None,
                                              op0=ALU.mult), reads=[pr_b], writes=[pr_b])
        bank, bb = K.next_bank()
        K.op(K.pe, lambda h: h.matmul(bank[:, 0:1], lhsT=ones_f[0:1, :], rhs=pr[0:1, 0:1], start=True, stop=True),
             reads=[pr_b, ones_f_b], writes=[bb])
        K.op(K.act, lambda h: h.activation(out=negc[:, e:e + 1], in_=bank[:, 0:1], func=AF.Copy),
             reads=[bb], writes=[negc_b])
        K.end_phase()

    def even_e1(l):
        e = l // 2
        W = EW[e]
        K.begin_phase()
        HT = K.sb("HT", [128, NC_, TT], BF16)
        ht_bufs = [Buf() for _ in range(NC_)]
        norm = Norm(mix_gT, mix_gT_b, HT, ht_bufs)
        WI = K.sb("WI", [128, 4, NC_, 512], BF16)
        WI_b = Buf()
        for g in range(4):
            K.dma(K.sp, WI[:, g], W["WEI"][g], reads=[W["b"]], writes=[WI_b])
        WIR = K.sb("WIR", [128, NC_, 128], BF16)
        WIR_b = Buf()
        K.dma(K.sp, WIR[:], W["WEIR"], reads=[W["b"]], writes=[WIR_b])
        PWs = K.sb("PWs", [128, 4, 2, 256], BF16)
        PWs_b = Buf()
        K.dma(K.sp, PWs[:], W["PW"], reads=[W["b"]], writes=[PWs_b])
        RC = K.sb("RC", [128, 4, TT], F32)
        RC_b = Buf()
        K.dma(K.sp, RC[:], rc_in, writes=[RC_b])
        raw = K.sb("raw", [128, 4, TT], F32)
        raw_bufs = [Buf() for _ in range(4)]
        sqr = Ring(K, "sqr", 5, [128, TT], BF16)
        rs2 = K.sb("rs2", [128, TT], F32)
        rs2_b = Buf()
        rstd2 = K.sb("rstd2", [128, TT], F32)
        rstd2_b = Buf()
        latr = Ring(K, "latr", 2, [128, 4, TT], BF16)
        ctr = Ring(K, "ctr", 2, [64, 2, TT], F32)
        t12 = Ring(K, "t12", 2, [64, 2, TT], F32)
        krr = Ring(K, "krr", 2, [64, TT], F32)
        kssr = Ring(K, "kssr", 2, [128, TT], F32)
        UH = K.sb("UH", [128, 8, 16], F32)
        uh_bufs = [Buf() for _ in range(8)]
        K.op(K.dve, lambda h: h.memset(UH[:], 0.0), writes=uh_bufs)
        ur = Ring(K, "ur", 2, [128, TT + 16], F32)
        ar = Ring(K, "ar", 2, [128, TT + 16], F32)
        br_ = Ring(K, "br", 2, [128, TT + 16], F32)
        plr = Ring(K, "plr", 2, [128, 2, TT], BF16)
        btr = Ring(K, "btr", 2, [128, TT], BF16)
        tmp0 = Ring(K, "tmp0", 2, [128, TT], F32)

        def latent(t, g, gT, gT_b, DST, dst_bufs):
            bks = []
            for j in range(4):
                bank, bb = K.next_bank()
                pairs = [(WI[:, g, k, j * 128:(j + 1) * 128], HT[:, k, :]) for k in range(NC_)]
                K.mm_group(bank[:], bb, pairs, reads=[WI_b] + ht_bufs)
                K.op(K.act, lambda h, bank=bank, j=j: h.activation(out=raw[:, j, :], in_=bank[:], func=AF.Copy),
                     reads=[bb], writes=[raw_bufs[j]])
                sq, sqb = sqr.next()
                K.op(K.act, lambda h, bank=bank, sq=sq: h.activation(out=sq[:], in_=bank[:], func=AF.Square),
                     reads=[bb], writes=[sqb])
                bks.append((sq, sqb))
            for j, (sq, sqb) in enumerate(bks):
                K.op(K.pe, lambda h, sq=sq, j=j: h.matmul(norm.stat_bank[:], lhsT=ones_512[:], rhs=sq[:],
                                                         start=(j == 0), stop=(j == 3)),
                     reads=[sqb, ones_512_b], writes=[norm.stat_bb])
            K.op(K.act, lambda h: h.activation(out=rs2[:], in_=norm.stat_bank[:], func=AF.Sqrt, bias=EPS, scale=1.0),
                 reads=[norm.stat_bb], writes=[rs2_b])
            K.op(K.dve, lambda h: h.reciprocal(out=rstd2[:], in_=rs2[:]), reads=[rs2_b], writes=[rstd2_b])
            lt, ltb = latr.next()
            for j in range(4):
                K.op(K.dve, lambda h, j=j, lt=lt: h.scalar_tensor_tensor(
                    out=lt[:, j, :], in0=raw[:, j, :], scalar=gT[:, e, j:j + 1], in1=rstd2[:], op0=ALU.mult,
                    op1=ALU.mult), reads=[raw_bufs[j], rstd2_b, gT_b], writes=[ltb])
            K.dma(K.sp, DST[:, :, t * TT:(t + 1) * TT].rearrange("c p t -> p c t"), lt[:], reads=[ltb],
                  writes=[dst_bufs[t]])

        def krope(t):
            sl = slice(t * TT, (t + 1) * TT)
            bR, bRb = K.next_bank()
            K.mm_group(bR[0:64, :], bRb, [(WIR[:, k, 0:64], HT[:, k, :]) for k in range(NC_)], reads=[WIR_b] + ht_bufs)
            bS, bSb = K.next_bank()
            K.mm_group(bS[0:64, :], bSb, [(WIR[:, k, 64:128], HT[:, k, :]) for k in range(NC_)],
                       reads=[WIR_b] + ht_bufs)
            sq, sqb = sqr.next()
            K.op(K.act, lambda h: h.activation(out=sq[0:64, :], in_=bR[0:64, :], func=AF.Square), reads=[bRb],
                 writes=[sqb])
            bT, bTb = K.next_bank()
            K.op(K.pe, lambda h: h.matmul(bT[:], lhsT=ones_1[0:64, :], rhs=sq[0:64, :], start=True, stop=True),
                 reads=[sqb, ones_1_b], writes=[bTb])
            ks, ksb = kssr.next()
            K.op(K.act, lambda h: h.activation(out=ks[:], in_=bT[:], func=AF.Copy), reads=[bTb], writes=[ksb])
            K.dma(K.sp, KRSSQ[:, sl], ks[:], reads=[ksb], writes=[krs_bufs[t]])
            ct, ctb = ctr.next()
            K.dma(K.sp, ct[:, 0, :], C2s[:, sl], reads=[cs_bufs[t]], writes=[ctb])
            K.dma(K.sp, ct[:, 1, :], S2s[:, sl], reads=[cs_bufs[t]], writes=[ctb])
            tt_, ttb = t12.next()
            K.op(K.dve, lambda h: h.scalar_tensor_tensor(out=tt_[:, 0, :], in0=bR[0:64, :], scalar=GQK[0:64, e, 1, 1:2],
                                                         in1=ct[:, 0, :], op0=ALU.mult, op1=ALU.mult),
                 reads=[bRb, ctb, GQK_b], writes=[ttb])
            K.op(K.dve, lambda h: h.scalar_tensor_tensor(out=tt_[:, 1, :], in0=bS[0:64, :], scalar=GQK[0:64, e, 1, 2:3],
                                                         in1=ct[:, 1, :], op0=ALU.mult, op1=ALU.mult),
                 reads=[bSb, ctb, GQK_b], writes=[ttb])
            kr, krb = krr.next()
            K.op(K.dve, lambda h: h.tensor_tensor(out=kr[:], in0=tt_[:, 0, :], in1=tt_[:, 1, :], op=ALU.add),
                 reads=[ttb], writes=[krb])
            K.dma(K.sp, KRBs[:, sl], kr[:], reads=[krb], writes=[krb_bufs[t]])

        def pool(t):
            for g in range(4):
                w = 2 << g
                pl, plb = plr.next()
                for ci in range(2):
                    i = g * 2 + ci
                    bank, bb = K.next_bank()
                    pairs = [(WI[:, 2 + i // 4, k, (i % 4) * 128:(i % 4 + 1) * 128], HT[:, k, :]) for k in range(NC_)]
                    K.mm_group(bank[:], bb, pairs, reads=[WI_b] + ht_bufs)
                    u_, ub = ur.next()
                    K.op(K.dve, lambda h, u_=u_, i=i: h.tensor_copy(out=u_[:, 0:16], in_=UH[:, i, :]),
                         reads=[uh_bufs[i]], writes=[ub])
                    K.op(K.act, lambda h, u_=u_, bank=bank: h.activation(out=u_[:, 16:TT + 16], in_=bank[:],
                                                                         func=AF.Copy), reads=[bb, ub], writes=[ub])
                    K.op(K.dve, lambda h, u_=u_, i=i: h.tensor_copy(out=UH[:, i, :], in_=u_[:, TT:TT + 16]),
                         reads=[ub], writes=[uh_bufs[i]])
                    a_, ab = ar.next()
                    b_, bbf = br_.next()
                    K.op(K.dve, lambda h, a_=a_, u_=u_: h.tensor_tensor(
                        out=a_[:, 1:TT + 16], in0=u_[:, 1:TT + 16], in1=u_[:, 0:TT + 15], op=ALU.add),
                        reads=[ub], writes=[ab])
                    cur, curb, oth, othb = a_, ab, b_, bbf
                    sh = 1
                    lo = 1
                    while sh * 2 < w:
                        sh *= 2
                        nlo = lo + sh
                        K.op(K.dve, lambda h, cur=cur, oth=oth, sh=sh, nlo=nlo: h.tensor_tensor(
                            out=oth[:, nlo:TT + 16], in0=cur[:, nlo:TT + 16], in1=cur[:, nlo - sh:TT + 16 - sh],
                            op=ALU.add), reads=[curb], writes=[othb])
                        cur, curb, oth, othb = oth, othb, cur, curb
                        lo = nlo
                    if t == 0:
                        tm, tmb = tmp0.next()
                        K.op(K.dve, lambda h, cur=cur, tm=tm, g=g: h.tensor_tensor(
                            out=tm[:], in0=cur[:, 16:TT + 16], in1=RC[:, g, :], op=ALU.mult),
                            reads=[curb, RC_b], writes=[tmb])
                        K.op(K.dve, lambda h, tm=tm, u_=u_, pl=pl, ci=ci: h.tensor_tensor(
                            out=pl[:, ci, :], in0=tm[:], in1=u_[:, 16:TT + 16], op=ALU.subtract),
                            reads=[tmb, ub], writes=[plb])
                    else:
                        K.op(K.dve, lambda h, cur=cur, u_=u_, pl=pl, ci=ci, w=w: h.scalar_tensor_tensor(
                            out=pl[:, ci, :], in0=cur[:, 16:TT + 16], scalar=1.0 / w, in1=u_[:, 16:TT + 16],
                            op0=ALU.mult, op1=ALU.subtract), reads=[curb, ub], writes=[plb])
                for dc in range(2):
                    bank, bb = K.next_bank()
                    pairs = [(PWs[:, g, c, dc * 128:(dc + 1) * 128], pl[:, c, :]) for c in range(2)]
                    K.mm_group(bank[:], bb, pairs, reads=[PWs_b, plb])
                    bt, btb = btr.next()
                    ch = g * 2 + dc
                    K.op(K.act, lambda h, bt=bt, bank=bank, ch=ch: h.activation(
                        out=bt[:], in_=bank[:], func=AF.Copy, scale=psT[:, e, ch:ch + 1]),
                        reads=[bb, psT_b], writes=[btb])
                    K.dma(K.sp, MIXT[8 + ch][:, t * TT:(t + 1) * TT], bt[:], reads=[btb], writes=[mixt_bufs[8 + ch][t]])

        norm.stage_a(l, 0)
        for t in range(NT):
            norm.stage_b(l, t)
            latent(t, 0, qagT, qagT_b, CQN, cqn_bufs)
            latent(t, 1, kvagT, kvagT_b, CKVN, ckvn_bufs)
            krope(t)
            if t + 1 < NT:
                norm.stage_a(l, t + 1)
            pool(t)
        K.end_phase()

    def even_e2(l):
        e = l // 2
        W = EW[e]
        K.begin_phase()
        K.nrot = 4
        K.bank_rr = 0
        WUQ = K.sb("WUQ", [128, 4, 1536], BF16)
        WUQS = K.sb("WUQS", [128, 4, 8, 64], BF16)
        WUKV = K.sb("WUKV", [128, 4, 2048], BF16)
        Wb = Buf()
        K.dma(K.sp, WUQ[:], W["WUQ"], reads=[W["b"]], writes=[Wb])
        K.dma(K.sp, WUQS[:], W["WUQS"], reads=[W["b"]], writes=[Wb])
        K.dma(K.sp, WUKV[:], W["WUKV"], reads=[W["b"]], writes=[Wb])
        KNs = [K.sb(f"KN{i}", [128, S], BF16) for i in range(2)]
        KRs = [K.sb(f"KR{i}", [64, S], BF16) for i in range(2)]
        VVs = [K.sb(f"VV{i}", [128, NB, 128], BF16) for i in range(2)]
        kn_bufs = [[Buf() for _ in range(NT)] for _ in range(2)]
        kr_bufs = [[Buf() for _ in range(NT)] for _ in range(2)]
        vv_bufs = [[Buf() for _ in range(NT)] for _ in range(2)]
        latr = Ring(K, "latr", 3, [128, 4, TT], BF16)
        sqr = Ring(K, "sqr", 4, [128, TT], BF16)
        kssr = Ring(K, "kssr", 2, [128, TT], F32)
        krbr = Ring(K, "krbr", 2, [64, TT], F32)
        ssr = Ring(K, "ssr", 2, [128, TT], F32)
        rsr = Ring(K, "rsr", 2, [128, TT], F32)
        rstdr = Ring(K, "rstdr", 3, [128, TT], F32)
        ctr = Ring(K, "ctr", 2, [64, 2, TT], F32)
        t12 = Ring(K, "t12", 2, [64, 3, TT], F32)
        qnr = Ring(K, "qnr", 2, [128, TT], BF16)
        qrr = Ring(K, "qrr", 2, [64, TT], BF16)
        ptr = Ring(K, "ptr", 6, [128, TT], BF16)
        linr = Ring(K, "linr", 2, [128, TT], F32)
        otr = Ring(K, "otr", 2, [128, TT], BF16)
        SCALE = float(192.0 ** -0.5)

        def build_kv_tile(hd, t):
            par = hd % 2
            KN, KR, VV = KNs[par], KRs[par], VVs[par]
            if True:
                sl = slice(t * TT, (t + 1) * TT)
                ck, ckb = latr.next()
                K.dma(K.sp, ck[:], CKVN[:, :, sl].rearrange("c p t -> p c t"), reads=[ckvn_bufs[t]], writes=[ckb])
                ks, ksb = kssr.next()
                K.dma(K.sp, ks[:], KRSSQ[:, sl], reads=[krs_bufs[t]], writes=[ksb])
                kb_, kbb = krbr.next()
                K.dma(K.sp, kb_[:], KRBs[:, sl], reads=[krb_bufs[t]], writes=[kbb])
                bank, bb = K.next_bank()
                K.mm_group(bank[:], bb, [(WUKV[:, k, hd * 256:hd * 256 + 128], ck[:, k, :]) for k in range(4)],
                           reads=[Wb, ckb])
                sq, sqb = sqr.next()
                K.op(K.act, lambda h: h.activation(out=sq[:], in_=bank[:], func=AF.Square), reads=[bb], writes=[sqb])
                bV, bVb = K.next_bank()

                def fnv(h):
                    ins = None
                    for blk in range(4):
                        for k in range(4):
                            ins = h.matmul(bV[:, blk * 128:(blk + 1) * 128], lhsT=ck[:, k, blk * 128:(blk + 1) * 128],
                                           rhs=WUKV[:, k, hd * 256 + 128:hd * 256 + 256], start=(k == 0),
                                           stop=(k == 3))
                    return ins

                K.op(K.pe, fnv, reads=[Wb, ckb], writes=[bVb])
                K.op(K.act, lambda h: h.activation(
                    out=VV[:, t * 4:(t + 1) * 4, :], in_=bV[:].rearrange("p (b d) -> p b d", b=4), func=AF.Copy),
                    reads=[bVb], writes=[vv_bufs[par][t]])
                bT, bTb = K.next_bank()
                K.op(K.pe, lambda h: h.matmul(bT[:], lhsT=ones_1[:], rhs=sq[:], start=True, stop=True),
                     reads=[sqb, ones_1_b], writes=[bTb])
                ss, ssb = ssr.next()
                K.op(K.dve, lambda h: h.tensor_tensor(out=ss[:], in0=bT[:], in1=ks[:], op=ALU.add),
                     reads=[bTb, ksb], writes=[ssb])
                rs, rsb = rsr.next()
                K.op(K.act, lambda h: h.activation(out=rs[:], in_=ss[:], func=AF.Ln, bias=epsc[:], scale=1.0 / 192),
                     reads=[ssb, epsc_b], writes=[rsb])
                rstd, rstdb = rstdr.next()
                K.op(K.act, lambda h: h.activation(out=rstd[:], in_=rs[:], func=AF.Exp, scale=-0.5), reads=[rsb],
                     writes=[rstdb])
                K.op(K.dve, lambda h: h.scalar_tensor_tensor(
                    out=KN[:, sl], in0=bank[:], scalar=GQK[:, e, 1, 0:1], in1=rstd[:], op0=ALU.mult, op1=ALU.mult),
                    reads=[bb, rstdb, GQK_b], writes=[kn_bufs[par][t]])
                K.op(K.dve, lambda h: h.tensor_tensor(out=KR[:, sl], in0=kb_[:], in1=rstd[0:64, :], op=ALU.mult),
                     reads=[kbb, rstdb], writes=[kr_bufs[par][t]])

        def qproA(hd, i):
            sl = slice(i * TT, (i + 1) * TT)
            cq, cqb = latr.next()
            K.dma(K.sp, cq[:], CQN[:, :, sl].rearrange("c p t -> p c t"), reads=[cqn_bufs[i]], writes=[cqb])
            ct, ctb = ctr.next()
            K.dma(K.sp, ct[:, 0, :], C2s[:, sl], reads=[cs_bufs[i]], writes=[ctb])
            K.dma(K.sp, ct[:, 1, :], S2s[:, sl], reads=[cs_bufs[i]], writes=[ctb])
            bN, bNb = K.next_bank()
            K.mm_group(bN[:], bNb, [(WUQ[:, k, hd * 192:hd * 192 + 128], cq[:, k, :]) for k in range(4)],
                       reads=[Wb, cqb])
            bR, bRb = K.next_bank()
            K.mm_group(bR[0:64, :], bRb, [(WUQ[:, k, hd * 192 + 128:hd * 192 + 192], cq[:, k, :]) for k in range(4)],
                       reads=[Wb, cqb])
            bS, bSb = K.next_bank()
            K.mm_group(bS[0:64, :], bSb, [(WUQS[:, k, hd, :], cq[:, k, :]) for k in range(4)], reads=[Wb, cqb])
            sq1, sq1b = sqr.next()
            K.op(K.act, lambda h: h.activation(out=sq1[:], in_=bN[:], func=AF.Square), reads=[bNb], writes=[sq1b])
            sq2, sq2b = sqr.next()
            K.op(K.act, lambda h: h.activation(out=sq2[0:64, :], in_=bR[0:64, :], func=AF.Square), reads=[bRb],
                 writes=[sq2b])
            bT, bTb = K.next_bank()

            def fns(h):
                h.matmul(bT[:], lhsT=ones_1[:], rhs=sq1[:], start=True, stop=False)
                return h.matmul(bT[:], lhsT=ones_1[0:64, :], rhs=sq2[0:64, :], start=False, stop=True)

            K.op(K.pe, fns, reads=[sq1b, sq2b, ones_1_b], writes=[bTb])
            return (hd, bT, bTb, bN, bNb, bR, bRb, bS, bSb, ct, ctb)

        def qproB(st):
            hd, bT, bTb, bN, bNb, bR, bRb, bS, bSb, ct, ctb = st
            rs, rsb = rsr.next()
            K.op(K.act, lambda h: h.activation(out=rs[:], in_=bT[:], func=AF.Ln, bias=epsc[:], scale=1.0 / 192),
                 reads=[bTb, epsc_b], writes=[rsb])
            rstd, rstdb = rstdr.next()
            K.op(K.act, lambda h: h.activation(out=rstd[:], in_=rs[:], func=AF.Exp, scale=-0.5), reads=[rsb],
                 writes=[rstdb])
            qn, qnb = qnr.next()
            K.op(K.dve, lambda h: h.scalar_tensor_tensor(
                out=qn[:], in0=bN[:], scalar=GQK[:, e, 0, 0:1], in1=rstd[:], op0=ALU.mult, op1=ALU.mult),
                reads=[bNb, rstdb, GQK_b], writes=[qnb])
            tt_, ttb = t12.next()
            K.op(K.dve, lambda h: h.scalar_tensor_tensor(
                out=tt_[:, 0, :], in0=bR[0:64, :], scalar=GQK[0:64, e, 0, 1:2], in1=ct[:, 0, :], op0=ALU.mult,
                op1=ALU.mult), reads=[bRb, ctb, GQK_b], writes=[ttb])
            K.op(K.dve, lambda h: h.scalar_tensor_tensor(
                out=tt_[:, 1, :], in0=bS[0:64, :], scalar=GQK[0:64, e, 0, 2:3], in1=ct[:, 1, :], op0=ALU.mult,
                op1=ALU.mult), reads=[bSb, ctb, GQK_b], writes=[ttb])
            K.op(K.dve, lambda h: h.tensor_tensor(out=tt_[:, 2, :], in0=tt_[:, 0, :], in1=tt_[:, 1, :], op=ALU.add),
                 reads=[ttb], writes=[ttb])
            qr, qrb = qrr.next()
            K.op(K.dve, lambda h: h.tensor_tensor(out=qr[:], in0=tt_[:, 2, :], in1=rstd[0:64, :], op=ALU.mult),
                 reads=[ttb, rstdb], writes=[qrb])
            return qn, qnb, qr, qrb

        def attention(hd, i, q, hook=None):
            qn, qnb, qr, qrb = q
            par = hd % 2
            KN, KR, VV = KNs[par], KRs[par], VVs[par]
            sl = slice(i * TT, (i + 1) * TT)
            bO, bOb = K.banks[4 + (i % 2)], K.bank_bufs[4 + (i % 2)]
            bL, bLb = K.banks[6 + (i % 2)], K.bank_bufs[6 + (i % 2)]
            nkb = 4 * i + 4

            def emit_s(j):
                d = j - 4 * i
                c0 = 128 * d if d > 0 else 0
                bank, bb = K.next_bank()
                tk = j // 4
                K.mm_group(bank[:, c0:], bb, [(KN[:, j * 128:(j + 1) * 128], qn[:, c0:]),
                                              (KR[:, j * 128:(j + 1) * 128], qr[:, c0:])],
                           reads=[kn_bufs[par][tk], kr_bufs[par][tk], qnb, qrb])
                pt, ptb = ptr.next()
                K.op(K.act, lambda h: h.activation(out=pt[:, c0:], in_=bank[:, c0:], func=AF.Exp,
                                                   bias=negc[:, e:e + 1], scale=SCALE),
                     reads=[bb, negc_b], writes=[ptb])
                if d >= 0:
                    K.op(K.dve, lambda h: h.tensor_tensor(out=pt[:, c0:c0 + 128], in0=pt[:, c0:c0 + 128],
                                                          in1=tri[:], op=ALU.mult), reads=[ptb, tri_b], writes=[ptb])
                return pt, ptb, c0

            DEPTH = 2
            pend = [emit_s(j) for j in range(min(DEPTH, nkb))]
            for j in range(nkb):
                pt, ptb, c0 = pend.pop(0)
                if j + DEPTH < nkb:
                    pend.append(emit_s(j + DEPTH))
                tk = j // 4

                def fpv(h, pt=pt, c0=c0, j=j):
                    h.matmul(bO[:, c0:], lhsT=VV[:, j, :], rhs=pt[:, c0:], start=(j == 0), stop=(j == nkb - 1))
                    return h.matmul(bL[:, c0:], lhsT=ones_1[:], rhs=pt[:, c0:], start=(j == 0), stop=(j == nkb - 1))

                K.op(K.pe, fpv, reads=[vv_bufs[par][tk], ones_1_b, ptb], writes=[bOb, bLb])
                if j == 0 and hook is not None:
                    hook()
            li_, lib = linr.next()
            K.op(K.dve, lambda h: h.reciprocal(out=li_[:], in_=bL[:]), reads=[bLb], writes=[lib])
            ot, otb = otr.next()
            K.op(K.dve, lambda h: h.tensor_tensor(out=ot[:], in0=bO[:], in1=li_[:], op=ALU.mult),
                 reads=[bOb, lib], writes=[otb])
            K.dma(K.sp, MIXT[hd][:, sl], ot[:], reads=[otb], writes=[mixt_bufs[hd][i]])

        for t in range(NT):
            build_kv_tile(0, t)
        q = qproB(qproA(0, 0))
        seq = [(hd, i) for hd in range(NH) for i in range(NT)]
        for n, (hd, i) in enumerate(seq):
            nq = qproB(qproA(*seq[n + 1])) if n + 1 < len(seq) else None

            def hook(hd=hd, i=i):
                if hd + 1 < NH:
                    build_kv_tile(hd + 1, i)

            attention(hd, i, q, hook)
            q = nq
        K.nrot = 7
        K.bank_rr = 0
        K.end_phase()

    def even_e3(l):
        e = l // 2
        W = EW[e]
        K.begin_phase()
        WEO = K.sb("WEO", [128, NC_, D], BF16)
        WEO_b = Buf()
        load_resident(WEO, W["WEO"], W["b"], WEO_b, 4, NC_)
        mtr = Ring(K, "mtr", 2, [128, NC_, TT], BF16)
        xres = Ring(K, "xres", 2, [128, TT], F32)
        ores = Ring(K, "ores", 3, [128, TT], F32)

        def load(t):
            mt, mtb = mtr.next()
            K.dma(K.sp, mt[:], MIXT[:, :, t * TT:(t + 1) * TT].rearrange("c p t -> p c t"),
                  reads=[mixt_bufs[c][t] for c in range(NC_)], writes=[mtb])
            return mt, mtb

        nxt = load(0)
        for t in range(NT):
            mt, mtb = nxt
            if t + 1 < NT:
                nxt = load(t + 1)
            outproj_tile(WEO, WEO_b, mt, [mtb], t, xres, ores)
        K.end_phase()

    def cast_layer(l, gate=None):
        if do_mixer:
            cast_mixer_weights(l, gate)
        if do_mlp:
            cast_mlp_weights(l, gate)

    cast_layer(layers[0])
    prologue()
    has_even = do_mixer and any(l % 2 == 0 for l in layers)
    if has_even:
        rope_tables()
    for li_, l in enumerate(layers):
        if li_ + 1 < len(layers):
            gate = (K.dve.sem, K.dve.cnt) if K.dve.cnt > 0 else None
            cast_layer(layers[li_ + 1], gate)
        if do_mixer:
            if l % 2 == 0:
                even_setup(l // 2)
                even_e1(l)
                even_e2(l)
                even_e3(l)
            else:
                odd_mixer(l)
        if do_mlp:
            mlp_phase(l)
    epilogue()
    K.barrier()
    K.es.close()
    return nc, K


def make_consts():
    p = np.arange(128)
    tri = (p[None, :] >= p[:, None]).astype(np.float32)
    inv_freq = (1.0 / (np.float32(10000.0) ** (np.arange(0, 64, 2, dtype=np.float32) / np.float32(64)))).astype(np.float32)
    rope = np.zeros((64, 2), np.float32)
    rope[:, 0] = np.concatenate([inv_freq, inv_freq])
    rope[:32, 1] = -1.0
    rope[32:, 1] = 1.0
    t = np.arange(TT)
    rc = np.zeros((128, 4, TT), np.float32)
    for g in range(4):
        w = 2 << g
        rc[:, g, :] = (1.0 / np.minimum(t + 1, w)).astype(np.float32)[None, :]
    return {"c_ident": np.eye(128, dtype=np.float32), "c_tri": tri, "c_rope": rope, "c_rc": rc}


_CACHE = {}


def kernel(**inputs):
    S = 4096
    n = 8
    key = ("full", S)
    if key not in _CACHE:
        _CACHE[key] = build_program(S, [0, 1, 2, 3])
    nc, _ = _CACHE[key]
    consts = make_consts()
    shared = {k: np.ascontiguousarray(v) for k, v in inputs.items() if k not in ("x", "positions")}
    in_maps = []
    for b in range(n):
        m = dict(shared)
        m.update(consts)
        m["x"] = np.ascontiguousarray(inputs["x"][b])
        m["positions"] = np.ascontiguousarray(inputs["positions"][b:b + 1])
        in_maps.append(m)
    res = run_bass_kernel_spmd(nc, in_maps, core_ids=list(range(n)))
    out = np.stack([np.asarray(r["y"]) for r in res.results], axis=0)
    return out.astype(np.float32, copy=False)
```
